# Optimizing a Trainium2 kernel written in Bass

```python
import jax, jax.numpy as jnp
from jax import lax
import numpy as np

D_MODEL = 1024
BATCH = 8
SEQ = 2048
DEPTH = 1
DEC_BATCH = 8
DEC_SEQ = 16
PAST_LEN = 2048

CHUNK = 64
Q_BLOCK = 128
EPS = 1e-6
NEG_INF = -1e30
N_HEADS_A = 8
NOPE_DIM = 64
ROPE_DIM = 32
V_DIM_A = 64
QK_DIM_A = NOPE_DIM + ROPE_DIM
Q_LORA = 384
KV_LORA = 256
ROPE_BASE = 10000.0
N_HEADS_B = 8
HEAD_DIM_B = 64
LEFT_CHUNKS = 8
BAND_WINDOW = LEFT_CHUNKS * CHUNK
REL_CLIP = 128
N_REL = 2 * REL_CLIP + 1
D_FF = -(-(8 * D_MODEL) // (3 * 256)) * 256
N_BRANCH = 2
COL_QKV_B = 3 * N_HEADS_B * HEAD_DIM_B
COL_GATE = N_BRANCH * D_MODEL
IN_SPLITS = [Q_LORA, Q_LORA + KV_LORA, Q_LORA + KV_LORA + ROPE_DIM, Q_LORA + KV_LORA + ROPE_DIM + COL_QKV_B]
IN_COLS = Q_LORA + KV_LORA + ROPE_DIM + COL_QKV_B + COL_GATE

kernel_name = 'hybrid_mla_chunkband_adaln_stream_step'


def rmsnorm(x, g):
    xf = x.astype(jnp.float32)
    xf = xf * lax.rsqrt(jnp.mean(xf * xf, axis=-1, keepdims=True) + EPS)
    return (xf * g.astype(jnp.float32)).astype(x.dtype)


def modulate(h, shift, scale):
    return h * (1.0 + scale) + shift


def rope(x, pos):
    half = ROPE_DIM // 2
    inv_freq = ROPE_BASE ** (-jnp.arange(half, dtype=jnp.float32) / half)
    ang = pos.astype(jnp.float32)[:, None] * inv_freq[None, :]
    shape = (1, pos.shape[0]) + (1,) * (x.ndim - 3) + (half,)
    cos = jnp.cos(ang).reshape(shape)
    sin = jnp.sin(ang).reshape(shape)
    xf = x.astype(jnp.float32)
    x1, x2 = xf[..., :half], xf[..., half:]
    return jnp.concatenate([x1 * cos - x2 * sin, x2 * cos + x1 * sin], axis=-1).astype(x.dtype)


def softmax_attend(q, k, v, bias):
    scale = q.shape[-1] ** -0.5
    s = jnp.einsum('bqhd,bkhd->bhqk', q, k).astype(jnp.float32) * scale + bias
    p = jax.nn.softmax(s, axis=-1).astype(v.dtype)
    return jnp.einsum('bhqk,bkhd->bqhd', p, v)


def ada_terms(c, w_ada, b_ada):
    m = jax.nn.silu(c) @ w_ada + b_ada
    return jnp.split(m[:, None, :], 6, axis=-1)


def mixer_inputs(h, pos, w_in, g_q_lora, w_q_up, g_kv_lora, g_qn_a, g_qr_a, g_kr_a, g_q_b, g_k_b):
    B, S, _ = h.shape
    z = h @ w_in
    c_q, c_kv, k_pe, qkv_b, gate_logits = jnp.split(z, IN_SPLITS, axis=-1)
    q_a = (rmsnorm(c_q, g_q_lora) @ w_q_up).reshape(B, S, N_HEADS_A, QK_DIM_A)
    q_a = jnp.concatenate([rmsnorm(q_a[..., :NOPE_DIM], g_qn_a),
                           rope(rmsnorm(q_a[..., NOPE_DIM:], g_qr_a), pos)], axis=-1)
    latent = rmsnorm(c_kv, g_kv_lora)
    k_rope = rope(rmsnorm(k_pe, g_kr_a), pos)
    qkv_b = qkv_b.reshape(B, S, 3, N_HEADS_B, HEAD_DIM_B)
    q_b = rmsnorm(qkv_b[:, :, 0], g_q_b)
    k_b = rmsnorm(qkv_b[:, :, 1], g_k_b)
    v_b = qkv_b[:, :, 2]
    gates = jax.nn.sigmoid(gate_logits).reshape(B, S, N_BRANCH, D_MODEL)
    return q_a, latent, k_rope, q_b, k_b, v_b, gates


def mla_keys(latent, k_rope, w_kv_up, g_kn_a):
    B, T, _ = latent.shape
    kv = (latent @ w_kv_up).reshape(B, T, N_HEADS_A, NOPE_DIM + V_DIM_A)
    k_nope = rmsnorm(kv[..., :NOPE_DIM], g_kn_a)
    k = jnp.concatenate([k_nope, jnp.broadcast_to(k_rope[:, :, None, :], (B, T, N_HEADS_A, ROPE_DIM))], axis=-1)
    return k, kv[..., NOPE_DIM:]


def mla_prompt(q, k, v, pos):
    B, S, H, D = q.shape
    nb = S // Q_BLOCK
    qb = jnp.moveaxis(q.reshape(B, nb, Q_BLOCK, H, D), 1, 0)
    qchunk = (pos // CHUNK).reshape(nb, Q_BLOCK)
    kchunk = pos // CHUNK

    def block(args):
        qi, qc = args
        bias = jnp.where(qc[:, None] >= kchunk[None, :], 0.0, NEG_INF).astype(jnp.float32)
        return softmax_attend(qi, k, v, bias)

    o = lax.map(block, (qb, qchunk))
    return jnp.moveaxis(o, 0, 1).reshape(B, S, H, v.shape[-1])


def band_bias(rel_bias, qpos, kpos):
    idx = jnp.clip(qpos[:, None] - kpos[None, :], -REL_CLIP, REL_CLIP) + REL_CLIP
    return rel_bias[:, idx].astype(jnp.float32)


def band_prompt(q, k, v, rel_bias):
    B, S, H, Dh = q.shape
    nc = S // CHUNK
    nband = LEFT_CHUNKS + 1

    def band(t):
        tc = t.reshape(B, nc, CHUNK, H, Dh)
        tp = jnp.concatenate([jnp.zeros((B, LEFT_CHUNKS, CHUNK, H, Dh), t.dtype), tc], axis=1)
        return jnp.concatenate([tp[:, o:o + nc] for o in range(nband)], axis=2)

    kb, vb = band(k), band(v)
    kpos_rel = jnp.arange(nband * CHUNK) - LEFT_CHUNKS * CHUNK
    bias = band_bias(rel_bias, jnp.arange(CHUNK), kpos_rel)
    valid = (jnp.arange(nc)[:, None] * CHUNK + kpos_rel[None, :]) >= 0
    s = jnp.einsum('bnqhd,bnkhd->bnhqk', q.reshape(B, nc, CHUNK, H, Dh), kb).astype(jnp.float32) * (Dh ** -0.5)
    s = jnp.where(valid[None, :, None, None, :], s + bias[None, None], NEG_INF)
    p = jax.nn.softmax(s, axis=-1).astype(v.dtype)
    o = jnp.einsum('bnhqk,bnkhd->bnqhd', p, vb)
    return o.reshape(B, S, H, Dh)


def band_sample(q, k_new, v_new, k_buf, v_buf, rel_bias, past_len):
    Sd = q.shape[1]
    W = k_buf.shape[1]
    k = jnp.concatenate([k_buf, k_new], axis=1)
    v = jnp.concatenate([v_buf, v_new], axis=1)
    kpos = jnp.concatenate([past_len - W + jnp.arange(W), past_len + jnp.arange(Sd)])
    qpos = past_len + jnp.arange(Sd)
    return softmax_attend(q, k, v, band_bias(rel_bias, qpos, kpos)[None])


def layer_tail(x, ada, o_a, o_b, gates, w_o_a, w_o_b, w_out, g_norm_ffn, w_gate, w_up, w_down):
    shift1, scale1, gate1, shift2, scale2, gate2 = ada
    B, S = o_a.shape[:2]
    y_a = o_a.reshape(B, S, N_HEADS_A * V_DIM_A) @ w_o_a
    y_b = o_b.reshape(B, S, N_HEADS_B * HEAD_DIM_B) @ w_o_b
    x = x + gate1 * ((gates[:, :, 0] * y_a + gates[:, :, 1] * y_b) @ w_out)
    h = modulate(rmsnorm(x, g_norm_ffn), shift2, scale2)
    return x + gate2 * ((jax.nn.silu(h @ w_gate) * (h @ w_up)) @ w_down)


def setup_inputs(seed: int = 0) -> dict:
    key = jax.random.key(seed)
    ks = jax.random.split(key, 32)
    L = DEPTH

    def nrm(k, shape, scale):
        return scale * jax.random.normal(k, shape, dtype=jnp.float32)

    def gain(k, shape):
        return 1.0 + 0.05 * jax.random.normal(k, shape, dtype=jnp.float32)

    band_keep = min(BAND_WINDOW, PAST_LEN)
    return {
        'x_prompt': nrm(ks[0], (BATCH, SEQ, D_MODEL), 1.0),
        'x_sample': nrm(ks[1], (DEC_BATCH, DEC_SEQ, D_MODEL), 1.0),
        'c_prompt': nrm(ks[2], (BATCH, D_MODEL), 1.0),
        'c_sample': nrm(ks[3], (DEC_BATCH, D_MODEL), 1.0),
        'cache_kv_latent': nrm(ks[4], (L, DEC_BATCH, PAST_LEN, KV_LORA), 1.0),
        'cache_k_rope': nrm(ks[5], (L, DEC_BATCH, PAST_LEN, ROPE_DIM), 1.0),
        'cache_band_k': nrm(ks[6], (L, DEC_BATCH, band_keep, N_HEADS_B, HEAD_DIM_B), 1.0),
        'cache_band_v': nrm(ks[7], (L, DEC_BATCH, band_keep, N_HEADS_B, HEAD_DIM_B), 1.0),
        'w_ada': nrm(ks[8], (L, D_MODEL, 6 * D_MODEL), 0.5 * D_MODEL ** -0.5),
        'b_ada': nrm(ks[9], (L, 6 * D_MODEL), 0.02),
        'g_norm_mix': gain(ks[10], (L, D_MODEL)),
        'w_in': nrm(ks[11], (L, D_MODEL, IN_COLS), D_MODEL ** -0.5),
        'g_q_lora': gain(ks[12], (L, Q_LORA)),
        'w_q_up': nrm(ks[13], (L, Q_LORA, N_HEADS_A * QK_DIM_A), Q_LORA ** -0.5),
        'g_kv_lora': gain(ks[14], (L, KV_LORA)),
        'w_kv_up': nrm(ks[15], (L, KV_LORA, N_HEADS_A * (NOPE_DIM + V_DIM_A)), KV_LORA ** -0.5),
        'g_qn_a': gain(ks[16], (L, NOPE_DIM)),
        'g_kn_a': gain(ks[17], (L, NOPE_DIM)),
        'g_qr_a': gain(ks[18], (L, ROPE_DIM)),
        'g_kr_a': gain(ks[19], (L, ROPE_DIM)),
        'g_q_b': gain(ks[20], (L, HEAD_DIM_B)),
        'g_k_b': gain(ks[21], (L, HEAD_DIM_B)),
        'rel_bias': nrm(ks[22], (L, N_HEADS_B, N_REL), 0.1),
        'w_o_a': nrm(ks[23], (L, N_HEADS_A * V_DIM_A, D_MODEL), (N_HEADS_A * V_DIM_A) ** -0.5),
        'w_o_b': nrm(ks[24], (L, N_HEADS_B * HEAD_DIM_B, D_MODEL), (N_HEADS_B * HEAD_DIM_B) ** -0.5),
        'w_out': nrm(ks[25], (L, D_MODEL, D_MODEL), D_MODEL ** -0.5),
        'g_norm_ffn': gain(ks[26], (L, D_MODEL)),
        'w_gate': nrm(ks[27], (L, D_MODEL, D_FF), D_MODEL ** -0.5),
        'w_up': nrm(ks[28], (L, D_MODEL, D_FF), D_MODEL ** -0.5),
        'w_down': nrm(ks[29], (L, D_FF, D_MODEL), D_FF ** -0.5),
    }


def reference(x_prompt, x_sample, c_prompt, c_sample, cache_kv_latent, cache_k_rope, cache_band_k, cache_band_v,
              w_ada, b_ada, g_norm_mix, w_in, g_q_lora, w_q_up, g_kv_lora, w_kv_up, g_qn_a, g_kn_a, g_qr_a, g_kr_a,
              g_q_b, g_k_b, rel_bias, w_o_a, w_o_b, w_out, g_norm_ffn, w_gate, w_up, w_down):
    S = x_prompt.shape[1]
    Sd = x_sample.shape[1]
    past_len = cache_kv_latent.shape[2]
    keep = min(BAND_WINDOW, S)
    pos_p = jnp.arange(S)
    pos_s = past_len + jnp.arange(Sd)
    xp, xs = x_prompt, x_sample
    lat_p, kr_p, bk_p, bv_p = [], [], [], []
    lat_s, kr_s, bk_s, bv_s = [], [], [], []
    for l in range(DEPTH):
        mix_w = (w_in[l], g_q_lora[l], w_q_up[l], g_kv_lora[l], g_qn_a[l], g_qr_a[l], g_kr_a[l], g_q_b[l], g_k_b[l])
        out_w = (w_o_a[l], w_o_b[l], w_out[l], g_norm_ffn[l], w_gate[l], w_up[l], w_down[l])
        ada_p = ada_terms(c_prompt, w_ada[l], b_ada[l])
        h = modulate(rmsnorm(xp, g_norm_mix[l]), ada_p[0], ada_p[1])
        q_a, lat, kr, q_b, k_b, v_b, gates = mixer_inputs(h, pos_p, *mix_w)
        k_a, v_a = mla_keys(lat, kr, w_kv_up[l], g_kn_a[l])
        o_a = mla_prompt(q_a, k_a, v_a, pos_p)
        o_b = band_prompt(q_b, k_b, v_b, rel_bias[l])
        xp = layer_tail(xp, ada_p, o_a, o_b, gates, *out_w)
        lat_p.append(lat)
        kr_p.append(kr)
        bk_p.append(k_b[:, S - keep:])
        bv_p.append(v_b[:, S - keep:])
        ada_s = ada_terms(c_sample, w_ada[l], b_ada[l])
        h = modulate(rmsnorm(xs, g_norm_mix[l]), ada_s[0], ada_s[1])
        q_a, lat, kr, q_b, k_b, v_b, gates = mixer_inputs(h, pos_s, *mix_w)
        k_a, v_a = mla_keys(jnp.concatenate([cache_kv_latent[l], lat], axis=1),
                            jnp.concatenate([cache_k_rope[l], kr], axis=1), w_kv_up[l], g_kn_a[l])
        o_a = softmax_attend(q_a, k_a, v_a, 0.0)
        o_b = band_sample(q_b, k_b, v_b, cache_band_k[l], cache_band_v[l], rel_bias[l], past_len)
        xs = layer_tail(xs, ada_s, o_a, o_b, gates, *out_w)
        lat_s.append(lat)
        kr_s.append(kr)
        bk_s.append(k_b)
        bv_s.append(v_b)
    return (xp, xs, jnp.stack(lat_p), jnp.stack(kr_p), jnp.stack(bk_p), jnp.stack(bv_p),
            jnp.stack(lat_s), jnp.stack(kr_s), jnp.stack(bk_s), jnp.stack(bv_s))
```

```python
import numpy as np
import concourse.bass as bass
import concourse.mybir as mybir
from concourse.bass_utils import run_bass_kernel_spmd

F32 = mybir.dt.float32
BF16 = mybir.dt.bfloat16
ALU = mybir.AluOpType
AF = mybir.ActivationFunctionType
AX = mybir.AxisListType


class _Op:
    __slots__ = ("eng", "emit", "dma", "deps", "signal", "count", "dsem", "dtarget", "pre", "seq")

    def __init__(self, eng, emit, dma):
        self.eng = eng
        self.emit = emit
        self.dma = dma
        self.deps = []
        self.signal = False
        self.count = 0
        self.dsem = None
        self.dtarget = 0
        self.pre = None
        self.seq = 0


def _region(ap):
    t = ap.tensor
    pat = ap.ap
    off = ap.offset
    sp = str(ap.space)
    if sp in ("SB", "PSUM"):
        sz = mybir.dt.size(ap.dtype)
        pstride = pat[0][0]
        npart = pat[0][1]
        if pstride == 0:
            pstride = 1 << 30
        p0 = off // pstride
        f0 = (off % pstride) * sz
        ext = 1
        for st, cnt in pat[1:]:
            ext += abs(st) * (cnt - 1)
        if sp == "PSUM":
            return (sp + ":" + t.name, (p0 // 32) * 32, ((p0 + npart + 31) // 32) * 32,
                    (f0 // 2048) * 2048, ((f0 + ext * sz + 2047) // 2048) * 2048)
        return (sp + ":" + t.name, p0, p0 + npart, f0, f0 + ext * sz)
    lo = off
    hi = off
    for st, cnt in pat:
        if st >= 0:
            hi += st * (cnt - 1)
        else:
            lo += st * (cnt - 1)
    return ("D:" + t.name, 0, 1, lo, hi + 1)


def _ovl(a, b):
    return a[1] < b[2] and b[1] < a[2] and a[3] < b[4] and b[3] < a[4]


def _contains(a, b):
    return a[1] <= b[1] and b[2] <= a[2] and a[3] <= b[3] and b[4] <= a[4]


class Prog:
    ENGS = ("pe", "act", "dve", "pool", "sp")

    def __init__(self, nc, n_dma_sp=28, n_dma_pool=12, n_dma_act=8):
        self.nc = nc
        self.ops = {e: [] for e in self.ENGS}
        self.all = []
        self.wr = {}
        self.rd = {}
        self.track_dram = set()
        self.last_cls = {}
        self.ring = {"sp": n_dma_sp, "pool": n_dma_pool, "act": n_dma_act}
        self.ndma = {"sp": 0, "pool": 0, "act": 0}
        self.dma_ops = {"sp": [], "pool": [], "act": []}

    def _tracked(self, ap):
        sp = str(ap.space)
        if sp in ("SB", "PSUM"):
            return True
        return ap.tensor.name in self.track_dram

    POOL_TO = None
    SERIAL = ("adp",)

    def add(self, eng, emit, reads=(), writes=(), dma=False):
        if eng == "pool" and not dma and self.POOL_TO:
            eng = self.POOL_TO
        op = _Op(eng, emit, dma)
        op.seq = len(self.all)
        deps = set()
        for ap in reads:
            if ap is None or isinstance(ap, (int, float)) or not self._tracked(ap):
                continue
            r = _region(ap)
            for (wr, wop) in self.wr.get(r[0], ()):
                if _ovl(wr, r):
                    deps.add((wop, "raw"))
            lst = self.rd.setdefault(r[0], [])
            if not dma:
                lst[:] = [(rr, rop) for (rr, rop) in lst if not (rop.eng == eng and not rop.dma and _contains(r, rr))]
            lst.append((r, op))
        for ap in writes:
            if ap is None or not self._tracked(ap):
                continue
            w = _region(ap)
            wl = self.wr.setdefault(w[0], [])
            for (wr, wop) in wl:
                if _ovl(wr, w):
                    deps.add((wop, "waw"))
            rl = self.rd.setdefault(w[0], [])
            for (rr, rop) in rl:
                if _ovl(rr, w) and rop is not op:
                    deps.add((rop, "war"))
            wl[:] = [(wr, wop) for (wr, wop) in wl if not _contains(w, wr)]
            wl.append((w, op))
            rl[:] = [(rr, rop) for (rr, rop) in rl if not _contains(w, rr) or rop is op]
        mode = self.SERIAL
        if mode is True and self.all:
            deps.add((self.all[-1], "ser"))
        elif mode:
            touches_psum = any((a is not None) and (not isinstance(a, (int, float))) and str(a.space) == "PSUM"
                               for a in list(reads) + list(writes))
            cls = []
            if "ad" in mode and eng in ("act", "dve") and not dma:
                cls.append("ad")
            if "adp" in mode and eng in ("act", "dve") and not dma and touches_psum:
                for a in list(reads) + list(writes):
                    if (a is not None) and (not isinstance(a, (int, float))) and str(a.space) == "PSUM":
                        r = _region(a)
                        for bk in range(r[3] // 2048, r[4] // 2048):
                            cls.append("adp:%s:%d" % (r[0], bk))
            if "psum" in mode and touches_psum:
                cls.append("psum")
            if "dma" in mode:
                cls.append("dma_any" if dma else "dma_cmp")
            for c in cls:
                if c == "dma_cmp":
                    p = self.last_cls.get("dma")
                    if p is not None:
                        deps.add((p, "ser"))
                    self.last_cls["cmp"] = None
                elif c == "dma_any":
                    p = self.last_cls.get("any")
                    if p is not None:
                        deps.add((p, "ser"))
                else:
                    p = self.last_cls.get(c)
                    if p is not None:
                        deps.add((p, "ser"))
        for (p, kind) in deps:
            if p is op:
                continue
            if kind == "ser":
                if not (p.eng == eng and eng == "pe" and not p.dma):
                    op.deps.append(p)
                    if not p.dma:
                        p.signal = True
                continue
            if not p.dma and p.eng == eng:
                if eng == "pe":
                    continue
            op.deps.append(p)
            if not p.dma:
                p.signal = True
        if dma:
            n = self.ndma[eng]
            ring = self.ring[eng]
            op.dsem = (eng, n % ring)
            op.dtarget = 16 * (n // ring + 1)
            if n >= ring:
                op.pre = self.dma_ops[eng][n - ring]
            self.dma_ops[eng].append(op)
            self.ndma[eng] = n + 1
        self.ops[eng].append(op)
        self.all.append(op)
        mode = self.SERIAL
        if mode and mode is not True:
            self.last_cls["any"] = op
            if dma:
                self.last_cls["dma"] = op
            if "ad" in mode and eng in ("act", "dve") and not dma:
                self.last_cls["ad"] = op
            if "adp" in mode and eng in ("act", "dve") and not dma:
                tp = any((a is not None) and (not isinstance(a, (int, float))) and str(a.space) == "PSUM"
                         for a in list(reads) + list(writes))
                if tp:
                    for a in list(reads) + list(writes):
                        if (a is not None) and (not isinstance(a, (int, float))) and str(a.space) == "PSUM":
                            r = _region(a)
                            for bk in range(r[3] // 2048, r[4] // 2048):
                                self.last_cls["adp:%s:%d" % (r[0], bk)] = op
            if "psum" in mode:
                tp = any((a is not None) and (not isinstance(a, (int, float))) and str(a.space) == "PSUM"
                         for a in list(reads) + list(writes))
                if tp:
                    self.last_cls["psum"] = op
        return op

    def mm(self, out, lhsT, rhs, start=True, stop=True, **kw):
        return self.add("pe", lambda e: e.matmul(out, lhsT, rhs, start=start, stop=stop, **kw),
                        reads=(lhsT, rhs), writes=(out,))

    def tr(self, out, in_, ident):
        return self.add("pe", lambda e: e.transpose(out, in_, ident), reads=(in_, ident), writes=(out,))

    def act(self, out, in_, func, bias=None, scale=None, accum_out=None, eng="act"):
        kw = {}
        if bias is not None:
            kw["bias"] = bias
        if scale is not None:
            kw["scale"] = scale
        if accum_out is not None:
            kw["accum_out"] = accum_out
        rds = [in_]
        if bias is not None and not isinstance(bias, (int, float)):
            rds.append(bias)
        if scale is not None and not isinstance(scale, (int, float)):
            rds.append(scale)
        return self.add(eng, lambda e: e.activation(out, in_, func, **kw), reads=rds, writes=(out, accum_out))

    def tt(self, eng, out, in0, in1, op):
        return self.add(eng, lambda e: e.tensor_tensor(out, in0, in1, op), reads=(in0, in1), writes=(out,))

    def ts(self, eng, out, in0, s1, s2, op0, op1=None, accum_out=None):
        kw = {}
        if op1 is not None:
            kw["op1"] = op1
        if accum_out is not None:
            kw["accum_out"] = accum_out
        rds = [in0]
        for s in (s1, s2):
            if s is not None and not isinstance(s, (int, float)):
                rds.append(s)
        return self.add(eng, lambda e: e.tensor_scalar(out, in0, s1, s2, op0, **kw), reads=rds,
                        writes=(out, accum_out))

    def stt(self, eng, out, in0, scalar, in1, op0, op1, accum_out=None):
        kw = {}
        if accum_out is not None:
            kw["accum_out"] = accum_out
        rds = [in0, in1]
        if not isinstance(scalar, (int, float)):
            rds.append(scalar)
        return self.add(eng, lambda e: e.scalar_tensor_tensor(out, in0, scalar, in1, op0, op1, **kw), reads=rds,
                        writes=(out, accum_out))

    def copy(self, eng, out, in_):
        if eng == "act":
            return self.add(eng, lambda e: e.activation(out, in_, AF.Copy), reads=(in_,), writes=(out,))
        return self.add(eng, lambda e: e.tensor_copy(out, in_), reads=(in_,), writes=(out,))

    def memset(self, eng, ap, val):
        return self.add(eng, lambda e: e.memset(ap, val), reads=(), writes=(ap,))

    def reduce(self, eng, out, in_, op, axis=AX.X):
        return self.add(eng, lambda e: e.tensor_reduce(out, in_, axis, op), reads=(in_,), writes=(out,))

    def ttr(self, eng, out, in0, in1, op0, op1, scale, scalar, accum_out):
        return self.add(eng, lambda e: e.tensor_tensor_reduce(out, in0, in1, op0, op1, scale, scalar, accum_out),
                        reads=(in0, in1), writes=(out, accum_out))

    def dma(self, eng, out, in_, **kw):
        return self.add(eng, lambda e: e.dma_start(out=out, in_=in_, **kw), reads=(in_,), writes=(out,), dma=True)

    def emit(self):
        nc = self.nc
        from contextlib import ExitStack
        with ExitStack() as es:
            esem = {e: es.enter_context(nc.semaphore("s_" + e)) for e in self.ENGS}
            dsems = {}
            for q, n in self.ring.items():
                for i in range(min(n, max(self.ndma[q], 1))):
                    dsems[(q, i)] = es.enter_context(nc.semaphore("d_%s%d" % (q, i)))
            for e in self.ENGS:
                c = 0
                for op in self.ops[e]:
                    if op.signal:
                        c += 1
                    op.count = c
            block = es.enter_context(nc.Block())
            engobj = {"pe": "tensor", "act": "scalar", "dve": "vector", "pool": "gpsimd", "sp": "sync"}
            last_dma = {}
            for q in self.dma_ops:
                for op in self.dma_ops[q]:
                    last_dma[op.dsem] = op.dtarget

            def make(ename):
                ops = self.ops[ename]

                def body(eng):
                    seen = {}
                    for op in ops:
                        waits = {}
                        for p in op.deps:
                            if p.dma:
                                key = dsems[p.dsem]
                                val = p.dtarget
                            else:
                                key = esem[p.eng]
                                val = p.count
                            k = id(key)
                            if k not in waits or waits[k][1] < val:
                                waits[k] = (key, val)
                        if op.pre is not None:
                            key = dsems[op.pre.dsem]
                            k = id(key)
                            val = op.pre.dtarget
                            if k not in waits or waits[k][1] < val:
                                waits[k] = (key, val)
                        for k, (key, val) in waits.items():
                            if seen.get(k, -1) >= val:
                                continue
                            seen[k] = val
                            eng.wait_ge(key, val)
                        ins = op.emit(eng)
                        if op.dma:
                            ins.then_inc(dsems[op.dsem], 16)
                        elif op.signal:
                            ins.then_inc(esem[ename], 1)
                    if ename == "sp":
                        for dk, tgt in last_dma.items():
                            eng.wait_ge(dsems[dk], tgt)
                        for e2 in self.ENGS:
                            if e2 != "sp" and self.ops[e2]:
                                c = self.ops[e2][-1].count
                                if c > 0:
                                    eng.wait_ge(esem[e2], c)
                return body

            for ename in self.ENGS:
                getattr(block, engobj[ename])(make(ename))

from contextlib import ExitStack

D = 1024
S = 2048
SD = 16
NT = 16
DFF = 2816
EPS = 1e-6
ARENA_WORDS = 52600


def _prod(s):
    r = 1
    for v in s:
        r *= v
    return r


class Arena:
    def __init__(self, t, nwords):
        self.t = t
        self.n = nwords
        self.top = 0
        self.peak = 0

    def alloc(self, shape, dt=F32, top=False):
        sz = mybir.dt.size(dt)
        n = _prod(shape)
        nw = (n * sz + 31) // 32 * 8
        if top:
            self.n -= nw
            off = self.n
        else:
            off = self.top
            self.top += nw
        self.peak = max(self.peak, self.top)
        assert self.top <= self.n, ("arena overflow", self.top, self.n)
        ap = self.t[:, off:off + nw]
        if dt != F32:
            ap = ap.bitcast(dt)
        ap = ap[:, 0:n]
        if len(shape) == 2:
            ap = ap.rearrange("p (a b) -> p a b", b=shape[1])
        elif len(shape) == 3:
            ap = ap.rearrange("p (a b c) -> p a b c", b=shape[1], c=shape[2])
        return ap

    def mark(self):
        return self.top

    def reset(self, m):
        self.top = m


class Ring:
    def __init__(self, arena, n, shape, dt=F32):
        self.bufs = [arena.alloc(shape, dt) for _ in range(n)]
        self.i = 0

    def next(self):
        b = self.bufs[self.i % len(self.bufs)]
        self.i += 1
        return b


class KTile:
    def __init__(self, kT, vaug, nk, parts, zeros, union, bias=None, cbias=False):
        self.kT = kT
        self.vaug = vaug
        self.nk = nk
        self.parts = parts
        self.zeros = zeros
        self.union = union
        self.bias = bias
        self.cbias = cbias


class _Stop(Exception):
    pass


def build_program(stop=None, dbg=None):
    nc = bass.Bass("TRN2", target_bir_lowering=False)

    def din(name, shape):
        return nc.dram_tensor(name, list(shape), F32, kind="ExternalInput").ap()

    def dout(name, shape):
        return nc.dram_tensor(name, list(shape), F32, kind="ExternalOutput").ap()

    xp = din("xp", [S, D]); xs = din("xs", [SD, D]); cT = din("cT", [128, 16])
    latc = din("latc", [2048, 256]); krc = din("krc", [2048, 32])
    bkc = din("bkc", [512, 512]); bvc = din("bvc", [512, 512])
    w_ada = din("w_ada", [D, 6 * D]); b_ada = din("b_ada", [1, 6 * D])
    g_norm_mix = din("g_norm_mix", [1, D]); w_in = din("w_in", [D, 4256])
    g_q_lora = din("g_q_lora", [1, 384]); w_q_up = din("w_q_up", [384, 768])
    g_kv_lora = din("g_kv_lora", [1, 256]); w_kv_up = din("w_kv_up", [256, 1024])
    g_qn_a = din("g_qn_a", [1, 64]); g_kn_a = din("g_kn_a", [1, 64])
    g_qr_a = din("g_qr_a", [1, 32]); g_kr_a = din("g_kr_a", [1, 32])
    g_q_b = din("g_q_b", [1, 64]); g_k_b = din("g_k_b", [1, 64])
    rel_bias = din("rel_bias", [8, 257])
    w_o_a = din("w_o_a", [512, D]); w_o_b = din("w_o_b", [512, D]); w_out = din("w_out", [D, D])
    g_norm_ffn = din("g_norm_ffn", [1, D])
    w_gate = din("w_gate", [D, DFF]); w_up = din("w_up", [D, DFF]); w_down = din("w_down", [DFF, D])
    identd = din("identd", [128, 128]); antid = din("antid", [128, 128])
    cstab = din("cstab", [128, 17, 32]); sntab = din("sntab", [128, 17, 32])

    yp = dout("yp", [S, D]); ys = dout("ys", [SD, D])
    latp = dout("latp", [S, 256]); krp = dout("krp", [S, 32])
    bkp = dout("bkp", [512, 512]); bvp = dout("bvp", [512, 512])
    lats = dout("lats", [SD, 256]); krs = dout("krs", [SD, 32])
    bks = dout("bks", [SD, 512]); bvs = dout("bvs", [SD, 512])
    ext2 = nc.dram_tensor("ext2", [8, 768], F32, kind="Internal").ap()

    def bc_rows(src, nparts, n, off=0):
        return bass.AP(src.tensor, off, [[0, nparts], [1, n]])

    with ExitStack() as es:
        arena_t = es.enter_context(nc.sbuf_tensor("arena", [128, ARENA_WORDS], F32))
        PS = [es.enter_context(nc.psum_tensor("ps%d" % i, [128, 1024], F32)) for i in range(4)]
        P = Prog(nc)
        P.track_dram.add("ext2")
        A = Arena(arena_t, ARENA_WORDS)

        def bank(b):
            return PS[b // 2][:, (b % 2) * 512:(b % 2) * 512 + 512]

        def bankbf(b):
            return PS[b // 2][:, (b % 2) * 512:(b % 2) * 512 + 512].bitcast(BF16)

        def dbl(d):
            return PS[d][:, :]

        def recip(out, in_):
            return P.add("dve", lambda e: e.reciprocal(out, in_), reads=(in_,), writes=(out,))

        ident_b = A.alloc([128], BF16)
        identf = A.alloc([128], F32)
        cs_sb = A.alloc([17, 32], F32)
        sn_sb = A.alloc([17, 32], F32)
        epsT = A.alloc([1], F32)
        modv = A.alloc([6, 8, 2], F32)
        ones_b = A.alloc([128], BF16)
        P.dma("pool", ident_b, identd)
        P.dma("sp", identf, identd)
        P.dma("sp", cs_sb, cstab)
        P.dma("sp", sn_sb, sntab)
        P.memset("pool", epsT, EPS)
        P.memset("pool", ones_b, 1.0)
        persist_mark = A.mark()

        def checkpoint(name):
            if stop == name:
                if dbg:
                    for (dname, fn) in dbg.items():
                        ap = fn(locals_ref)
                        shp = list(ap.shape)
                        dt_ = nc.dram_tensor(dname, shp, ap.dtype, kind="ExternalOutput").ap()
                        P.dma("sp", dt_, ap)
                raise _Stop()

        locals_ref = {}

        def rstd(ssq, inv_n, out, T):
            P.act(out, ssq, AF.Ln, bias=epsT[0:T, 0:1], scale=inv_n)
            P.act(out, out, AF.Exp, scale=-0.5)

        try:
            sT = A.alloc([8, 2], BF16)
            cTf = A.alloc([8, 2], F32)
            m2 = A.alloc([6 * D], F32)
            bad = A.alloc([6 * D], F32)
            gm2 = A.alloc([2, D], F32)
            rows = A.alloc([2, D], F32)
            wring = Ring(A, 3, [8, 1024], BF16)
            P.dma("sp", cTf, cT.rearrange("p (k j) -> p k j", j=2))
            P.act(sT, cTf, AF.Silu)
            P.dma("sp", bad[0:2, :], bc_rows(b_ada, 2, 6 * D))
            P.dma("sp", gm2[0:2, 0, :], bc_rows(g_norm_mix, 2, D))
            P.dma("sp", gm2[0:2, 1, :], bc_rows(g_norm_ffn, 2, D))
            w_ada_v = w_ada.rearrange("(k p) c -> p k c", p=128)
            for cc in range(6):
                wch = wring.next()
                P.dma("pool", wch, w_ada_v[:, :, cc * 1024:(cc + 1) * 1024])
                for half in range(2):
                    pb = bank(cc % 2 * 2 + half)
                    for k in range(8):
                        P.mm(pb[0:2, :], sT[:, k, :], wch[:, k, half * 512:(half + 1) * 512], start=(k == 0), stop=(k == 7))
                    col = cc * 1024 + half * 512
                    P.tt("dve", m2[0:2, col:col + 512], pb[0:2, :], bad[0:2, col:col + 512], ALU.add)
            P.stt("dve", rows[0:2, 0, :], m2[0:2, 1024:2048], 1.0, gm2[0:2, 0, :], ALU.add, ALU.mult)
            P.stt("dve", rows[0:2, 1, :], m2[0:2, 4096:5120], 1.0, gm2[0:2, 1, :], ALU.add, ALU.mult)
            vecs = [m2[0:2, 0:1024], rows[0:2, 0, :], m2[0:2, 3072:4096], rows[0:2, 1, :], m2[0:2, 2048:3072],
                    m2[0:2, 5120:6144]]
            pb = bank(4)
            vhi = A.alloc([D], BF16)
            vlo = A.alloc([D], BF16)
            for v in range(6):
                P.copy("dve", vhi[0:2, :], vecs[v])
                P.tt("dve", vlo[0:2, :], vecs[v], vhi[0:2, :], ALU.subtract)
                for k in range(8):
                    c = (v * 8 + k) * 2
                    P.mm(pb[:, c:c + 2], vhi[0:2, k * 128:(k + 1) * 128], ident_b[0:2, 0:2], start=True, stop=False)
                    P.mm(pb[:, c:c + 2], vlo[0:2, k * 128:(k + 1) * 128], ident_b[0:2, 0:2], start=False, stop=True)
            P.copy("dve", modv, pb[:, 0:96].rearrange("p (v k j) -> p v k j", k=8, j=2))
            A.reset(persist_mark)
            locals_ref.update(modv=modv, m2=m2)
            checkpoint("p0")

            def build_gbc(vi, j, dst, dring):
                dhring = Ring(A, 4, [128], BF16)
                for half in range(2):
                    pb = bank(2 + half)
                    for c in range(4):
                        k = half * 4 + c
                        dg = dring.next()
                        dh = dhring.next()
                        dl = dhring.next()
                        P.ts("dve", dg, identf, modv[:, vi, k, j:j + 1], None, ALU.mult)
                        P.copy("dve", dh, dg)
                        P.tt("dve", dl, dg, dh, ALU.subtract)
                        P.mm(pb[:, c * 128:(c + 1) * 128], ones_b, dh, start=True, stop=False)
                        P.mm(pb[:, c * 128:(c + 1) * 128], ones_b, dl, start=False, stop=True)
                    P.copy("act", dst[:, half * 512:(half + 1) * 512], pb)

            def make_hT(xsrc, T, j, vA, vS, dst, junkb, xn, st):
                checkpoint("m0")
                P.act(junkb[0:T, :], xsrc, AF.Square, accum_out=st[0:T, 0:1])
                checkpoint("m1")
                rstd(st[0:T, 0:1], 1.0 / D, st[0:T, 1:2], T)
                checkpoint("m2")
                P.act(xn[0:T, :], xsrc, AF.Copy, scale=st[0:T, 1:2])
                checkpoint("m3")
                pb = bankbf(0)
                for k in range(8):
                    P.tr(pb[:, k * T:(k + 1) * T], xn[0:T, k * 128:(k + 1) * 128], ident_b[0:T, 0:T])
                checkpoint("m4")
                for k in range(8):
                    if k % 2 == 0:
                        P.ts("dve", dst[:, k, 0:T], pb[:, k * T:(k + 1) * T], modv[:, vA, k, j:j + 1],
                             modv[:, vS, k, j:j + 1], ALU.mult, ALU.add)
                checkpoint("m5")
                for k in range(8):
                    if k % 2 == 1:
                        P.act(dst[:, k, 0:T], pb[:, k * T:(k + 1) * T], AF.Identity, bias=modv[:, vS, k, j:j + 1],
                              scale=modv[:, vA, k, j:j + 1])
                checkpoint("m6")

            def rope(x, H, ti, T, out, ra, rb):
                csb = cs_sb[0:T, ti, :].unsqueeze(1).to_broadcast([T, H, 32])
                sn1 = sn_sb[0:T, ti, 0:16].unsqueeze(1).to_broadcast([T, H, 16])
                sn2 = sn_sb[0:T, ti, 16:32].unsqueeze(1).to_broadcast([T, H, 16])
                P.tt("pool", ra[0:T, 0:H, :], x, csb, ALU.mult)
                P.tt("pool", rb[0:T, 0:H, 0:16], x[:, :, 16:32], sn1, ALU.mult)
                P.tt("pool", rb[0:T, 0:H, 16:32], x[:, :, 0:16], sn2, ALU.mult)
                P.tt("pool", out, ra[0:T, 0:H, :], rb[0:T, 0:H, :], ALU.add)

            def attend_head(h, qT, ktiles, N, dst, sbanks, obank, ptring, bring, rring, cb):
                O = obank
                nkt = len(ktiles)

                def issue_S(i):
                    kt = ktiles[i]
                    Sb = sbanks.next()
                    c0, c1 = kt.union
                    P.mm(Sb[0:kt.nk, c0:c1], kt.kT(h), qT[:, c0:c1])
                    return Sb

                Sn = issue_S(0)
                for i in range(nkt):
                    Sb = Sn
                    if i + 1 < nkt:
                        Sn = issue_S(i + 1)
                    kt = ktiles[i]
                    PT = ptring.next()
                    for (p0, p1, (a, b)) in kt.parts:
                        if a >= b:
                            continue
                        src = Sb[p0:p1, a:b]
                        if kt.bias is not None:
                            tb = bring.next()
                            P.tt("dve", tb[p0:p1, a:b], src, kt.bias(h)[p0:p1, a:b], ALU.add)
                            P.act(PT[p0:p1, a:b], tb[p0:p1, a:b], AF.Exp)
                        elif kt.cbias:
                            P.act(PT[p0:p1, a:b], src, AF.Exp, bias=cb[p0:p1, h:h + 1])
                        else:
                            P.act(PT[p0:p1, a:b], src, AF.Exp)
                    for (p0, p1, (a, b)) in kt.zeros:
                        P.memset("pool", PT[p0:p1, a:b], 0.0)
                    c0, c1 = kt.union
                    P.mm(O[:, c0:c1], kt.vaug(h), PT[0:kt.nk, c0:c1], start=(i == 0), stop=(i == nkt - 1))
                rb = rring.next()
                if h % 2 == 0:
                    recip(rb[0:64, 0:N], O[64:128, 0:N])
                    P.tt("dve", dst, O[0:64, 0:N], rb[0:64, 0:N], ALU.mult)
                else:
                    recip(rb[64:128, 0:N], O[0:64, 0:N])
                    P.tt("dve", dst, O[64:128, 0:N], rb[64:128, 0:N], ALU.mult)

            class BankRing:
                def __init__(self, ids):
                    self.ids = ids
                    self.i = 0

                def next(self):
                    b = bank(self.ids[self.i % len(self.ids)])
                    self.i += 1
                    return b

            w_in_v = w_in.rearrange("(k p) c -> p k c", p=128)
            oTa = A.alloc([4, S + SD], BF16, top=True)
            oTb = A.alloc([4, S + SD], BF16, top=True)
            p1_mark = A.mark()

            def xsrc_of(ti):
                return (xs, SD, 1) if ti == 16 else (xp[ti * 128:(ti + 1) * 128, :], 128, 0)

            KT = A.alloc([8, 2048], BF16)
            KTn = A.alloc([8, SD], BF16)
            Vst = A.alloc([16, 768], BF16)
            Vn = A.alloc([768], BF16)
            w1 = A.alloc([8, 672], BF16)
            wq = A.alloc([3, 768], BF16)
            wkv = A.alloc([2, 1024], BF16)
            gql = A.alloc([384]); gkv = A.alloc([256]); gqn_s = A.alloc([64]); gqr_s = A.alloc([32])
            gkn = A.alloc([64]); gkr = A.alloc([32])
            P.dma("pool", w1, w_in_v[:, :, 0:672])
            P.dma("pool", wq, w_q_up.rearrange("(k p) c -> p k c", p=128))
            P.dma("pool", wkv, w_kv_up.rearrange("(k p) c -> p k c", p=128))
            checkpoint("a00")
            for (dst, src, n) in [(gql, g_q_lora, 384), (gkv, g_kv_lora, 256), (gqn_s, g_qn_a, 64), (gqr_s, g_qr_a, 32),
                                  (gkn, g_kn_a, 64), (gkr, g_kr_a, 32)]:
                P.dma("sp", dst, bc_rows(src, 128, n))
            checkpoint("a01")
            P.ts("dve", gqn_s, gqn_s, 96.0 ** -0.5, None, ALU.mult)
            P.ts("dve", gqr_s, gqr_s, 96.0 ** -0.5, None, ALU.mult)
            checkpoint("a02")
            P.memset("dve", Vst, 1.0)
            P.memset("dve", Vn, 1.0)
            xring = Ring(A, 2, [D], F32)
            junkb = A.alloc([D], BF16)
            xnr = Ring(A, 2, [D], BF16)
            hTr = Ring(A, 2, [8, 128], BF16)
            sring = Ring(A, 4, [32], F32)
            sqf = A.alloc([D], F32)
            tqf = A.alloc([D], F32)
            cqnr = Ring(A, 2, [384], BF16)
            cqnTr = Ring(A, 2, [3, 128], BF16)
            qanr = Ring(A, 2, [512], BF16)
            qrrr = Ring(A, 2, [256], BF16)
            kanr = Ring(A, 4, [512], BF16)
            krbr = Ring(A, 4, [32], BF16)
            lat32r = Ring(A, 4, [256], F32)
            lat16r = Ring(A, 4, [256], BF16)
            latTr = Ring(A, 4, [2, 128], BF16)
            krnr = Ring(A, 4, [32], F32)
            kr32r = Ring(A, 2, [32], F32)
            kr16r = Ring(A, 2, [32], BF16)
            rar = Ring(A, 2, [8, 32], F32)
            rbr = Ring(A, 2, [8, 32], F32)
            QT = A.alloc([8, 512], BF16)
            ptring = Ring(A, 4, [512], BF16)
            rring = Ring(A, 2, [512], F32)

            def kv_from_lat(lat16, T, ka, st, Vdst):
                pb = bankbf(0)
                for k in range(2):
                    P.tr(pb[:, k * T:(k + 1) * T], lat16[0:T, k * 128:(k + 1) * 128], ident_b[0:T, 0:T])
                latT = latTr.next()
                P.copy("dve", latT[:, :, 0:T], pb[:, 0:2 * T].rearrange("p (k t) -> p k t", t=T))
                checkpoint("k2")
                kz = dbl(3)
                for half in range(2):
                    for k in range(2):
                        P.mm(kz[0:T, half * 512:(half + 1) * 512], latT[:, k, 0:T], wkv[:, k, half * 512:(half + 1) * 512],
                             start=(k == 0), stop=(k == 1))
                kv = kz[0:T, :].rearrange("p (h d) -> p h d", d=128)
                sqv = sqf[0:T, 0:512].rearrange("p (h d) -> p h d", d=64)
                checkpoint("k3")
                P.act(sqv, kv[:, :, 0:64], AF.Square)
                checkpoint("k4")
                P.reduce("dve", st[0:T, 22:30], sqv, ALU.add)
                checkpoint("k5")
                rstd(st[0:T, 22:30], 1.0 / 64, st[0:T, 22:30], T)
                checkpoint("k6")
                tk = tqf[0:T, 0:512].rearrange("p (h d) -> p h d", d=64)
                P.tt("dve", tk, kv[:, :, 0:64], st[0:T, 22:30].unsqueeze(2).to_broadcast([T, 8, 64]), ALU.mult)
                P.tt("pool", ka[0:T, :].rearrange("p (h d) -> p h d", d=64), tk,
                     gkn[0:T, :].unsqueeze(1).to_broadcast([T, 8, 64]), ALU.mult)
                checkpoint("k7")
                vsrc = kv[:, :, 64:128].rearrange("p (j e) d -> p j e d", e=2)
                vd = Vdst.rearrange("p (j e d) -> p j e d", e=3, d=64)
                P.copy("act", vd[:, :, 0, :], vsrc[:, :, 0, :])
                P.copy("act", vd[:, :, 2, :], vsrc[:, :, 1, :])
                checkpoint("k8")

            def k_transposes(kan, krb, T, KTdst):
                pb = bankbf(1)
                for pr in range(4):
                    P.tr(pb[:, pr * T:(pr + 1) * T], kan[0:T, pr * 128:(pr + 1) * 128], ident_b[0:T, 0:T])
                P.tr(pb[0:32, 4 * T:5 * T], krb[0:T, :], ident_b[0:T, 0:T])
                pv = pb[:, 0:4 * T].rearrange("p (j t) -> p j t", t=T)
                kd = KTdst.rearrange("p (j e) t -> p j e t", e=2)
                P.copy("act", kd[0:64, :, 0, :], pv[0:64, :, :])
                P.copy("dve", kd[0:64, :, 1, :], pv[64:128, :, :])
                P.copy("dve", KTdst[64:96, :, :], pb[0:32, 4 * T:5 * T].unsqueeze(1).to_broadcast([32, 8, T]))

            def p1a_tile(ti, QTdst, KTdst, Vdst, lat_out, kr_out):
                xsrc, T, j = xsrc_of(ti)
                xt = xring.next()
                P.dma("sp", xt[0:T, :], xsrc)
                hT = hTr.next()
                st = sring.next()
                make_hT(xt[0:T, :], T, j, 1, 0, hT, junkb, xnr.next(), st)
                if ti == 16:
                    checkpoint("t1")
                z = dbl(1)
                for k in range(8):
                    P.mm(z[0:T, 0:384], hT[:, k, 0:T], w1[:, k, 0:384], start=(k == 0), stop=(k == 7))
                for k in range(8):
                    P.mm(z[0:T, 512:800], hT[:, k, 0:T], w1[:, k, 384:672], start=(k == 0), stop=(k == 7))
                if ti == 16:
                    checkpoint("t2")
                P.act(junkb[0:T, 0:384], z[0:T, 0:384], AF.Square, accum_out=st[0:T, 2:3])
                rstd(st[0:T, 2:3], 1.0 / 384, st[0:T, 3:4], T)
                cqn = cqnr.next()
                P.stt("dve", cqn[0:T, :], z[0:T, 0:384], st[0:T, 3:4], gql[0:T, :], ALU.mult, ALU.mult)
                pb = bankbf(0)
                for k in range(3):
                    P.tr(pb[:, k * T:(k + 1) * T], cqn[0:T, k * 128:(k + 1) * 128], ident_b[0:T, 0:T])
                cqnT = cqnTr.next()
                P.copy("dve", cqnT[:, :, 0:T], pb[:, 0:3 * T].rearrange("p (k t) -> p k t", t=T))
                if ti == 16:
                    checkpoint("t3")
                qz = dbl(2)
                for (a, b) in ((0, 512), (512, 768)):
                    for k in range(3):
                        P.mm(qz[0:T, a:b], cqnT[:, k, 0:T], wq[:, k, a:b], start=(k == 0), stop=(k == 2))
                qv = qz[0:T, 0:768].rearrange("p (h d) -> p h d", d=96)
                sqv = sqf[0:T, 0:768].rearrange("p (h d) -> p h d", d=96)
                P.act(sqv, qv, AF.Square)
                P.reduce("dve", st[0:T, 4:12], sqv[:, :, 0:64], ALU.add)
                P.reduce("dve", st[0:T, 12:20], sqv[:, :, 64:96], ALU.add)
                rstd(st[0:T, 4:12], 1.0 / 64, st[0:T, 4:12], T)
                rstd(st[0:T, 12:20], 1.0 / 32, st[0:T, 12:20], T)
                if ti == 16:
                    checkpoint("t4")
                tq = tqf[0:T, 0:768].rearrange("p (h d) -> p h d", d=96)
                P.tt("dve", tq[:, :, 0:64], qv[:, :, 0:64], st[0:T, 4:12].unsqueeze(2).to_broadcast([T, 8, 64]), ALU.mult)
                P.tt("dve", tq[:, :, 64:96], qv[:, :, 64:96], st[0:T, 12:20].unsqueeze(2).to_broadcast([T, 8, 32]), ALU.mult)
                qan = qanr.next()
                qrr = qrrr.next()
                P.tt("pool", qan[0:T, :].rearrange("p (h d) -> p h d", d=64), tq[:, :, 0:64],
                     gqn_s[0:T, :].unsqueeze(1).to_broadcast([T, 8, 64]), ALU.mult)
                P.tt("pool", tq[:, :, 64:96], tq[:, :, 64:96], gqr_s[0:T, :].unsqueeze(1).to_broadcast([T, 8, 32]), ALU.mult)
                rope(tq[:, :, 64:96], 8, ti, T, qrr[0:T, :].rearrange("p (h d) -> p h d", d=32), rar.next(), rbr.next())
                pb = bankbf(1)
                for pr in range(4):
                    P.tr(pb[:, pr * T:(pr + 1) * T], qan[0:T, pr * 128:(pr + 1) * 128], ident_b[0:T, 0:T])
                for pr in range(2):
                    P.tr(pb[:, (4 + pr) * T:(5 + pr) * T], qrr[0:T, pr * 128:(pr + 1) * 128], ident_b[0:T, 0:T])
                pv = pb[:, 0:4 * T].rearrange("p (j t) -> p j t", t=T)
                qd = QTdst.rearrange("p (j e) t -> p j e t", e=2)
                P.copy("act", qd[0:64, :, 0, :], pv[0:64, :, :])
                P.copy("dve", qd[0:64, :, 1, :], pv[64:128, :, :])
                for hh in range(8):
                    src = pb[(hh % 4) * 32:(hh % 4) * 32 + 32, (4 + hh // 4) * T:(5 + hh // 4) * T]
                    P.copy("act" if hh % 2 == 0 else "dve", QTdst[64:96, hh, :], src)
                P.act(junkb[0:T, 0:256], z[0:T, 512:768], AF.Square, accum_out=st[0:T, 20:21])
                rstd(st[0:T, 20:21], 1.0 / 256, st[0:T, 21:22], T)
                lat32 = lat32r.next()
                P.stt("dve", lat32[0:T, :], z[0:T, 512:768], st[0:T, 21:22], gkv[0:T, :], ALU.mult, ALU.mult)
                P.dma("sp", lat_out, lat32[0:T, :])
                lat16 = lat16r.next()
                P.copy("pool", lat16[0:T, :], lat32[0:T, :])
                if ti == 16:
                    checkpoint("t8")
                kan = kanr.next()
                krb = krbr.next()
                P.act(junkb[0:T, 0:32], z[0:T, 768:800], AF.Square, accum_out=st[0:T, 30:31])
                rstd(st[0:T, 30:31], 1.0 / 32, st[0:T, 31:32], T)
                krn = krnr.next()
                P.stt("dve", krn[0:T, :], z[0:T, 768:800], st[0:T, 31:32], gkr[0:T, :], ALU.mult, ALU.mult)
                kr32 = kr32r.next()
                rope(krn[0:T, :].unsqueeze(1), 1, ti, T, kr32[0:T, :].unsqueeze(1), rar.next(), rbr.next())
                P.dma("sp", kr_out, kr32[0:T, :])
                P.copy("pool", krb[0:T, :], kr32[0:T, :])
                if ti == 16:
                    checkpoint("t9")
                kv_from_lat(lat16, T, kan, st, Vdst)
                k_transposes(kan, krb, T, KTdst)

            def full_tile(kTf, vf, nk, N, bias=None, cbias=False):
                return KTile(kTf, vf, nk, [(0, nk, (0, N))], [], (0, N), bias=bias, cbias=cbias)

            def vaug_of(vtile):
                def f(h):
                    pr = h // 2
                    o = pr * 192 + (0 if h % 2 == 0 else 64)
                    return vtile[:, o:o + 128]
                return f

            sb_a = BankRing([2, 3, 4])
            checkpoint("a0")
            for jt in range(16):
                lat16 = lat16r.next()
                lat32c = lat32r.next()
                P.dma("sp", lat32c, latc[jt * 128:(jt + 1) * 128, :])
                P.copy("dve", lat16, lat32c)
                kr32c = krnr.next()
                P.dma("sp", kr32c, krc[jt * 128:(jt + 1) * 128, :])
                kan = kanr.next()
                krb = krbr.next()
                checkpoint("k0")
                P.copy("pool", krb, kr32c)
                checkpoint("k1")
                st = sring.next()
                kv_from_lat(lat16, 128, kan, st, Vst[:, jt, :])
                k_transposes(kan, krb, 128, KT[0:96, :, jt * 128:(jt + 1) * 128])
                if jt == 0:
                    locals_ref.update(KT=KT, Vst=Vst)
                    checkpoint("a1")
                checkpoint("c%d" % jt)
            QTs = A.alloc([8, SD], BF16)
            checkpoint("a2")
            p1a_tile(16, QTs[0:96, :, :], KTn[0:96, :, :], Vn[0:SD, :], lats, krs)
            checkpoint("a3")
            for h in range(8):
                kts = []
                for jt in range(16):
                    kts.append(full_tile((lambda hh, jt=jt: KT[0:96, hh, jt * 128:(jt + 1) * 128]),
                                         (lambda hh, jt=jt: vaug_of(Vst[:, jt, :])(hh)), 128, SD))
                kts.append(full_tile((lambda hh: KTn[0:96, hh, :]), (lambda hh: vaug_of(Vn[0:SD, :])(hh)), SD, SD))
                dst = oTa[(h % 2) * 64:(h % 2) * 64 + 64, h // 2, S:S + SD]
                attend_head(h, QTs[0:96, h, :], kts, SD, dst, sb_a, bank(5 + h % 2), ptring, None, rring, None)
            locals_ref.update(oTa=oTa, KT=KT, Vst=Vst, QTs=QTs, KTn=KTn, Vn=Vn)
            checkpoint("p1as")
            for b in range(4):
                for tt_ in range(4):
                    ti = 4 * b + tt_
                    p1a_tile(ti, QT[0:96, :, tt_ * 128:(tt_ + 1) * 128], KT[0:96, :, ti * 128:(ti + 1) * 128],
                             Vst[:, ti, :], latp[ti * 128:(ti + 1) * 128, :], krp[ti * 128:(ti + 1) * 128, :])
                for h in range(8):
                    kts = []
                    for jt in range(4 * b + 4):
                        kf = (lambda hh, jt=jt: KT[0:96, hh, jt * 128:(jt + 1) * 128])
                        vf = (lambda hh, jt=jt: vaug_of(Vst[:, jt, :])(hh))
                        if jt < 4 * b:
                            kts.append(full_tile(kf, vf, 128, 512))
                        else:
                            c0 = 128 * (jt - 4 * b)
                            kts.append(KTile(kf, vf, 128, [(0, 64, (c0, 512)), (64, 128, (c0 + 64, 512))],
                                             [(64, 128, (c0, c0 + 64))], (c0, 512)))
                    dst = oTa[(h % 2) * 64:(h % 2) * 64 + 64, h // 2, b * 512:(b + 1) * 512]
                    attend_head(h, QT[0:96, h, :], kts, 512, dst, sb_a, bank(5 + h % 2), ptring, None, rring, None)
            A.reset(p1_mark)

            checkpoint("p1a")
            KBT = A.alloc([4, 1024], BF16)
            VBst = A.alloc([8, 768], BF16)
            wb = A.alloc([8, 1536], BF16)
            gqb_s = A.alloc([64]); gkb = A.alloc([64])
            cb = A.alloc([8], F32)
            biasT = [A.alloc([8, 256], BF16) for _ in range(3)]
            P.dma("pool", wb, w_in_v[:, :, 672:2208])
            P.dma("sp", gqb_s, bc_rows(g_q_b, 128, 64))
            P.dma("sp", gkb, bc_rows(g_k_b, 128, 64))
            P.ts("dve", gqb_s, gqb_s, 0.125, None, ALU.mult)
            P.dma("sp", cb.unsqueeze(2), bass.AP(rel_bias.tensor, 256, [[0, 128], [257, 8], [1, 1]]),
                  allow_slow_non_contiguous=True)
            P.memset("dve", VBst, 1.0)
            bm = A.mark()
            e8 = A.alloc([768], F32)
            antif = A.alloc([128], BF16)
            Hb = A.alloc([2048], BF16)
            Hst = A.alloc([2048], F32)
            P.dma("pool", antif, antid)
            P.memset("pool", e8[0:8, :], 0.0)
            P.dma("sp", e8[0:8, 128:385], rel_bias)
            P.copy("dve", e8[0:8, 385:768], e8[0:8, 384:385].to_broadcast([8, 383]))
            P.dma("sp", ext2, e8[0:8, :])
            for r, Dd in enumerate((128, 0, -128)):
                P.dma("sp", Hst.rearrange("p (h q) -> p h q", q=256),
                      bass.AP(ext2.tensor, 128 + Dd + 1, [[1, 128], [768, 8], [1, 256]]))
                for c in range(4):
                    pb = bank(c % 2)
                    if c == 0:
                        P.copy("dve", Hb, Hst)
                    P.mm(pb, antif, Hb[:, c * 512:(c + 1) * 512])
                    P.copy("dve", biasT[r][:, 2 * c:2 * c + 2, :], pb.rearrange("p (h q) -> p h q", q=256))
            A.reset(bm)
            xring = Ring(A, 2, [D], F32)
            junkb = A.alloc([D], BF16)
            xnr = Ring(A, 2, [D], BF16)
            hTr = Ring(A, 2, [8, 128], BF16)
            sring = Ring(A, 4, [32], F32)
            sqf = A.alloc([512], F32)
            tqf = A.alloc([512], F32)
            qb16r = Ring(A, 2, [512], BF16)
            kb32r = Ring(A, 2, [512], F32)
            kb16r = Ring(A, 2, [512], BF16)
            vb32r = Ring(A, 2, [512], F32)
            QBT = A.alloc([4, 256], BF16)
            QBTs = A.alloc([4, SD], BF16)
            ptring = Ring(A, 4, [512], BF16)
            bring = Ring(A, 2, [512], F32)
            rring = Ring(A, 2, [512], F32)

            def headnorm(zsrc, T, st, c0, gain, out, outeng="pool"):
                zv = zsrc.rearrange("p (h d) -> p h d", d=64)
                sqv = sqf[0:T, :].rearrange("p (h d) -> p h d", d=64)
                P.act(sqv, zv, AF.Square)
                P.reduce("dve", st[0:T, c0:c0 + 8], sqv, ALU.add)
                rstd(st[0:T, c0:c0 + 8], 1.0 / 64, st[0:T, c0:c0 + 8], T)
                tv = tqf[0:T, :].rearrange("p (h d) -> p h d", d=64)
                P.tt("dve", tv, zv, st[0:T, c0:c0 + 8].unsqueeze(2).to_broadcast([T, 8, 64]), ALU.mult)
                P.tt(outeng, out.rearrange("p (h d) -> p h d", d=64), tv,
                     gain[0:T, :].unsqueeze(1).to_broadcast([T, 8, 64]), ALU.mult)

            def kb_to_store(kb16, T, slot):
                pb = bankbf(1)
                for pr in range(4):
                    P.tr(pb[:, pr * T:(pr + 1) * T], kb16[0:T, pr * 128:(pr + 1) * 128], ident_b[0:T, 0:T])
                P.copy("act", KBT[:, :, slot * 128:slot * 128 + T], pb[:, 0:4 * T].rearrange("p (j t) -> p j t", t=T))

            def vb_to_store(vsrc, T, slot):
                vd = VBst[0:T, slot, :].rearrange("p (j e d) -> p j e d", e=3, d=64)
                vs = vsrc.rearrange("p (j e d) -> p j e d", e=2, d=64)
                P.copy("pool", vd[:, :, 0, :], vs[:, :, 0, :])
                P.copy("pool", vd[:, :, 2, :], vs[:, :, 1, :])

            def p1b_tile(ti, QBdst, slot, bk_out, bv_out):
                xsrc, T, j = xsrc_of(ti)
                xt = xring.next()
                P.dma("sp", xt[0:T, :], xsrc)
                hT = hTr.next()
                st = sring.next()
                make_hT(xt[0:T, :], T, j, 1, 0, hT, junkb, xnr.next(), st)
                zqk = dbl(1)
                zv = dbl(2)
                for (dstp, c0) in ((zqk[0:T, 0:512], 0), (zqk[0:T, 512:1024], 512), (zv[0:T, 0:512], 1024)):
                    for k in range(8):
                        P.mm(dstp, hT[:, k, 0:T], wb[:, k, c0:c0 + 512], start=(k == 0), stop=(k == 7))
                qb16 = qb16r.next()
                headnorm(zqk[0:T, 0:512], T, st, 2, gqb_s, qb16[0:T, :])
                pb = bankbf(1)
                for pr in range(4):
                    P.tr(pb[:, pr * T:(pr + 1) * T], qb16[0:T, pr * 128:(pr + 1) * 128], ident_b[0:T, 0:T])
                P.copy("act", QBdst, pb[:, 0:4 * T].rearrange("p (j t) -> p j t", t=T))
                kb32 = kb32r.next()
                headnorm(zqk[0:T, 512:1024], T, st, 10, gkb, kb32[0:T, :])
                if bk_out is not None:
                    P.dma("sp", bk_out, kb32[0:T, :])
                kb16 = kb16r.next()
                P.copy("pool", kb16[0:T, :], kb32[0:T, :])
                kb_to_store(kb16, T, slot)
                vb32 = vb32r.next()
                P.copy("act", vb32[0:T, :], zv[0:T, 0:512])
                if bv_out is not None:
                    P.dma("sp", bv_out, vb32[0:T, :])
                vb_to_store(vb32[0:T, :], T, slot)

            def band_k(slot, nk):
                return lambda hh: KBT[(hh % 2) * 64:(hh % 2) * 64 + 64, hh // 2, slot * 128:slot * 128 + nk]

            def band_v(slot, nk):
                return lambda hh: vaug_of(VBst[0:nk, slot, :])(hh)

            sb_b = BankRing([4, 5, 6])
            for jt in range(4):
                kb16 = kb16r.next()
                P.dma("pool", kb16, bkc[jt * 128:(jt + 1) * 128, :])
                kb_to_store(kb16, 128, jt)
                vb16 = qb16r.next()
                P.dma("pool", vb16, bvc[jt * 128:(jt + 1) * 128, :])
                vb_to_store(vb16, 128, jt)
            p1b_tile(16, QBTs[:, :, :], 4, bks, bvs)
            for h in range(8):
                kts = []
                for jt in range(3):
                    kts.append(full_tile(band_k(jt, 128), band_v(jt, 128), 128, SD, cbias=True))
                kts.append(full_tile(band_k(3, 128), band_v(3, 128), 128, SD, bias=(lambda hh: biasT[0][:, hh, :])))
                kts.append(full_tile(band_k(4, SD), band_v(4, SD), SD, SD, bias=(lambda hh: biasT[1][:, hh, :])))
                dst = oTb[(h % 2) * 64:(h % 2) * 64 + 64, h // 2, S:S + SD]
                attend_head(h, QBTs[(h % 2) * 64:(h % 2) * 64 + 64, h // 2, :], kts, SD, dst, sb_b, bank(7), ptring, bring,
                            rring, cb)
            geom = {
                0: ([(0, 64, (0, 64)), (64, 128, (0, 128))], [(0, 64, (64, 128))], (0, 128)),
                1: ([(0, 64, (0, 192)), (64, 128, (0, 256))], [(0, 64, (192, 256))], (0, 256)),
                2: ([(0, 128, (0, 256))], [], (0, 256)),
                3: ([(0, 128, (0, 256))], [], (0, 256)),
                4: ([(0, 64, (0, 256)), (64, 128, (64, 256))], [(64, 128, (0, 64))], (0, 256)),
                5: ([(0, 64, (128, 256)), (64, 128, (192, 256))], [(64, 128, (128, 192))], (128, 256)),
            }
            for i in range(8):
                for tt_ in range(2):
                    ti = 2 * i + tt_
                    last = ti >= 12
                    p1b_tile(ti, QBT[:, :, tt_ * 128:(tt_ + 1) * 128], ti % 8,
                             bkp[(ti - 12) * 128:(ti - 11) * 128, :] if last else None,
                             bvp[(ti - 12) * 128:(ti - 11) * 128, :] if last else None)
                Q0 = 256 * i
                for h in range(8):
                    kts = []
                    for t in (2, 3, 1, 0, 4, 5):
                        K0 = Q0 - 512 + 128 * t
                        if K0 < 0:
                            continue
                        slot = (K0 // 128) % 8
                        parts, zeros, union = geom[t]
                        bias = None
                        if t >= 3:
                            bias = (lambda hh, r=t - 3: biasT[r][:, hh, :])
                        kts.append(KTile(band_k(slot, 128), band_v(slot, 128), 128, parts, zeros, union, bias=bias,
                                         cbias=(t < 3)))
                    dst = oTb[(h % 2) * 64:(h % 2) * 64 + 64, h // 2, Q0:Q0 + 256]
                    attend_head(h, QBT[(h % 2) * 64:(h % 2) * 64 + 64, h // 2, :], kts, 256, dst, sb_b, bank(7), ptring,
                                bring, rring, cb)
            A.reset(p1_mark)

            locals_ref.update(oTb=oTb)
            checkpoint("p1b")
            x1 = A.alloc([17, D], F32)
            p2_mark = A.mark()
            wg2 = A.alloc([8, 2048], BF16)
            woa = A.alloc([4, D], BF16)
            wob = A.alloc([4, D], BF16)
            wout = A.alloc([8, D], BF16)
            P.dma("pool", wg2, w_in_v[:, :, 2208:4256])
            P.dma("pool", woa, w_o_a.rearrange("(k p) c -> p k c", p=128))
            P.dma("pool", wob, w_o_b.rearrange("(k p) c -> p k c", p=128))
            P.dma("pool", wout, w_out.rearrange("(k p) c -> p k c", p=128))
            gbc1 = [A.alloc([D], F32) for _ in range(2)]
            tmpo = Ring(A, 1, [D], F32)
            dring = Ring(A, 2, [128], F32)
            build_gbc(4, 0, gbc1[0], dring)
            build_gbc(4, 1, gbc1[1], dring)
            xnr = Ring(A, 1, [D], BF16)
            sring = Ring(A, 4, [32], F32)
            hT2r = Ring(A, 2, [8, 128], BF16)
            Gr = Ring(A, 2, [D], BF16)
            t1r = Ring(A, 1, [D], BF16)
            t2r = Ring(A, 1, [D], BF16)
            mixr = Ring(A, 1, [D], BF16)
            mixTr = Ring(A, 1, [8, 128], BF16)
            for ti in list(range(16)) + [16]:
                xsrc, T, j = xsrc_of(ti)
                col0 = ti * 128
                P.dma("sp", x1[0:T, ti, :], xsrc)
                hT2 = hT2r.next()
                xn = xnr.next()
                make_hT(x1[0:T, ti, :], T, j, 1, 0, hT2, xn, xn, sring.next())
                tparts = []
                for br, (wo_, oT_, pg, py, tr_) in enumerate(((woa, oTa, dbl(1), dbl(2), t1r), (wob, oTb, dbl(3), dbl(1), t2r))):
                    for half in range(2):
                        c0 = br * 1024 + half * 512
                        for k in range(8):
                            P.mm(pg[0:T, half * 512:(half + 1) * 512], hT2[:, k, 0:T], wg2[:, k, c0:c0 + 512],
                                 start=(k == 0), stop=(k == 7))
                    Gs = Gr.next()
                    P.act(Gs[0:T, :], pg[0:T, :], AF.Sigmoid)
                    for half in range(2):
                        for pr in range(4):
                            P.mm(py[0:T, half * 512:(half + 1) * 512], oT_[:, pr, col0:col0 + T],
                                 wo_[:, pr, half * 512:(half + 1) * 512], start=(pr == 0), stop=(pr == 3))
                    tb_ = tr_.next()
                    P.tt("dve", tb_[0:T, :], py[0:T, :], Gs[0:T, :], ALU.mult)
                    tparts.append(tb_)
                mixed = mixr.next()
                P.tt("pool", mixed[0:T, :], tparts[0][0:T, :], tparts[1][0:T, :], ALU.add)
                pb = bankbf(0)
                for k in range(8):
                    P.tr(pb[:, k * T:(k + 1) * T], mixed[0:T, k * 128:(k + 1) * 128], ident_b[0:T, 0:T])
                mixT = mixTr.next()
                P.copy("act", mixT[:, :, 0:T], pb[:, 0:8 * T].rearrange("p (k t) -> p k t", t=T))
                po = dbl(2)
                for half in range(2):
                    for fc in range(8):
                        P.mm(po[0:T, half * 512:(half + 1) * 512], mixT[:, fc, 0:T],
                             wout[:, fc, half * 512:(half + 1) * 512], start=(fc == 0), stop=(fc == 7))
                to = tmpo.next()
                P.tt("dve", to[0:T, :], po[0:T, :], gbc1[j][0:T, :], ALU.mult)
                P.tt("pool", x1[0:T, ti, :], x1[0:T, ti, :], to[0:T, :], ALU.add)
            A.reset(p2_mark)

            locals_ref.update(x1=x1)
            checkpoint("p2")
            A.n = ARENA_WORDS
            h2T = A.alloc([8, S + SD], BF16)
            gbc2 = [A.alloc([D], F32) for _ in range(2)]
            dring = Ring(A, 2, [128], F32)
            build_gbc(5, 0, gbc2[0], dring)
            build_gbc(5, 1, gbc2[1], dring)
            xnr = Ring(A, 1, [D], BF16)
            sring = Ring(A, 4, [32], F32)
            for ti in range(17):
                T = 128 if ti < 16 else SD
                j = 0 if ti < 16 else 1
                xn_ = xnr.next()
                make_hT(x1[0:T, ti, :], T, j, 3, 2, h2T[:, :, ti * 128:ti * 128 + T], xn_, xn_, sring.next())
            GS = 4
            groups = []
            c = 0
            while c < 22:
                g = min(GS, 22 - c)
                groups.append((c, g))
                c += g
            wgr = Ring(A, 2, [8, GS * 128], BF16)
            wur = Ring(A, 2, [8, GS * 128], BF16)
            wdr = Ring(A, 2, [GS, D], BF16)
            aTr = Ring(A, 2, [GS, 512], BF16)
            sgr = Ring(A, 2, [512], F32)
            tmpo = Ring(A, 1, [D], F32)
            w_gate_v = w_gate.rearrange("(k p) c -> p k c", p=128)
            w_up_v = w_up.rearrange("(k p) c -> p k c", p=128)
            w_down_v = w_down.rearrange("(g p) c -> p g c", p=128)
            gub = BankRing([0, 1, 2, 3])
            blocks = [(0, 512), (512, 512), (1024, 512), (1536, 512), (2048, SD)]
            for gi, (c0, g) in enumerate(groups):
                wg_ = wgr.next(); wu_ = wur.next(); wd_ = wdr.next()
                P.dma("pool", wg_[:, :, 0:g * 128], w_gate_v[:, :, c0 * 128:(c0 + g) * 128])
                P.dma("pool", wu_[:, :, 0:g * 128], w_up_v[:, :, c0 * 128:(c0 + g) * 128])
                P.dma("pool", wd_[:, 0:g, :], w_down_v[:, c0:c0 + g, :])
                lastg = gi == len(groups) - 1
                for (t0, n) in blocks:
                    j = 0 if t0 < S else 1
                    aT = aTr.next()
                    for gg in range(g):
                        pg = gub.next()
                        for k in range(8):
                            P.mm(pg[:, 0:n], wg_[:, k, gg * 128:(gg + 1) * 128], h2T[:, k, t0:t0 + n], start=(k == 0), stop=(k == 7))
                        pu = gub.next()
                        for k in range(8):
                            P.mm(pu[:, 0:n], wu_[:, k, gg * 128:(gg + 1) * 128], h2T[:, k, t0:t0 + n], start=(k == 0), stop=(k == 7))
                        sg = sgr.next()
                        P.act(sg[:, 0:n], pg[:, 0:n], AF.Silu)
                        P.tt("dve", aT[:, gg, 0:n], pu[:, 0:n], sg[:, 0:n], ALU.mult)
                    ntile = (n + 127) // 128
                    for tl in range(ntile):
                        ti = t0 // 128 + tl
                        T = min(128, n - tl * 128)
                        po = dbl(2 + tl % 2)
                        for half in range(2):
                            for gg in range(g):
                                P.mm(po[0:T, half * 512:(half + 1) * 512], aT[:, gg, tl * 128:tl * 128 + T],
                                     wd_[:, gg, half * 512:(half + 1) * 512], start=(gg == 0), stop=(gg == g - 1))
                        to = tmpo.next()
                        P.tt("dve", to[0:T, :], po[0:T, :], gbc2[j][0:T, :], ALU.mult)
                        P.tt("pool", x1[0:T, ti, :], x1[0:T, ti, :], to[0:T, :], ALU.add)
                        if lastg:
                            if ti < 16:
                                P.dma("sp", yp[ti * 128:(ti + 1) * 128, :], x1[0:T, ti, :])
                            else:
                                P.dma("sp", ys, x1[0:T, ti, :])
        except _Stop:
            pass
        P.emit()
        print("arena peak words", A.peak, "of", ARENA_WORDS, "n_ops", len(P.all), {e: len(P.ops[e]) for e in P.ENGS})
    return nc


_NC_CACHE = {}


def _consts():
    ident = np.eye(128, dtype=np.float32)
    anti = np.ascontiguousarray(ident[::-1])
    half = 16
    inv_freq = (10000.0 ** (-np.arange(half, dtype=np.float32) / half)).astype(np.float32)
    cs = np.zeros((128, 17, 32), np.float32)
    sn = np.zeros((128, 17, 32), np.float32)
    for ti in range(17):
        pos = (np.arange(128) + ti * 128).astype(np.float32)
        ang = (pos[:, None] * inv_freq[None, :]).astype(np.float32)
        c = np.cos(ang).astype(np.float32)
        s = np.sin(ang).astype(np.float32)
        cs[:, ti, 0:16] = c
        cs[:, ti, 16:32] = c
        sn[:, ti, 0:16] = -s
        sn[:, ti, 16:32] = s
    return ident, anti, cs, sn


def kernel(x_prompt, x_sample, c_prompt, c_sample, cache_kv_latent, cache_k_rope, cache_band_k, cache_band_v,
           w_ada, b_ada, g_norm_mix, w_in, g_q_lora, w_q_up, g_kv_lora, w_kv_up, g_qn_a, g_kn_a, g_qr_a, g_kr_a,
           g_q_b, g_k_b, rel_bias, w_o_a, w_o_b, w_out, g_norm_ffn, w_gate, w_up, w_down):
    f = lambda a: np.ascontiguousarray(np.asarray(a, dtype=np.float32))
    if "nc" not in _NC_CACHE:
        _NC_CACHE["nc"] = build_program()
    nc = _NC_CACHE["nc"]
    ident, anti, cs, sn = _consts()
    shared = {
        "w_ada": f(w_ada[0]), "b_ada": f(b_ada[0]).reshape(1, -1), "g_norm_mix": f(g_norm_mix[0]).reshape(1, -1),
        "w_in": f(w_in[0]), "g_q_lora": f(g_q_lora[0]).reshape(1, -1), "w_q_up": f(w_q_up[0]),
        "g_kv_lora": f(g_kv_lora[0]).reshape(1, -1), "w_kv_up": f(w_kv_up[0]),
        "g_qn_a": f(g_qn_a[0]).reshape(1, -1), "g_kn_a": f(g_kn_a[0]).reshape(1, -1),
        "g_qr_a": f(g_qr_a[0]).reshape(1, -1), "g_kr_a": f(g_kr_a[0]).reshape(1, -1),
        "g_q_b": f(g_q_b[0]).reshape(1, -1), "g_k_b": f(g_k_b[0]).reshape(1, -1),
        "rel_bias": f(rel_bias[0]), "w_o_a": f(w_o_a[0]), "w_o_b": f(w_o_b[0]), "w_out": f(w_out[0]),
        "g_norm_ffn": f(g_norm_ffn[0]).reshape(1, -1), "w_gate": f(w_gate[0]), "w_up": f(w_up[0]),
        "w_down": f(w_down[0]), "identd": ident, "antid": anti, "cstab": cs, "sntab": sn,
    }
    in_maps = []
    for b in range(8):
        cpair = np.stack([np.asarray(c_prompt[b], np.float32), np.asarray(c_sample[b], np.float32)], axis=-1)
        cTl = np.ascontiguousarray(cpair.reshape(8, 128, 2).transpose(1, 0, 2).reshape(128, 16))
        m = dict(shared)
        m.update({
            "xp": f(x_prompt[b]), "xs": f(x_sample[b]), "cT": cTl,
            "latc": f(cache_kv_latent[0, b]), "krc": f(cache_k_rope[0, b]),
            "bkc": f(cache_band_k[0, b]).reshape(512, 512), "bvc": f(cache_band_v[0, b]).reshape(512, 512),
        })
        in_maps.append(m)
    res = run_bass_kernel_spmd(nc, in_maps, core_ids=list(range(8)))
    R = res.results
    st = lambda k: np.stack([np.asarray(R[b][k], np.float32) for b in range(8)], axis=0)
    y_p = st("yp"); y_s = st("ys")
    lat_p = st("latp")[None]; kr_p = st("krp")[None]
    bk_p = st("bkp").reshape(8, 512, 8, 64)[None]; bv_p = st("bvp").reshape(8, 512, 8, 64)[None]
    lat_s = st("lats")[None]; kr_s = st("krs")[None]
    bk_s = st("bks").reshape(8, SD, 8, 64)[None]; bv_s = st("bvs").reshape(8, SD, 8, 64)[None]
    return (y_p, y_s, lat_p, kr_p, bk_p, bv_p, lat_s, kr_s, bk_s, bv_s)
```

```python
import numpy as np
import concourse.bass as bass
import concourse.mybir as mybir
from concourse.bass_utils import run_bass_kernel_spmd

F32 = mybir.dt.float32
BF16 = mybir.dt.bfloat16
ALU = mybir.AluOpType
AF = mybir.ActivationFunctionType
AX = mybir.AxisListType


class _Op:
    __slots__ = ("eng", "emit", "dma", "deps", "signal", "count", "dsem", "dtarget", "pre", "seq")

    def __init__(self, eng, emit, dma):
        self.eng = eng
        self.emit = emit
        self.dma = dma
        self.deps = []
        self.signal = False
        self.count = 0
        self.dsem = None
        self.dtarget = 0
        self.pre = None
        self.seq = 0


def _region(ap):
    t = ap.tensor
    pat = ap.ap
    off = ap.offset
    sp = str(ap.space)
    if sp in ("SB", "PSUM"):
        sz = mybir.dt.size(ap.dtype)
        pstride = pat[0][0]
        npart = pat[0][1]
        if pstride == 0:
            pstride = 1 << 30
        p0 = off // pstride
        f0 = (off % pstride) * sz
        ext = 1
        for st, cnt in pat[1:]:
            ext += abs(st) * (cnt - 1)
        if sp == "PSUM":
            return (sp + ":" + t.name, (p0 // 32) * 32, ((p0 + npart + 31) // 32) * 32,
                    (f0 // 2048) * 2048, ((f0 + ext * sz + 2047) // 2048) * 2048)
        return (sp + ":" + t.name, p0, p0 + npart, f0, f0 + ext * sz)
    lo = off
    hi = off
    for st, cnt in pat:
        if st >= 0:
            hi += st * (cnt - 1)
        else:
            lo += st * (cnt - 1)
    return ("D:" + t.name, 0, 1, lo, hi + 1)


def _ovl(a, b):
    return a[1] < b[2] and b[1] < a[2] and a[3] < b[4] and b[3] < a[4]


def _contains(a, b):
    return a[1] <= b[1] and b[2] <= a[2] and a[3] <= b[3] and b[4] <= a[4]


class Prog:
    ENGS = ("pe", "act", "dve", "pool", "sp")

    def __init__(self, nc, n_dma_sp=28, n_dma_pool=12, n_dma_act=8):
        self.nc = nc
        self.ops = {e: [] for e in self.ENGS}
        self.all = []
        self.wr = {}
        self.rd = {}
        self.track_dram = set()
        self.last_cls = {}
        self.ring = {"sp": n_dma_sp, "pool": n_dma_pool, "act": n_dma_act}
        self.ndma = {"sp": 0, "pool": 0, "act": 0}
        self.dma_ops = {"sp": [], "pool": [], "act": []}

    def _tracked(self, ap):
        sp = str(ap.space)
        if sp in ("SB", "PSUM"):
            return True
        return ap.tensor.name in self.track_dram

    POOL_TO = "dve"
    SERIAL = ("adp",)

    def add(self, eng, emit, reads=(), writes=(), dma=False):
        if eng == "pool" and not dma and self.POOL_TO:
            eng = self.POOL_TO
        op = _Op(eng, emit, dma)
        op.seq = len(self.all)
        deps = set()
        for ap in reads:
            if ap is None or isinstance(ap, (int, float)) or not self._tracked(ap):
                continue
            r = _region(ap)
            for (wr, wop) in self.wr.get(r[0], ()):
                if _ovl(wr, r):
                    deps.add((wop, "raw"))
            lst = self.rd.setdefault(r[0], [])
            if not dma:
                lst[:] = [(rr, rop) for (rr, rop) in lst if not (rop.eng == eng and not rop.dma and _contains(r, rr))]
            lst.append((r, op))
        for ap in writes:
            if ap is None or not self._tracked(ap):
                continue
            w = _region(ap)
            wl = self.wr.setdefault(w[0], [])
            for (wr, wop) in wl:
                if _ovl(wr, w):
                    deps.add((wop, "waw"))
            rl = self.rd.setdefault(w[0], [])
            for (rr, rop) in rl:
                if _ovl(rr, w) and rop is not op:
                    deps.add((rop, "war"))
            wl[:] = [(wr, wop) for (wr, wop) in wl if not _contains(w, wr)]
            wl.append((w, op))
            rl[:] = [(rr, rop) for (rr, rop) in rl if not _contains(w, rr) or rop is op]
        mode = self.SERIAL
        if mode is True and self.all:
            deps.add((self.all[-1], "ser"))
        elif mode:
            touches_psum = any((a is not None) and (not isinstance(a, (int, float))) and str(a.space) == "PSUM"
                               for a in list(reads) + list(writes))
            cls = []
            if "ad" in mode and eng in ("act", "dve") and not dma:
                cls.append("ad")
            if "adp" in mode and eng in ("act", "dve") and not dma and touches_psum:
                for a in list(reads) + list(writes):
                    if (a is not None) and (not isinstance(a, (int, float))) and str(a.space) == "PSUM":
                        r = _region(a)
                        for bk in range(r[3] // 2048, r[4] // 2048):
                            cls.append("adp:%s:%d" % (r[0], bk))
            if "psum" in mode and touches_psum:
                cls.append("psum")
            if "dma" in mode:
                cls.append("dma_any" if dma else "dma_cmp")
            for c in cls:
                if c == "dma_cmp":
                    p = self.last_cls.get("dma")
                    if p is not None:
                        deps.add((p, "ser"))
                    self.last_cls["cmp"] = None
                elif c == "dma_any":
                    p = self.last_cls.get("any")
                    if p is not None:
                        deps.add((p, "ser"))
                else:
                    p = self.last_cls.get(c)
                    if p is not None:
                        deps.add((p, "ser"))
        for (p, kind) in deps:
            if p is op:
                continue
            if kind == "ser":
                if not (p.eng == eng and eng == "pe" and not p.dma):
                    op.deps.append(p)
                    if not p.dma:
                        p.signal = True
                continue
            if not p.dma and p.eng == eng:
                if eng == "pe":
                    continue
            op.deps.append(p)
            if not p.dma:
                p.signal = True
        if dma:
            n = self.ndma[eng]
            ring = self.ring[eng]
            op.dsem = (eng, n % ring)
            op.dtarget = 16 * (n // ring + 1)
            if n >= ring:
                op.pre = self.dma_ops[eng][n - ring]
            self.dma_ops[eng].append(op)
            self.ndma[eng] = n + 1
        self.ops[eng].append(op)
        self.all.append(op)
        mode = self.SERIAL
        if mode and mode is not True:
            self.last_cls["any"] = op
            if dma:
                self.last_cls["dma"] = op
            if "ad" in mode and eng in ("act", "dve") and not dma:
                self.last_cls["ad"] = op
            if "adp" in mode and eng in ("act", "dve") and not dma:
                tp = any((a is not None) and (not isinstance(a, (int, float))) and str(a.space) == "PSUM"
                         for a in list(reads) + list(writes))
                if tp:
                    for a in list(reads) + list(writes):
                        if (a is not None) and (not isinstance(a, (int, float))) and str(a.space) == "PSUM":
                            r = _region(a)
                            for bk in range(r[3] // 2048, r[4] // 2048):
                                self.last_cls["adp:%s:%d" % (r[0], bk)] = op
            if "psum" in mode:
                tp = any((a is not None) and (not isinstance(a, (int, float))) and str(a.space) == "PSUM"
                         for a in list(reads) + list(writes))
                if tp:
                    self.last_cls["psum"] = op
        return op

    def mm(self, out, lhsT, rhs, start=True, stop=True, **kw):
        return self.add("pe", lambda e: e.matmul(out, lhsT, rhs, start=start, stop=stop, **kw),
                        reads=(lhsT, rhs), writes=(out,))

    def tr(self, out, in_, ident):
        return self.add("pe", lambda e: e.transpose(out, in_, ident), reads=(in_, ident), writes=(out,))

    def act(self, out, in_, func, bias=None, scale=None, accum_out=None, eng="act"):
        kw = {}
        if bias is not None:
            kw["bias"] = bias
        if scale is not None:
            kw["scale"] = scale
        if accum_out is not None:
            kw["accum_out"] = accum_out
        rds = [in_]
        if bias is not None and not isinstance(bias, (int, float)):
            rds.append(bias)
        if scale is not None and not isinstance(scale, (int, float)):
            rds.append(scale)
        return self.add(eng, lambda e: e.activation(out, in_, func, **kw), reads=rds, writes=(out, accum_out))

    def tt(self, eng, out, in0, in1, op):
        return self.add(eng, lambda e: e.tensor_tensor(out, in0, in1, op), reads=(in0, in1), writes=(out,))

    def ts(self, eng, out, in0, s1, s2, op0, op1=None, accum_out=None):
        kw = {}
        if op1 is not None:
            kw["op1"] = op1
        if accum_out is not None:
            kw["accum_out"] = accum_out
        rds = [in0]
        for s in (s1, s2):
            if s is not None and not isinstance(s, (int, float)):
                rds.append(s)
        return self.add(eng, lambda e: e.tensor_scalar(out, in0, s1, s2, op0, **kw), reads=rds,
                        writes=(out, accum_out))

    def stt(self, eng, out, in0, scalar, in1, op0, op1, accum_out=None):
        kw = {}
        if accum_out is not None:
            kw["accum_out"] = accum_out
        rds = [in0, in1]
        if not isinstance(scalar, (int, float)):
            rds.append(scalar)
        return self.add(eng, lambda e: e.scalar_tensor_tensor(out, in0, scalar, in1, op0, op1, **kw), reads=rds,
                        writes=(out, accum_out))

    def copy(self, eng, out, in_):
        if eng == "act":
            return self.add(eng, lambda e: e.activation(out, in_, AF.Copy), reads=(in_,), writes=(out,))
        return self.add(eng, lambda e: e.tensor_copy(out, in_), reads=(in_,), writes=(out,))

    def memset(self, eng, ap, val):
        return self.add(eng, lambda e: e.memset(ap, val), reads=(), writes=(ap,))

    def reduce(self, eng, out, in_, op, axis=AX.X):
        return self.add(eng, lambda e: e.tensor_reduce(out, in_, axis, op), reads=(in_,), writes=(out,))

    def ttr(self, eng, out, in0, in1, op0, op1, scale, scalar, accum_out):
        return self.add(eng, lambda e: e.tensor_tensor_reduce(out, in0, in1, op0, op1, scale, scalar, accum_out),
                        reads=(in0, in1), writes=(out, accum_out))

    def dma(self, eng, out, in_, **kw):
        return self.add(eng, lambda e: e.dma_start(out=out, in_=in_, **kw), reads=(in_,), writes=(out,), dma=True)

    def emit(self):
        nc = self.nc
        from contextlib import ExitStack
        with ExitStack() as es:
            esem = {e: es.enter_context(nc.semaphore("s_" + e)) for e in self.ENGS}
            dsems = {}
            for q, n in self.ring.items():
                for i in range(min(n, max(self.ndma[q], 1))):
                    dsems[(q, i)] = es.enter_context(nc.semaphore("d_%s%d" % (q, i)))
            for e in self.ENGS:
                c = 0
                for op in self.ops[e]:
                    if op.signal:
                        c += 1
                    op.count = c
            block = es.enter_context(nc.Block())
            engobj = {"pe": "tensor", "act": "scalar", "dve": "vector", "pool": "gpsimd", "sp": "sync"}
            last_dma = {}
            for q in self.dma_ops:
                for op in self.dma_ops[q]:
                    last_dma[op.dsem] = op.dtarget

            def make(ename):
                ops = self.ops[ename]

                def body(eng):
                    seen = {}
                    for op in ops:
                        waits = {}
                        for p in op.deps:
                            if p.dma:
                                key = dsems[p.dsem]
                                val = p.dtarget
                            else:
                                key = esem[p.eng]
                                val = p.count
                            k = id(key)
                            if k not in waits or waits[k][1] < val:
                                waits[k] = (key, val)
                        if op.pre is not None:
                            key = dsems[op.pre.dsem]
                            k = id(key)
                            val = op.pre.dtarget
                            if k not in waits or waits[k][1] < val:
                                waits[k] = (key, val)
                        for k, (key, val) in waits.items():
                            if seen.get(k, -1) >= val:
                                continue
                            seen[k] = val
                            eng.wait_ge(key, val)
                        ins = op.emit(eng)
                        if op.dma:
                            ins.then_inc(dsems[op.dsem], 16)
                        elif op.signal:
                            ins.then_inc(esem[ename], 1)
                    if ename == "sp":
                        for dk, tgt in last_dma.items():
                            eng.wait_ge(dsems[dk], tgt)
                        for e2 in self.ENGS:
                            if e2 != "sp" and self.ops[e2]:
                                c = self.ops[e2][-1].count
                                if c > 0:
                                    eng.wait_ge(esem[e2], c)
                return body

            for ename in self.ENGS:
                getattr(block, engobj[ename])(make(ename))

from contextlib import ExitStack

D = 1024
S = 2048
SD = 16
NT = 16
DFF = 2816
EPS = 1e-6
ARENA_WORDS = 52600


def _prod(s):
    r = 1
    for v in s:
        r *= v
    return r


class Arena:
    def __init__(self, t, nwords):
        self.t = t
        self.n = nwords
        self.top = 0
        self.peak = 0

    def alloc(self, shape, dt=F32, top=False):
        sz = mybir.dt.size(dt)
        n = _prod(shape)
        nw = (n * sz + 31) // 32 * 8
        if top:
            self.n -= nw
            off = self.n
        else:
            off = self.top
            self.top += nw
        self.peak = max(self.peak, self.top)
        assert self.top <= self.n, ("arena overflow", self.top, self.n)
        ap = self.t[:, off:off + nw]
        if dt != F32:
            ap = ap.bitcast(dt)
        ap = ap[:, 0:n]
        if len(shape) == 2:
            ap = ap.rearrange("p (a b) -> p a b", b=shape[1])
        elif len(shape) == 3:
            ap = ap.rearrange("p (a b c) -> p a b c", b=shape[1], c=shape[2])
        return ap

    def mark(self):
        return self.top

    def reset(self, m):
        self.top = m


class Ring:
    def __init__(self, arena, n, shape, dt=F32):
        self.bufs = [arena.alloc(shape, dt) for _ in range(n)]
        self.i = 0

    def next(self):
        b = self.bufs[self.i % len(self.bufs)]
        self.i += 1
        return b


class KTile:
    def __init__(self, kT, vaug, nk, parts, zeros, union, bias=None, cbias=False):
        self.kT = kT
        self.vaug = vaug
        self.nk = nk
        self.parts = parts
        self.zeros = zeros
        self.union = union
        self.bias = bias
        self.cbias = cbias


class _Stop(Exception):
    pass


def build_program(stop=None, dbg=None):
    nc = bass.Bass("TRN2", target_bir_lowering=False)

    def din(name, shape):
        return nc.dram_tensor(name, list(shape), F32, kind="ExternalInput").ap()

    def dout(name, shape):
        return nc.dram_tensor(name, list(shape), F32, kind="ExternalOutput").ap()

    xp = din("xp", [S, D]); xs = din("xs", [SD, D]); cT = din("cT", [128, 16])
    latc = din("latc", [2048, 256]); krc = din("krc", [2048, 32])
    bkc = din("bkc", [512, 512]); bvc = din("bvc", [512, 512])
    w_ada = din("w_ada", [D, 6 * D]); b_ada = din("b_ada", [1, 6 * D])
    g_norm_mix = din("g_norm_mix", [1, D]); w_in = din("w_in", [D, 4256])
    g_q_lora = din("g_q_lora", [1, 384]); w_q_up = din("w_q_up", [384, 768])
    g_kv_lora = din("g_kv_lora", [1, 256]); w_kv_up = din("w_kv_up", [256, 1024])
    g_qn_a = din("g_qn_a", [1, 64]); g_kn_a = din("g_kn_a", [1, 64])
    g_qr_a = din("g_qr_a", [1, 32]); g_kr_a = din("g_kr_a", [1, 32])
    g_q_b = din("g_q_b", [1, 64]); g_k_b = din("g_k_b", [1, 64])
    rel_bias = din("rel_bias", [8, 257])
    w_o_a = din("w_o_a", [512, D]); w_o_b = din("w_o_b", [512, D]); w_out = din("w_out", [D, D])
    g_norm_ffn = din("g_norm_ffn", [1, D])
    w_gate = din("w_gate", [D, DFF]); w_up = din("w_up", [D, DFF]); w_down = din("w_down", [DFF, D])
    identd = din("identd", [128, 128]); antid = din("antid", [128, 128])
    cstab = din("cstab", [128, 17, 32]); sntab = din("sntab", [128, 17, 32])

    yp = dout("yp", [S, D]); ys = dout("ys", [SD, D])
    latp = dout("latp", [S, 256]); krp = dout("krp", [S, 32])
    bkp = dout("bkp", [512, 512]); bvp = dout("bvp", [512, 512])
    lats = dout("lats", [SD, 256]); krs = dout("krs", [SD, 32])
    bks = dout("bks", [SD, 512]); bvs = dout("bvs", [SD, 512])
    ext2 = nc.dram_tensor("ext2", [8, 768], F32, kind="Internal").ap()

    def bc_rows(src, nparts, n, off=0):
        return bass.AP(src.tensor, off, [[0, nparts], [1, n]])

    with ExitStack() as es:
        arena_t = es.enter_context(nc.sbuf_tensor("arena", [128, ARENA_WORDS], F32))
        PS = [es.enter_context(nc.psum_tensor("ps%d" % i, [128, 1024], F32)) for i in range(4)]
        P = Prog(nc)
        P.track_dram.add("ext2")
        A = Arena(arena_t, ARENA_WORDS)

        def bank(b):
            return PS[b // 2][:, (b % 2) * 512:(b % 2) * 512 + 512]

        def bankbf(b):
            return PS[b // 2][:, (b % 2) * 512:(b % 2) * 512 + 512].bitcast(BF16)

        def dbl(d):
            return PS[d][:, :]

        def recip(out, in_):
            return P.add("dve", lambda e: e.reciprocal(out, in_), reads=(in_,), writes=(out,))

        ident_b = A.alloc([128], BF16)
        identf = A.alloc([128], F32)
        cs_sb = A.alloc([17, 32], F32)
        sn_sb = A.alloc([17, 32], F32)
        epsT = A.alloc([1], F32)
        modv = A.alloc([6, 8, 2], F32)
        ones_b = A.alloc([128], BF16)
        P.dma("pool", ident_b, identd)
        P.dma("sp", identf, identd)
        P.dma("sp", cs_sb, cstab)
        P.dma("sp", sn_sb, sntab)
        P.memset("pool", epsT, EPS)
        P.memset("pool", ones_b, 1.0)
        persist_mark = A.mark()

        def checkpoint(name):
            if stop == name:
                if dbg:
                    for (dname, fn) in dbg.items():
                        ap = fn(locals_ref)
                        shp = list(ap.shape)
                        dt_ = nc.dram_tensor(dname, shp, ap.dtype, kind="ExternalOutput").ap()
                        P.dma("sp", dt_, ap)
                raise _Stop()

        locals_ref = {}

        def rstd(ssq, inv_n, out, T):
            P.act(out, ssq, AF.Ln, bias=epsT[0:T, 0:1], scale=inv_n)
            P.act(out, out, AF.Exp, scale=-0.5)

        try:
            sT = A.alloc([8, 2], BF16)
            cTf = A.alloc([8, 2], F32)
            m2 = A.alloc([6 * D], F32)
            bad = A.alloc([6 * D], F32)
            gm2 = A.alloc([2, D], F32)
            rows = A.alloc([2, D], F32)
            wring = Ring(A, 3, [8, 1024], BF16)
            P.dma("sp", cTf, cT.rearrange("p (k j) -> p k j", j=2))
            P.act(sT, cTf, AF.Silu)
            P.dma("sp", bad[0:2, :], bc_rows(b_ada, 2, 6 * D))
            P.dma("sp", gm2[0:2, 0, :], bc_rows(g_norm_mix, 2, D))
            P.dma("sp", gm2[0:2, 1, :], bc_rows(g_norm_ffn, 2, D))
            w_ada_v = w_ada.rearrange("(k p) c -> p k c", p=128)
            for cc in range(6):
                wch = wring.next()
                P.dma("pool", wch, w_ada_v[:, :, cc * 1024:(cc + 1) * 1024])
                for half in range(2):
                    pb = bank(cc % 2 * 2 + half)
                    for k in range(8):
                        P.mm(pb[0:2, :], sT[:, k, :], wch[:, k, half * 512:(half + 1) * 512], start=(k == 0), stop=(k == 7))
                    col = cc * 1024 + half * 512
                    P.tt("dve", m2[0:2, col:col + 512], pb[0:2, :], bad[0:2, col:col + 512], ALU.add)
            P.stt("dve", rows[0:2, 0, :], m2[0:2, 1024:2048], 1.0, gm2[0:2, 0, :], ALU.add, ALU.mult)
            P.stt("dve", rows[0:2, 1, :], m2[0:2, 4096:5120], 1.0, gm2[0:2, 1, :], ALU.add, ALU.mult)
            vecs = [m2[0:2, 0:1024], rows[0:2, 0, :], m2[0:2, 3072:4096], rows[0:2, 1, :], m2[0:2, 2048:3072],
                    m2[0:2, 5120:6144]]
            pb = bank(4)
            vhi = A.alloc([D], BF16)
            vlo = A.alloc([D], BF16)
            for v in range(6):
                P.copy("dve", vhi[0:2, :], vecs[v])
                P.tt("dve", vlo[0:2, :], vecs[v], vhi[0:2, :], ALU.subtract)
                for k in range(8):
                    c = (v * 8 + k) * 2
                    P.mm(pb[:, c:c + 2], vhi[0:2, k * 128:(k + 1) * 128], ident_b[0:2, 0:2], start=True, stop=False)
                    P.mm(pb[:, c:c + 2], vlo[0:2, k * 128:(k + 1) * 128], ident_b[0:2, 0:2], start=False, stop=True)
            P.copy("dve", modv, pb[:, 0:96].rearrange("p (v k j) -> p v k j", k=8, j=2))
            A.reset(persist_mark)
            locals_ref.update(modv=modv, m2=m2)
            checkpoint("p0")

            def build_gbc(vi, j, dst, dring):
                dhring = Ring(A, 4, [128], BF16)
                for half in range(2):
                    pb = bank(2 + half)
                    for c in range(4):
                        k = half * 4 + c
                        dg = dring.next()
                        dh = dhring.next()
                        dl = dhring.next()
                        P.ts("dve", dg, identf, modv[:, vi, k, j:j + 1], None, ALU.mult)
                        P.copy("dve", dh, dg)
                        P.tt("dve", dl, dg, dh, ALU.subtract)
                        P.mm(pb[:, c * 128:(c + 1) * 128], ones_b, dh, start=True, stop=False)
                        P.mm(pb[:, c * 128:(c + 1) * 128], ones_b, dl, start=False, stop=True)
                    P.copy("act", dst[:, half * 512:(half + 1) * 512], pb)

            def make_hT(xsrc, T, j, vA, vS, dst, junkb, xn, st):
                checkpoint("m0")
                P.act(junkb[0:T, :], xsrc, AF.Square, accum_out=st[0:T, 0:1])
                checkpoint("m1")
                rstd(st[0:T, 0:1], 1.0 / D, st[0:T, 1:2], T)
                checkpoint("m2")
                P.act(xn[0:T, :], xsrc, AF.Copy, scale=st[0:T, 1:2])
                checkpoint("m3")
                pb = bankbf(0)
                for k in range(8):
                    P.tr(pb[:, k * T:(k + 1) * T], xn[0:T, k * 128:(k + 1) * 128], ident_b[0:T, 0:T])
                checkpoint("m4")
                for k in range(8):
                    if k % 2 == 0:
                        P.ts("dve", dst[:, k, 0:T], pb[:, k * T:(k + 1) * T], modv[:, vA, k, j:j + 1],
                             modv[:, vS, k, j:j + 1], ALU.mult, ALU.add)
                checkpoint("m5")
                for k in range(8):
                    if k % 2 == 1:
                        P.act(dst[:, k, 0:T], pb[:, k * T:(k + 1) * T], AF.Identity, bias=modv[:, vS, k, j:j + 1],
                              scale=modv[:, vA, k, j:j + 1])
                checkpoint("m6")

            def rope(x, H, ti, T, out, ra, rb):
                csb = cs_sb[0:T, ti, :].unsqueeze(1).to_broadcast([T, H, 32])
                sn1 = sn_sb[0:T, ti, 0:16].unsqueeze(1).to_broadcast([T, H, 16])
                sn2 = sn_sb[0:T, ti, 16:32].unsqueeze(1).to_broadcast([T, H, 16])
                P.tt("pool", ra[0:T, 0:H, :], x, csb, ALU.mult)
                P.tt("pool", rb[0:T, 0:H, 0:16], x[:, :, 16:32], sn1, ALU.mult)
                P.tt("pool", rb[0:T, 0:H, 16:32], x[:, :, 0:16], sn2, ALU.mult)
                P.tt("pool", out, ra[0:T, 0:H, :], rb[0:T, 0:H, :], ALU.add)

            def attend_head(h, qT, ktiles, N, dst, sbanks, obank, ptring, bring, rring, cb):
                O = obank
                nkt = len(ktiles)

                def issue_S(i):
                    kt = ktiles[i]
                    Sb = sbanks.next()
                    c0, c1 = kt.union
                    P.mm(Sb[0:kt.nk, c0:c1], kt.kT(h), qT[:, c0:c1])
                    return Sb

                DEPTH = 2
                Sq = [issue_S(i) for i in range(min(DEPTH, nkt))]
                for i in range(nkt):
                    Sb = Sq.pop(0)
                    if i + DEPTH < nkt:
                        Sq.append(issue_S(i + DEPTH))
                    kt = ktiles[i]
                    PT = ptring.next()
                    for (p0, p1, (a, b)) in kt.parts:
                        if a >= b:
                            continue
                        src = Sb[p0:p1, a:b]
                        if kt.bias is not None:
                            tb = bring.next()
                            P.tt("dve", tb[p0:p1, a:b], src, kt.bias(h)[p0:p1, a:b], ALU.add)
                            P.act(PT[p0:p1, a:b], tb[p0:p1, a:b], AF.Exp)
                        elif kt.cbias:
                            P.act(PT[p0:p1, a:b], src, AF.Exp, bias=cb[p0:p1, h:h + 1])
                        else:
                            P.act(PT[p0:p1, a:b], src, AF.Exp)
                    for (p0, p1, (a, b)) in kt.zeros:
                        P.memset("pool", PT[p0:p1, a:b], 0.0)
                    c0, c1 = kt.union
                    P.mm(O[:, c0:c1], kt.vaug(h), PT[0:kt.nk, c0:c1], start=(i == 0), stop=(i == nkt - 1))
                rb = rring.next()
                if h % 2 == 0:
                    recip(rb[0:64, 0:N], O[64:128, 0:N])
                    P.tt("dve", dst, O[0:64, 0:N], rb[0:64, 0:N], ALU.mult)
                else:
                    recip(rb[64:128, 0:N], O[0:64, 0:N])
                    P.tt("dve", dst, O[64:128, 0:N], rb[64:128, 0:N], ALU.mult)

            class BankRing:
                def __init__(self, ids):
                    self.ids = ids
                    self.i = 0

                def next(self):
                    b = bank(self.ids[self.i % len(self.ids)])
                    self.i += 1
                    return b

            w_in_v = w_in.rearrange("(k p) c -> p k c", p=128)
            oTa = A.alloc([4, S + SD], BF16, top=True)
            oTb = A.alloc([4, S + SD], BF16, top=True)
            p1_mark = A.mark()

            def xsrc_of(ti):
                return (xs, SD, 1) if ti == 16 else (xp[ti * 128:(ti + 1) * 128, :], 128, 0)

            KT = A.alloc([8, 2048], BF16)
            KTn = A.alloc([8, SD], BF16)
            Vst = A.alloc([16, 768], BF16)
            Vn = A.alloc([768], BF16)
            w1 = A.alloc([8, 672], BF16)
            wq = A.alloc([3, 768], BF16)
            wkv = A.alloc([2, 1024], BF16)
            gql = A.alloc([384]); gkv = A.alloc([256]); gqn_s = A.alloc([64]); gqr_s = A.alloc([32])
            gkn = A.alloc([64]); gkr = A.alloc([32])
            P.dma("pool", w1, w_in_v[:, :, 0:672])
            P.dma("pool", wq, w_q_up.rearrange("(k p) c -> p k c", p=128))
            P.dma("pool", wkv, w_kv_up.rearrange("(k p) c -> p k c", p=128))
            checkpoint("a00")
            for (dst, src, n) in [(gql, g_q_lora, 384), (gkv, g_kv_lora, 256), (gqn_s, g_qn_a, 64), (gqr_s, g_qr_a, 32),
                                  (gkn, g_kn_a, 64), (gkr, g_kr_a, 32)]:
                P.dma("sp", dst, bc_rows(src, 128, n))
            checkpoint("a01")
            P.ts("dve", gqn_s, gqn_s, 96.0 ** -0.5, None, ALU.mult)
            P.ts("dve", gqr_s, gqr_s, 96.0 ** -0.5, None, ALU.mult)
            checkpoint("a02")
            P.memset("dve", Vst, 1.0)
            P.memset("dve", Vn, 1.0)
            xring = Ring(A, 2, [D], F32)
            junkb = A.alloc([D], BF16)
            xnr = Ring(A, 2, [D], BF16)
            hTr = Ring(A, 2, [8, 128], BF16)
            sring = Ring(A, 4, [32], F32)
            sqf = A.alloc([D], F32)
            tqf = A.alloc([D], F32)
            cqnr = Ring(A, 2, [384], BF16)
            cqnTr = Ring(A, 2, [3, 128], BF16)
            qanr = Ring(A, 2, [512], BF16)
            qrrr = Ring(A, 2, [256], BF16)
            kanr = Ring(A, 4, [512], BF16)
            krbr = Ring(A, 4, [32], BF16)
            lat32r = Ring(A, 4, [256], F32)
            lat16r = Ring(A, 4, [256], BF16)
            latTr = Ring(A, 4, [2, 128], BF16)
            krnr = Ring(A, 4, [32], F32)
            kr32r = Ring(A, 2, [32], F32)
            kr16r = Ring(A, 2, [32], BF16)
            rar = Ring(A, 2, [8, 32], F32)
            rbr = Ring(A, 2, [8, 32], F32)
            QT = A.alloc([8, 512], BF16)
            ptring = Ring(A, 4, [512], BF16)
            rring = Ring(A, 2, [512], F32)

            def kv_from_lat(lat16, T, ka, st, Vdst):
                pb = bankbf(0)
                for k in range(2):
                    P.tr(pb[:, k * T:(k + 1) * T], lat16[0:T, k * 128:(k + 1) * 128], ident_b[0:T, 0:T])
                latT = latTr.next()
                P.copy("dve", latT[:, :, 0:T], pb[:, 0:2 * T].rearrange("p (k t) -> p k t", t=T))
                checkpoint("k2")
                kz = dbl(3)
                for half in range(2):
                    for k in range(2):
                        P.mm(kz[0:T, half * 512:(half + 1) * 512], latT[:, k, 0:T], wkv[:, k, half * 512:(half + 1) * 512],
                             start=(k == 0), stop=(k == 1))
                kv = kz[0:T, :].rearrange("p (h d) -> p h d", d=128)
                sqv = sqf[0:T, 0:512].rearrange("p (h d) -> p h d", d=64)
                checkpoint("k3")
                P.act(sqv, kv[:, :, 0:64], AF.Square)
                checkpoint("k4")
                P.reduce("dve", st[0:T, 22:30], sqv, ALU.add)
                checkpoint("k5")
                rstd(st[0:T, 22:30], 1.0 / 64, st[0:T, 22:30], T)
                checkpoint("k6")
                tk = tqf[0:T, 0:512].rearrange("p (h d) -> p h d", d=64)
                P.tt("dve", tk, kv[:, :, 0:64], st[0:T, 22:30].unsqueeze(2).to_broadcast([T, 8, 64]), ALU.mult)
                P.tt("pool", ka[0:T, :].rearrange("p (h d) -> p h d", d=64), tk,
                     gkn[0:T, :].unsqueeze(1).to_broadcast([T, 8, 64]), ALU.mult)
                checkpoint("k7")
                vsrc = kv[:, :, 64:128].rearrange("p (j e) d -> p j e d", e=2)
                vd = Vdst.rearrange("p (j e d) -> p j e d", e=3, d=64)
                P.copy("act", vd[:, :, 0, :], vsrc[:, :, 0, :])
                P.copy("act", vd[:, :, 2, :], vsrc[:, :, 1, :])
                checkpoint("k8")

            def k_transposes(kan, krb, T, KTdst):
                pb = bankbf(1)
                for pr in range(4):
                    P.tr(pb[:, pr * T:(pr + 1) * T], kan[0:T, pr * 128:(pr + 1) * 128], ident_b[0:T, 0:T])
                P.tr(pb[0:32, 4 * T:5 * T], krb[0:T, :], ident_b[0:T, 0:T])
                pv = pb[:, 0:4 * T].rearrange("p (j t) -> p j t", t=T)
                kd = KTdst.rearrange("p (j e) t -> p j e t", e=2)
                P.copy("act", kd[0:64, :, 0, :], pv[0:64, :, :])
                P.copy("dve", kd[0:64, :, 1, :], pv[64:128, :, :])
                P.copy("dve", KTdst[64:96, :, :], pb[0:32, 4 * T:5 * T].unsqueeze(1).to_broadcast([32, 8, T]))

            def p1a_tile(ti, QTdst, KTdst, Vdst, lat_out, kr_out):
                xsrc, T, j = xsrc_of(ti)
                xt = xring.next()
                P.dma("sp", xt[0:T, :], xsrc)
                hT = hTr.next()
                st = sring.next()
                make_hT(xt[0:T, :], T, j, 1, 0, hT, junkb, xnr.next(), st)
                if ti == 16:
                    checkpoint("t1")
                z = dbl(1)
                for k in range(8):
                    P.mm(z[0:T, 0:384], hT[:, k, 0:T], w1[:, k, 0:384], start=(k == 0), stop=(k == 7))
                for k in range(8):
                    P.mm(z[0:T, 512:800], hT[:, k, 0:T], w1[:, k, 384:672], start=(k == 0), stop=(k == 7))
                if ti == 16:
                    checkpoint("t2")
                P.act(junkb[0:T, 0:384], z[0:T, 0:384], AF.Square, accum_out=st[0:T, 2:3])
                rstd(st[0:T, 2:3], 1.0 / 384, st[0:T, 3:4], T)
                cqn = cqnr.next()
                P.stt("dve", cqn[0:T, :], z[0:T, 0:384], st[0:T, 3:4], gql[0:T, :], ALU.mult, ALU.mult)
                pb = bankbf(0)
                for k in range(3):
                    P.tr(pb[:, k * T:(k + 1) * T], cqn[0:T, k * 128:(k + 1) * 128], ident_b[0:T, 0:T])
                cqnT = cqnTr.next()
                P.copy("dve", cqnT[:, :, 0:T], pb[:, 0:3 * T].rearrange("p (k t) -> p k t", t=T))
                if ti == 16:
                    checkpoint("t3")
                qz = dbl(2)
                for (a, b) in ((0, 512), (512, 768)):
                    for k in range(3):
                        P.mm(qz[0:T, a:b], cqnT[:, k, 0:T], wq[:, k, a:b], start=(k == 0), stop=(k == 2))
                qv = qz[0:T, 0:768].rearrange("p (h d) -> p h d", d=96)
                sqv = sqf[0:T, 0:768].rearrange("p (h d) -> p h d", d=96)
                P.act(sqv, qv, AF.Square)
                P.reduce("dve", st[0:T, 4:12], sqv[:, :, 0:64], ALU.add)
                P.reduce("dve", st[0:T, 12:20], sqv[:, :, 64:96], ALU.add)
                rstd(st[0:T, 4:12], 1.0 / 64, st[0:T, 4:12], T)
                rstd(st[0:T, 12:20], 1.0 / 32, st[0:T, 12:20], T)
                if ti == 16:
                    checkpoint("t4")
                tq = tqf[0:T, 0:768].rearrange("p (h d) -> p h d", d=96)
                P.tt("dve", tq[:, :, 0:64], qv[:, :, 0:64], st[0:T, 4:12].unsqueeze(2).to_broadcast([T, 8, 64]), ALU.mult)
                P.tt("dve", tq[:, :, 64:96], qv[:, :, 64:96], st[0:T, 12:20].unsqueeze(2).to_broadcast([T, 8, 32]), ALU.mult)
                qan = qanr.next()
                qrr = qrrr.next()
                P.tt("pool", qan[0:T, :].rearrange("p (h d) -> p h d", d=64), tq[:, :, 0:64],
                     gqn_s[0:T, :].unsqueeze(1).to_broadcast([T, 8, 64]), ALU.mult)
                P.tt("pool", tq[:, :, 64:96], tq[:, :, 64:96], gqr_s[0:T, :].unsqueeze(1).to_broadcast([T, 8, 32]), ALU.mult)
                rope(tq[:, :, 64:96], 8, ti, T, qrr[0:T, :].rearrange("p (h d) -> p h d", d=32), rar.next(), rbr.next())
                pb = bankbf(1)
                for pr in range(4):
                    P.tr(pb[:, pr * T:(pr + 1) * T], qan[0:T, pr * 128:(pr + 1) * 128], ident_b[0:T, 0:T])
                for pr in range(2):
                    P.tr(pb[:, (4 + pr) * T:(5 + pr) * T], qrr[0:T, pr * 128:(pr + 1) * 128], ident_b[0:T, 0:T])
                pv = pb[:, 0:4 * T].rearrange("p (j t) -> p j t", t=T)
                qd = QTdst.rearrange("p (j e) t -> p j e t", e=2)
                P.copy("act", qd[0:64, :, 0, :], pv[0:64, :, :])
                P.copy("dve", qd[0:64, :, 1, :], pv[64:128, :, :])
                for hh in range(8):
                    src = pb[(hh % 4) * 32:(hh % 4) * 32 + 32, (4 + hh // 4) * T:(5 + hh // 4) * T]
                    P.copy("act" if hh % 2 == 0 else "dve", QTdst[64:96, hh, :], src)
                P.act(junkb[0:T, 0:256], z[0:T, 512:768], AF.Square, accum_out=st[0:T, 20:21])
                rstd(st[0:T, 20:21], 1.0 / 256, st[0:T, 21:22], T)
                lat32 = lat32r.next()
                P.stt("dve", lat32[0:T, :], z[0:T, 512:768], st[0:T, 21:22], gkv[0:T, :], ALU.mult, ALU.mult)
                P.dma("sp", lat_out, lat32[0:T, :])
                lat16 = lat16r.next()
                P.copy("pool", lat16[0:T, :], lat32[0:T, :])
                if ti == 16:
                    checkpoint("t8")
                kan = kanr.next()
                krb = krbr.next()
                P.act(junkb[0:T, 0:32], z[0:T, 768:800], AF.Square, accum_out=st[0:T, 30:31])
                rstd(st[0:T, 30:31], 1.0 / 32, st[0:T, 31:32], T)
                krn = krnr.next()
                P.stt("dve", krn[0:T, :], z[0:T, 768:800], st[0:T, 31:32], gkr[0:T, :], ALU.mult, ALU.mult)
                kr32 = kr32r.next()
                rope(krn[0:T, :].unsqueeze(1), 1, ti, T, kr32[0:T, :].unsqueeze(1), rar.next(), rbr.next())
                P.dma("sp", kr_out, kr32[0:T, :])
                P.copy("pool", krb[0:T, :], kr32[0:T, :])
                if ti == 16:
                    checkpoint("t9")
                kv_from_lat(lat16, T, kan, st, Vdst)
                k_transposes(kan, krb, T, KTdst)

            def full_tile(kTf, vf, nk, N, bias=None, cbias=False):
                return KTile(kTf, vf, nk, [(0, nk, (0, N))], [], (0, N), bias=bias, cbias=cbias)

            def vaug_of(vtile):
                def f(h):
                    pr = h // 2
                    o = pr * 192 + (0 if h % 2 == 0 else 64)
                    return vtile[:, o:o + 128]
                return f

            sb_a = BankRing([2, 3, 4])
            checkpoint("a0")
            for jt in range(16):
                lat16 = lat16r.next()
                lat32c = lat32r.next()
                P.dma("sp", lat32c, latc[jt * 128:(jt + 1) * 128, :])
                P.copy("dve", lat16, lat32c)
                kr32c = krnr.next()
                P.dma("sp", kr32c, krc[jt * 128:(jt + 1) * 128, :])
                kan = kanr.next()
                krb = krbr.next()
                checkpoint("k0")
                P.copy("pool", krb, kr32c)
                checkpoint("k1")
                st = sring.next()
                kv_from_lat(lat16, 128, kan, st, Vst[:, jt, :])
                k_transposes(kan, krb, 128, KT[0:96, :, jt * 128:(jt + 1) * 128])
                if jt == 0:
                    locals_ref.update(KT=KT, Vst=Vst)
                    checkpoint("a1")
                checkpoint("c%d" % jt)
            QTs = A.alloc([8, SD], BF16)
            checkpoint("a2")
            p1a_tile(16, QTs[0:96, :, :], KTn[0:96, :, :], Vn[0:SD, :], lats, krs)
            checkpoint("a3")
            for h in range(8):
                kts = []
                for jt in range(16):
                    kts.append(full_tile((lambda hh, jt=jt: KT[0:96, hh, jt * 128:(jt + 1) * 128]),
                                         (lambda hh, jt=jt: vaug_of(Vst[:, jt, :])(hh)), 128, SD))
                kts.append(full_tile((lambda hh: KTn[0:96, hh, :]), (lambda hh: vaug_of(Vn[0:SD, :])(hh)), SD, SD))
                dst = oTa[(h % 2) * 64:(h % 2) * 64 + 64, h // 2, S:S + SD]
                attend_head(h, QTs[0:96, h, :], kts, SD, dst, sb_a, bank(5 + h % 2), ptring, None, rring, None)
            locals_ref.update(oTa=oTa, KT=KT, Vst=Vst, QTs=QTs, KTn=KTn, Vn=Vn)
            checkpoint("p1as")
            for b in range(4):
                for tt_ in range(4):
                    ti = 4 * b + tt_
                    p1a_tile(ti, QT[0:96, :, tt_ * 128:(tt_ + 1) * 128], KT[0:96, :, ti * 128:(ti + 1) * 128],
                             Vst[:, ti, :], latp[ti * 128:(ti + 1) * 128, :], krp[ti * 128:(ti + 1) * 128, :])
                for h in range(8):
                    kts = []
                    for jt in range(4 * b + 4):
                        kf = (lambda hh, jt=jt: KT[0:96, hh, jt * 128:(jt + 1) * 128])
                        vf = (lambda hh, jt=jt: vaug_of(Vst[:, jt, :])(hh))
                        if jt < 4 * b:
                            kts.append(full_tile(kf, vf, 128, 512))
                        else:
                            c0 = 128 * (jt - 4 * b)
                            kts.append(KTile(kf, vf, 128, [(0, 64, (c0, 512)), (64, 128, (c0 + 64, 512))],
                                             [(64, 128, (c0, c0 + 64))], (c0, 512)))
                    dst = oTa[(h % 2) * 64:(h % 2) * 64 + 64, h // 2, b * 512:(b + 1) * 512]
                    attend_head(h, QT[0:96, h, :], kts, 512, dst, sb_a, bank(5 + h % 2), ptring, None, rring, None)
            A.reset(p1_mark)

            checkpoint("p1a")
            KBT = A.alloc([4, 1024], BF16)
            VBst = A.alloc([8, 768], BF16)
            wb = A.alloc([8, 1536], BF16)
            gqb_s = A.alloc([64]); gkb = A.alloc([64])
            cb = A.alloc([8], F32)
            biasT = [A.alloc([8, 256], BF16) for _ in range(3)]
            P.dma("pool", wb, w_in_v[:, :, 672:2208])
            P.dma("sp", gqb_s, bc_rows(g_q_b, 128, 64))
            P.dma("sp", gkb, bc_rows(g_k_b, 128, 64))
            P.ts("dve", gqb_s, gqb_s, 0.125, None, ALU.mult)
            P.dma("sp", cb.unsqueeze(2), bass.AP(rel_bias.tensor, 256, [[0, 128], [257, 8], [1, 1]]),
                  allow_slow_non_contiguous=True)
            P.memset("dve", VBst, 1.0)
            bm = A.mark()
            e8 = A.alloc([768], F32)
            antif = A.alloc([128], BF16)
            Hb = A.alloc([2048], BF16)
            Hst = A.alloc([2048], F32)
            P.dma("pool", antif, antid)
            P.memset("pool", e8[0:8, :], 0.0)
            P.dma("sp", e8[0:8, 128:385], rel_bias)
            P.copy("dve", e8[0:8, 385:768], e8[0:8, 384:385].to_broadcast([8, 383]))
            P.dma("sp", ext2, e8[0:8, :])
            for r, Dd in enumerate((128, 0, -128)):
                P.dma("sp", Hst.rearrange("p (h q) -> p h q", q=256),
                      bass.AP(ext2.tensor, 128 + Dd + 1, [[1, 128], [768, 8], [1, 256]]))
                for c in range(4):
                    pb = bank(c % 2)
                    if c == 0:
                        P.copy("dve", Hb, Hst)
                    P.mm(pb, antif, Hb[:, c * 512:(c + 1) * 512])
                    P.copy("dve", biasT[r][:, 2 * c:2 * c + 2, :], pb.rearrange("p (h q) -> p h q", q=256))
            A.reset(bm)
            xring = Ring(A, 2, [D], F32)
            junkb = A.alloc([D], BF16)
            xnr = Ring(A, 2, [D], BF16)
            hTr = Ring(A, 2, [8, 128], BF16)
            sring = Ring(A, 4, [32], F32)
            sqf = A.alloc([512], F32)
            tqf = A.alloc([512], F32)
            qb16r = Ring(A, 2, [512], BF16)
            kb32r = Ring(A, 2, [512], F32)
            kb16r = Ring(A, 2, [512], BF16)
            vb32r = Ring(A, 2, [512], F32)
            QBT = A.alloc([4, 256], BF16)
            QBTs = A.alloc([4, SD], BF16)
            ptring = Ring(A, 4, [512], BF16)
            bring = Ring(A, 2, [512], F32)
            rring = Ring(A, 2, [512], F32)

            def headnorm(zsrc, T, st, c0, gain, out, outeng="pool"):
                zv = zsrc.rearrange("p (h d) -> p h d", d=64)
                sqv = sqf[0:T, :].rearrange("p (h d) -> p h d", d=64)
                P.act(sqv, zv, AF.Square)
                P.reduce("dve", st[0:T, c0:c0 + 8], sqv, ALU.add)
                rstd(st[0:T, c0:c0 + 8], 1.0 / 64, st[0:T, c0:c0 + 8], T)
                tv = tqf[0:T, :].rearrange("p (h d) -> p h d", d=64)
                P.tt("dve", tv, zv, st[0:T, c0:c0 + 8].unsqueeze(2).to_broadcast([T, 8, 64]), ALU.mult)
                P.tt(outeng, out.rearrange("p (h d) -> p h d", d=64), tv,
                     gain[0:T, :].unsqueeze(1).to_broadcast([T, 8, 64]), ALU.mult)

            def kb_to_store(kb16, T, slot):
                pb = bankbf(1)
                for pr in range(4):
                    P.tr(pb[:, pr * T:(pr + 1) * T], kb16[0:T, pr * 128:(pr + 1) * 128], ident_b[0:T, 0:T])
                P.copy("act", KBT[:, :, slot * 128:slot * 128 + T], pb[:, 0:4 * T].rearrange("p (j t) -> p j t", t=T))

            def vb_to_store(vsrc, T, slot):
                vd = VBst[0:T, slot, :].rearrange("p (j e d) -> p j e d", e=3, d=64)
                vs = vsrc.rearrange("p (j e d) -> p j e d", e=2, d=64)
                P.copy("pool", vd[:, :, 0, :], vs[:, :, 0, :])
                P.copy("pool", vd[:, :, 2, :], vs[:, :, 1, :])

            def p1b_tile(ti, QBdst, slot, bk_out, bv_out):
                xsrc, T, j = xsrc_of(ti)
                xt = xring.next()
                P.dma("sp", xt[0:T, :], xsrc)
                hT = hTr.next()
                st = sring.next()
                make_hT(xt[0:T, :], T, j, 1, 0, hT, junkb, xnr.next(), st)
                zqk = dbl(1)
                zv = dbl(2)
                for (dstp, c0) in ((zqk[0:T, 0:512], 0), (zqk[0:T, 512:1024], 512), (zv[0:T, 0:512], 1024)):
                    for k in range(8):
                        P.mm(dstp, hT[:, k, 0:T], wb[:, k, c0:c0 + 512], start=(k == 0), stop=(k == 7))
                qb16 = qb16r.next()
                headnorm(zqk[0:T, 0:512], T, st, 2, gqb_s, qb16[0:T, :])
                pb = bankbf(1)
                for pr in range(4):
                    P.tr(pb[:, pr * T:(pr + 1) * T], qb16[0:T, pr * 128:(pr + 1) * 128], ident_b[0:T, 0:T])
                P.copy("act", QBdst, pb[:, 0:4 * T].rearrange("p (j t) -> p j t", t=T))
                kb32 = kb32r.next()
                headnorm(zqk[0:T, 512:1024], T, st, 10, gkb, kb32[0:T, :])
                if bk_out is not None:
                    P.dma("sp", bk_out, kb32[0:T, :])
                kb16 = kb16r.next()
                P.copy("pool", kb16[0:T, :], kb32[0:T, :])
                kb_to_store(kb16, T, slot)
                vb32 = vb32r.next()
                P.copy("act", vb32[0:T, :], zv[0:T, 0:512])
                if bv_out is not None:
                    P.dma("sp", bv_out, vb32[0:T, :])
                vb_to_store(vb32[0:T, :], T, slot)

            def band_k(slot, nk):
                return lambda hh: KBT[(hh % 2) * 64:(hh % 2) * 64 + 64, hh // 2, slot * 128:slot * 128 + nk]

            def band_v(slot, nk):
                return lambda hh: vaug_of(VBst[0:nk, slot, :])(hh)

            sb_b = BankRing([4, 5, 6])
            for jt in range(4):
                kb16 = kb16r.next()
                P.dma("pool", kb16, bkc[jt * 128:(jt + 1) * 128, :])
                kb_to_store(kb16, 128, jt)
                vb16 = qb16r.next()
                P.dma("pool", vb16, bvc[jt * 128:(jt + 1) * 128, :])
                vb_to_store(vb16, 128, jt)
            p1b_tile(16, QBTs[:, :, :], 4, bks, bvs)
            for h in range(8):
                kts = []
                for jt in range(3):
                    kts.append(full_tile(band_k(jt, 128), band_v(jt, 128), 128, SD, cbias=True))
                kts.append(full_tile(band_k(3, 128), band_v(3, 128), 128, SD, bias=(lambda hh: biasT[0][:, hh, :])))
                kts.append(full_tile(band_k(4, SD), band_v(4, SD), SD, SD, bias=(lambda hh: biasT[1][:, hh, :])))
                dst = oTb[(h % 2) * 64:(h % 2) * 64 + 64, h // 2, S:S + SD]
                attend_head(h, QBTs[(h % 2) * 64:(h % 2) * 64 + 64, h // 2, :], kts, SD, dst, sb_b, bank(7), ptring, bring,
                            rring, cb)
            geom = {
                0: ([(0, 64, (0, 64)), (64, 128, (0, 128))], [(0, 64, (64, 128))], (0, 128)),
                1: ([(0, 64, (0, 192)), (64, 128, (0, 256))], [(0, 64, (192, 256))], (0, 256)),
                2: ([(0, 128, (0, 256))], [], (0, 256)),
                3: ([(0, 128, (0, 256))], [], (0, 256)),
                4: ([(0, 64, (0, 256)), (64, 128, (64, 256))], [(64, 128, (0, 64))], (0, 256)),
                5: ([(0, 64, (128, 256)), (64, 128, (192, 256))], [(64, 128, (128, 192))], (128, 256)),
            }
            for i in range(8):
                for tt_ in range(2):
                    ti = 2 * i + tt_
                    last = ti >= 12
                    p1b_tile(ti, QBT[:, :, tt_ * 128:(tt_ + 1) * 128], ti % 8,
                             bkp[(ti - 12) * 128:(ti - 11) * 128, :] if last else None,
                             bvp[(ti - 12) * 128:(ti - 11) * 128, :] if last else None)
                Q0 = 256 * i
                for h in range(8):
                    kts = []
                    for t in (2, 3, 1, 0, 4, 5):
                        K0 = Q0 - 512 + 128 * t
                        if K0 < 0:
                            continue
                        slot = (K0 // 128) % 8
                        parts, zeros, union = geom[t]
                        bias = None
                        if t >= 3:
                            bias = (lambda hh, r=t - 3: biasT[r][:, hh, :])
                        kts.append(KTile(band_k(slot, 128), band_v(slot, 128), 128, parts, zeros, union, bias=bias,
                                         cbias=(t < 3)))
                    dst = oTb[(h % 2) * 64:(h % 2) * 64 + 64, h // 2, Q0:Q0 + 256]
                    attend_head(h, QBT[(h % 2) * 64:(h % 2) * 64 + 64, h // 2, :], kts, 256, dst, sb_b, bank(7), ptring,
                                bring, rring, cb)
            A.reset(p1_mark)

            locals_ref.update(oTb=oTb)
            checkpoint("p1b")
            x1 = A.alloc([17, D], F32)
            p2_mark = A.mark()
            wg2 = A.alloc([8, 2048], BF16)
            woa = A.alloc([4, D], BF16)
            wob = A.alloc([4, D], BF16)
            wout = A.alloc([8, D], BF16)
            P.dma("pool", wg2, w_in_v[:, :, 2208:4256])
            P.dma("pool", woa, w_o_a.rearrange("(k p) c -> p k c", p=128))
            P.dma("pool", wob, w_o_b.rearrange("(k p) c -> p k c", p=128))
            P.dma("pool", wout, w_out.rearrange("(k p) c -> p k c", p=128))
            gbc1 = [A.alloc([D], F32) for _ in range(2)]
            tmpo = Ring(A, 1, [D], F32)
            dring = Ring(A, 2, [128], F32)
            build_gbc(4, 0, gbc1[0], dring)
            build_gbc(4, 1, gbc1[1], dring)
            xnr = Ring(A, 1, [D], BF16)
            sring = Ring(A, 4, [32], F32)
            hT2r = Ring(A, 2, [8, 128], BF16)
            Gr = Ring(A, 2, [D], BF16)
            t1r = Ring(A, 1, [D], BF16)
            t2r = Ring(A, 1, [D], BF16)
            mixr = Ring(A, 1, [D], BF16)
            mixTr = Ring(A, 1, [8, 128], BF16)
            for ti in list(range(16)) + [16]:
                xsrc, T, j = xsrc_of(ti)
                col0 = ti * 128
                P.dma("sp", x1[0:T, ti, :], xsrc)
                hT2 = hT2r.next()
                xn = xnr.next()
                make_hT(x1[0:T, ti, :], T, j, 1, 0, hT2, xn, xn, sring.next())
                tparts = []
                for br, (wo_, oT_, pg, py, tr_) in enumerate(((woa, oTa, dbl(1), dbl(2), t1r), (wob, oTb, dbl(3), dbl(1), t2r))):
                    for half in range(2):
                        c0 = br * 1024 + half * 512
                        for k in range(8):
                            P.mm(pg[0:T, half * 512:(half + 1) * 512], hT2[:, k, 0:T], wg2[:, k, c0:c0 + 512],
                                 start=(k == 0), stop=(k == 7))
                    Gs = Gr.next()
                    P.act(Gs[0:T, :], pg[0:T, :], AF.Sigmoid)
                    for half in range(2):
                        for pr in range(4):
                            P.mm(py[0:T, half * 512:(half + 1) * 512], oT_[:, pr, col0:col0 + T],
                                 wo_[:, pr, half * 512:(half + 1) * 512], start=(pr == 0), stop=(pr == 3))
                    tb_ = tr_.next()
                    P.tt("dve", tb_[0:T, :], py[0:T, :], Gs[0:T, :], ALU.mult)
                    tparts.append(tb_)
                mixed = mixr.next()
                P.tt("pool", mixed[0:T, :], tparts[0][0:T, :], tparts[1][0:T, :], ALU.add)
                pb = bankbf(0)
                for k in range(8):
                    P.tr(pb[:, k * T:(k + 1) * T], mixed[0:T, k * 128:(k + 1) * 128], ident_b[0:T, 0:T])
                mixT = mixTr.next()
                P.copy("act", mixT[:, :, 0:T], pb[:, 0:8 * T].rearrange("p (k t) -> p k t", t=T))
                po = dbl(2)
                for half in range(2):
                    for fc in range(8):
                        P.mm(po[0:T, half * 512:(half + 1) * 512], mixT[:, fc, 0:T],
                             wout[:, fc, half * 512:(half + 1) * 512], start=(fc == 0), stop=(fc == 7))
                to = tmpo.next()
                P.tt("dve", to[0:T, :], po[0:T, :], gbc1[j][0:T, :], ALU.mult)
                P.tt("pool", x1[0:T, ti, :], x1[0:T, ti, :], to[0:T, :], ALU.add)
            A.reset(p2_mark)

            locals_ref.update(x1=x1)
            checkpoint("p2")
            A.n = ARENA_WORDS
            h2T = A.alloc([8, S + SD], BF16)
            gbc2 = [A.alloc([D], F32) for _ in range(2)]
            dring = Ring(A, 2, [128], F32)
            build_gbc(5, 0, gbc2[0], dring)
            build_gbc(5, 1, gbc2[1], dring)
            xnr = Ring(A, 1, [D], BF16)
            sring = Ring(A, 4, [32], F32)
            for ti in range(17):
                T = 128 if ti < 16 else SD
                j = 0 if ti < 16 else 1
                xn_ = xnr.next()
                make_hT(x1[0:T, ti, :], T, j, 3, 2, h2T[:, :, ti * 128:ti * 128 + T], xn_, xn_, sring.next())
            GS = 4
            groups = []
            c = 0
            while c < 22:
                g = min(GS, 22 - c)
                groups.append((c, g))
                c += g
            wgr = Ring(A, 2, [8, GS * 128], BF16)
            wur = Ring(A, 2, [8, GS * 128], BF16)
            wdr = Ring(A, 2, [GS, D], BF16)
            aTr = Ring(A, 2, [GS, 512], BF16)
            sgr = Ring(A, 2, [512], F32)
            tmpo = Ring(A, 1, [D], F32)
            w_gate_v = w_gate.rearrange("(k p) c -> p k c", p=128)
            w_up_v = w_up.rearrange("(k p) c -> p k c", p=128)
            w_down_v = w_down.rearrange("(g p) c -> p g c", p=128)
            gub = BankRing([0, 1, 2, 3])
            blocks = [(0, 512), (512, 512), (1024, 512), (1536, 512), (2048, SD)]
            for gi, (c0, g) in enumerate(groups):
                wg_ = wgr.next(); wu_ = wur.next(); wd_ = wdr.next()
                P.dma("pool", wg_[:, :, 0:g * 128], w_gate_v[:, :, c0 * 128:(c0 + g) * 128])
                P.dma("pool", wu_[:, :, 0:g * 128], w_up_v[:, :, c0 * 128:(c0 + g) * 128])
                P.dma("pool", wd_[:, 0:g, :], w_down_v[:, c0:c0 + g, :])
                lastg = gi == len(groups) - 1
                for (t0, n) in blocks:
                    j = 0 if t0 < S else 1
                    aT = aTr.next()
                    for gg in range(g):
                        pg = gub.next()
                        for k in range(8):
                            P.mm(pg[:, 0:n], wg_[:, k, gg * 128:(gg + 1) * 128], h2T[:, k, t0:t0 + n], start=(k == 0), stop=(k == 7))
                        pu = gub.next()
                        for k in range(8):
                            P.mm(pu[:, 0:n], wu_[:, k, gg * 128:(gg + 1) * 128], h2T[:, k, t0:t0 + n], start=(k == 0), stop=(k == 7))
                        sg = sgr.next()
                        P.act(sg[:, 0:n], pg[:, 0:n], AF.Silu)
                        P.tt("dve", aT[:, gg, 0:n], pu[:, 0:n], sg[:, 0:n], ALU.mult)
                    ntile = (n + 127) // 128
                    for tl in range(ntile):
                        ti = t0 // 128 + tl
                        T = min(128, n - tl * 128)
                        po = dbl(2 + tl % 2)
                        for half in range(2):
                            for gg in range(g):
                                P.mm(po[0:T, half * 512:(half + 1) * 512], aT[:, gg, tl * 128:tl * 128 + T],
                                     wd_[:, gg, half * 512:(half + 1) * 512], start=(gg == 0), stop=(gg == g - 1))
                        to = tmpo.next()
                        P.tt("dve", to[0:T, :], po[0:T, :], gbc2[j][0:T, :], ALU.mult)
                        P.tt("pool", x1[0:T, ti, :], x1[0:T, ti, :], to[0:T, :], ALU.add)
                        if lastg:
                            if ti < 16:
                                P.dma("sp", yp[ti * 128:(ti + 1) * 128, :], x1[0:T, ti, :])
                            else:
                                P.dma("sp", ys, x1[0:T, ti, :])
        except _Stop:
            pass
        P.emit()
        print("arena peak words", A.peak, "of", ARENA_WORDS, "n_ops", len(P.all), {e: len(P.ops[e]) for e in P.ENGS})
    return nc


_NC_CACHE = {}


def _consts():
    ident = np.eye(128, dtype=np.float32)
    anti = np.ascontiguousarray(ident[::-1])
    half = 16
    inv_freq = (10000.0 ** (-np.arange(half, dtype=np.float32) / half)).astype(np.float32)
    cs = np.zeros((128, 17, 32), np.float32)
    sn = np.zeros((128, 17, 32), np.float32)
    for ti in range(17):
        pos = (np.arange(128) + ti * 128).astype(np.float32)
        ang = (pos[:, None] * inv_freq[None, :]).astype(np.float32)
        c = np.cos(ang).astype(np.float32)
        s = np.sin(ang).astype(np.float32)
        cs[:, ti, 0:16] = c
        cs[:, ti, 16:32] = c
        sn[:, ti, 0:16] = -s
        sn[:, ti, 16:32] = s
    return ident, anti, cs, sn


def kernel(x_prompt, x_sample, c_prompt, c_sample, cache_kv_latent, cache_k_rope, cache_band_k, cache_band_v,
           w_ada, b_ada, g_norm_mix, w_in, g_q_lora, w_q_up, g_kv_lora, w_kv_up, g_qn_a, g_kn_a, g_qr_a, g_kr_a,
           g_q_b, g_k_b, rel_bias, w_o_a, w_o_b, w_out, g_norm_ffn, w_gate, w_up, w_down):
    f = lambda a: np.ascontiguousarray(np.asarray(a, dtype=np.float32))
    if "nc" not in _NC_CACHE:
        _NC_CACHE["nc"] = build_program()
    nc = _NC_CACHE["nc"]
    ident, anti, cs, sn = _consts()
    shared = {
        "w_ada": f(w_ada[0]), "b_ada": f(b_ada[0]).reshape(1, -1), "g_norm_mix": f(g_norm_mix[0]).reshape(1, -1),
        "w_in": f(w_in[0]), "g_q_lora": f(g_q_lora[0]).reshape(1, -1), "w_q_up": f(w_q_up[0]),
        "g_kv_lora": f(g_kv_lora[0]).reshape(1, -1), "w_kv_up": f(w_kv_up[0]),
        "g_qn_a": f(g_qn_a[0]).reshape(1, -1), "g_kn_a": f(g_kn_a[0]).reshape(1, -1),
        "g_qr_a": f(g_qr_a[0]).reshape(1, -1), "g_kr_a": f(g_kr_a[0]).reshape(1, -1),
        "g_q_b": f(g_q_b[0]).reshape(1, -1), "g_k_b": f(g_k_b[0]).reshape(1, -1),
        "rel_bias": f(rel_bias[0]), "w_o_a": f(w_o_a[0]), "w_o_b": f(w_o_b[0]), "w_out": f(w_out[0]),
        "g_norm_ffn": f(g_norm_ffn[0]).reshape(1, -1), "w_gate": f(w_gate[0]), "w_up": f(w_up[0]),
        "w_down": f(w_down[0]), "identd": ident, "antid": anti, "cstab": cs, "sntab": sn,
    }
    in_maps = []
    for b in range(8):
        cpair = np.stack([np.asarray(c_prompt[b], np.float32), np.asarray(c_sample[b], np.float32)], axis=-1)
        cTl = np.ascontiguousarray(cpair.reshape(8, 128, 2).transpose(1, 0, 2).reshape(128, 16))
        m = dict(shared)
        m.update({
            "xp": f(x_prompt[b]), "xs": f(x_sample[b]), "cT": cTl,
            "latc": f(cache_kv_latent[0, b]), "krc": f(cache_k_rope[0, b]),
            "bkc": f(cache_band_k[0, b]).reshape(512, 512), "bvc": f(cache_band_v[0, b]).reshape(512, 512),
        })
        in_maps.append(m)
    res = run_bass_kernel_spmd(nc, in_maps, core_ids=list(range(8)))
    R = res.results
    st = lambda k: np.stack([np.asarray(R[b][k], np.float32) for b in range(8)], axis=0)
    y_p = st("yp"); y_s = st("ys")
    lat_p = st("latp")[None]; kr_p = st("krp")[None]
    bk_p = st("bkp").reshape(8, 512, 8, 64)[None]; bv_p = st("bvp").reshape(8, 512, 8, 64)[None]
    lat_s = st("lats")[None]; kr_s = st("krs")[None]
    bk_s = st("bks").reshape(8, SD, 8, 64)[None]; bv_s = st("bvs").reshape(8, SD, 8, 64)[None]
    return (y_p, y_s, lat_p, kr_p, bk_p, bv_p, lat_s, kr_s, bk_s, bv_s)
```

```python
import numpy as np
import concourse.bass as bass
import concourse.mybir as mybir
from concourse.bass_utils import run_bass_kernel_spmd

F32 = mybir.dt.float32
BF16 = mybir.dt.bfloat16
ALU = mybir.AluOpType
AF = mybir.ActivationFunctionType
AX = mybir.AxisListType


class _Op:
    __slots__ = ("eng", "emit", "dma", "deps", "signal", "count", "dsem", "dtarget", "pre", "seq")

    def __init__(self, eng, emit, dma):
        self.eng = eng
        self.emit = emit
        self.dma = dma
        self.deps = []
        self.signal = False
        self.count = 0
        self.dsem = None
        self.dtarget = 0
        self.pre = None
        self.seq = 0


def _region(ap):
    t = ap.tensor
    pat = ap.ap
    off = ap.offset
    sp = str(ap.space)
    if sp in ("SB", "PSUM"):
        sz = mybir.dt.size(ap.dtype)
        pstride = pat[0][0]
        npart = pat[0][1]
        if pstride == 0:
            pstride = 1 << 30
        p0 = off // pstride
        f0 = (off % pstride) * sz
        ext = 1
        for st, cnt in pat[1:]:
            ext += abs(st) * (cnt - 1)
        if sp == "PSUM":
            return (sp + ":" + t.name, (p0 // 32) * 32, ((p0 + npart + 31) // 32) * 32,
                    (f0 // 2048) * 2048, ((f0 + ext * sz + 2047) // 2048) * 2048)
        return (sp + ":" + t.name, p0, p0 + npart, f0, f0 + ext * sz)
    lo = off
    hi = off
    for st, cnt in pat:
        if st >= 0:
            hi += st * (cnt - 1)
        else:
            lo += st * (cnt - 1)
    return ("D:" + t.name, 0, 1, lo, hi + 1)


def _ovl(a, b):
    return a[1] < b[2] and b[1] < a[2] and a[3] < b[4] and b[3] < a[4]


def _contains(a, b):
    return a[1] <= b[1] and b[2] <= a[2] and a[3] <= b[3] and b[4] <= a[4]


class Prog:
    ENGS = ("pe", "act", "dve", "pool", "sp")

    def __init__(self, nc, n_dma_sp=28, n_dma_pool=12, n_dma_act=8):
        self.nc = nc
        self.ops = {e: [] for e in self.ENGS}
        self.all = []
        self.wr = {}
        self.rd = {}
        self.track_dram = set()
        self.last_cls = {}
        self.ring = {"sp": n_dma_sp, "pool": n_dma_pool, "act": n_dma_act}
        self.ndma = {"sp": 0, "pool": 0, "act": 0}
        self.dma_ops = {"sp": [], "pool": [], "act": []}

    def _tracked(self, ap):
        sp = str(ap.space)
        if sp in ("SB", "PSUM"):
            return True
        return ap.tensor.name in self.track_dram

    POOL_TO = "dve"
    SERIAL = ("adp",)

    _cap = None

    def begin_capture(self):
        self._cap = []

    def end_capture(self):
        c = self._cap
        self._cap = None
        return c

    def replay(self, lists):
        lists = [l for l in lists if l]
        idx = [0] * len(lists)
        while True:
            best = None
            for i, l in enumerate(lists):
                if idx[i] < len(l):
                    fr = idx[i] / len(l)
                    if best is None or fr < best[0]:
                        best = (fr, i)
            if best is None:
                break
            i = best[1]
            self.add(*lists[i][idx[i]])
            idx[i] += 1

    def add(self, eng, emit, reads=(), writes=(), dma=False):
        if self._cap is not None:
            self._cap.append((eng, emit, tuple(reads), tuple(writes), dma))
            return None
        if eng == "pool" and not dma and self.POOL_TO:
            eng = self.POOL_TO
        op = _Op(eng, emit, dma)
        op.seq = len(self.all)
        deps = set()
        for ap in reads:
            if ap is None or isinstance(ap, (int, float)) or not self._tracked(ap):
                continue
            r = _region(ap)
            for (wr, wop) in self.wr.get(r[0], ()):
                if _ovl(wr, r):
                    deps.add((wop, "raw"))
            lst = self.rd.setdefault(r[0], [])
            if not dma:
                lst[:] = [(rr, rop) for (rr, rop) in lst if not (rop.eng == eng and not rop.dma and _contains(r, rr))]
            lst.append((r, op))
        for ap in writes:
            if ap is None or not self._tracked(ap):
                continue
            w = _region(ap)
            wl = self.wr.setdefault(w[0], [])
            for (wr, wop) in wl:
                if _ovl(wr, w):
                    deps.add((wop, "waw"))
            rl = self.rd.setdefault(w[0], [])
            for (rr, rop) in rl:
                if _ovl(rr, w) and rop is not op:
                    deps.add((rop, "war"))
            wl[:] = [(wr, wop) for (wr, wop) in wl if not _contains(w, wr)]
            wl.append((w, op))
            rl[:] = [(rr, rop) for (rr, rop) in rl if not _contains(w, rr) or rop is op]
        mode = self.SERIAL
        if mode is True and self.all:
            deps.add((self.all[-1], "ser"))
        elif mode:
            touches_psum = any((a is not None) and (not isinstance(a, (int, float))) and str(a.space) == "PSUM"
                               for a in list(reads) + list(writes))
            cls = []
            if "ad" in mode and eng in ("act", "dve") and not dma:
                cls.append("ad")
            if "adp" in mode and eng in ("act", "dve") and not dma and touches_psum:
                for a in list(reads) + list(writes):
                    if (a is not None) and (not isinstance(a, (int, float))) and str(a.space) == "PSUM":
                        r = _region(a)
                        for bk in range(r[3] // 2048, r[4] // 2048):
                            cls.append("adp:%s:%d" % (r[0], bk))
            if "psum" in mode and touches_psum:
                cls.append("psum")
            if "dma" in mode:
                cls.append("dma_any" if dma else "dma_cmp")
            for c in cls:
                if c == "dma_cmp":
                    p = self.last_cls.get("dma")
                    if p is not None:
                        deps.add((p, "ser"))
                    self.last_cls["cmp"] = None
                elif c == "dma_any":
                    p = self.last_cls.get("any")
                    if p is not None:
                        deps.add((p, "ser"))
                else:
                    p = self.last_cls.get(c)
                    if p is not None:
                        deps.add((p, "ser"))
        for (p, kind) in deps:
            if p is op:
                continue
            if kind == "ser":
                if not (p.eng == eng and eng == "pe" and not p.dma):
                    op.deps.append(p)
                    if not p.dma:
                        p.signal = True
                continue
            if not p.dma and p.eng == eng:
                if eng == "pe":
                    continue
            op.deps.append(p)
            if not p.dma:
                p.signal = True
        if dma:
            n = self.ndma[eng]
            ring = self.ring[eng]
            op.dsem = (eng, n % ring)
            op.dtarget = 16 * (n // ring + 1)
            if n >= ring:
                op.pre = self.dma_ops[eng][n - ring]
            self.dma_ops[eng].append(op)
            self.ndma[eng] = n + 1
        self.ops[eng].append(op)
        self.all.append(op)
        mode = self.SERIAL
        if mode and mode is not True:
            self.last_cls["any"] = op
            if dma:
                self.last_cls["dma"] = op
            if "ad" in mode and eng in ("act", "dve") and not dma:
                self.last_cls["ad"] = op
            if "adp" in mode and eng in ("act", "dve") and not dma:
                tp = any((a is not None) and (not isinstance(a, (int, float))) and str(a.space) == "PSUM"
                         for a in list(reads) + list(writes))
                if tp:
                    for a in list(reads) + list(writes):
                        if (a is not None) and (not isinstance(a, (int, float))) and str(a.space) == "PSUM":
                            r = _region(a)
                            for bk in range(r[3] // 2048, r[4] // 2048):
                                self.last_cls["adp:%s:%d" % (r[0], bk)] = op
            if "psum" in mode:
                tp = any((a is not None) and (not isinstance(a, (int, float))) and str(a.space) == "PSUM"
                         for a in list(reads) + list(writes))
                if tp:
                    self.last_cls["psum"] = op
        return op

    def mm(self, out, lhsT, rhs, start=True, stop=True, **kw):
        return self.add("pe", lambda e: e.matmul(out, lhsT, rhs, start=start, stop=stop, **kw),
                        reads=(lhsT, rhs), writes=(out,))

    def tr(self, out, in_, ident):
        return self.add("pe", lambda e: e.transpose(out, in_, ident), reads=(in_, ident), writes=(out,))

    def act(self, out, in_, func, bias=None, scale=None, accum_out=None, eng="act"):
        kw = {}
        if bias is not None:
            kw["bias"] = bias
        if scale is not None:
            kw["scale"] = scale
        if accum_out is not None:
            kw["accum_out"] = accum_out
        rds = [in_]
        if bias is not None and not isinstance(bias, (int, float)):
            rds.append(bias)
        if scale is not None and not isinstance(scale, (int, float)):
            rds.append(scale)
        return self.add(eng, lambda e: e.activation(out, in_, func, **kw), reads=rds, writes=(out, accum_out))

    def tt(self, eng, out, in0, in1, op):
        return self.add(eng, lambda e: e.tensor_tensor(out, in0, in1, op), reads=(in0, in1), writes=(out,))

    def ts(self, eng, out, in0, s1, s2, op0, op1=None, accum_out=None):
        kw = {}
        if op1 is not None:
            kw["op1"] = op1
        if accum_out is not None:
            kw["accum_out"] = accum_out
        rds = [in0]
        for s in (s1, s2):
            if s is not None and not isinstance(s, (int, float)):
                rds.append(s)
        return self.add(eng, lambda e: e.tensor_scalar(out, in0, s1, s2, op0, **kw), reads=rds,
                        writes=(out, accum_out))

    def stt(self, eng, out, in0, scalar, in1, op0, op1, accum_out=None):
        kw = {}
        if accum_out is not None:
            kw["accum_out"] = accum_out
        rds = [in0, in1]
        if not isinstance(scalar, (int, float)):
            rds.append(scalar)
        return self.add(eng, lambda e: e.scalar_tensor_tensor(out, in0, scalar, in1, op0, op1, **kw), reads=rds,
                        writes=(out, accum_out))

    def copy(self, eng, out, in_):
        if eng == "act":
            return self.add(eng, lambda e: e.activation(out, in_, AF.Copy), reads=(in_,), writes=(out,))
        return self.add(eng, lambda e: e.tensor_copy(out, in_), reads=(in_,), writes=(out,))

    def memset(self, eng, ap, val):
        return self.add(eng, lambda e: e.memset(ap, val), reads=(), writes=(ap,))

    def reduce(self, eng, out, in_, op, axis=AX.X):
        return self.add(eng, lambda e: e.tensor_reduce(out, in_, axis, op), reads=(in_,), writes=(out,))

    def ttr(self, eng, out, in0, in1, op0, op1, scale, scalar, accum_out):
        return self.add(eng, lambda e: e.tensor_tensor_reduce(out, in0, in1, op0, op1, scale, scalar, accum_out),
                        reads=(in0, in1), writes=(out, accum_out))

    def dma(self, eng, out, in_, **kw):
        return self.add(eng, lambda e: e.dma_start(out=out, in_=in_, **kw), reads=(in_,), writes=(out,), dma=True)

    def emit(self):
        nc = self.nc
        from contextlib import ExitStack
        with ExitStack() as es:
            esem = {e: es.enter_context(nc.semaphore("s_" + e)) for e in self.ENGS}
            dsems = {}
            for q, n in self.ring.items():
                for i in range(min(n, max(self.ndma[q], 1))):
                    dsems[(q, i)] = es.enter_context(nc.semaphore("d_%s%d" % (q, i)))
            for e in self.ENGS:
                c = 0
                for op in self.ops[e]:
                    if op.signal:
                        c += 1
                    op.count = c
            block = es.enter_context(nc.Block())
            engobj = {"pe": "tensor", "act": "scalar", "dve": "vector", "pool": "gpsimd", "sp": "sync"}
            last_dma = {}
            for q in self.dma_ops:
                for op in self.dma_ops[q]:
                    last_dma[op.dsem] = op.dtarget

            def make(ename):
                ops = self.ops[ename]

                def body(eng):
                    seen = {}
                    for op in ops:
                        waits = {}
                        for p in op.deps:
                            if p.dma:
                                key = dsems[p.dsem]
                                val = p.dtarget
                            else:
                                key = esem[p.eng]
                                val = p.count
                            k = id(key)
                            if k not in waits or waits[k][1] < val:
                                waits[k] = (key, val)
                        if op.pre is not None:
                            key = dsems[op.pre.dsem]
                            k = id(key)
                            val = op.pre.dtarget
                            if k not in waits or waits[k][1] < val:
                                waits[k] = (key, val)
                        for k, (key, val) in waits.items():
                            if seen.get(k, -1) >= val:
                                continue
                            seen[k] = val
                            eng.wait_ge(key, val)
                        ins = op.emit(eng)
                        if op.dma:
                            ins.then_inc(dsems[op.dsem], 16)
                        elif op.signal:
                            ins.then_inc(esem[ename], 1)
                    if ename == "sp":
                        for dk, tgt in last_dma.items():
                            eng.wait_ge(dsems[dk], tgt)
                        for e2 in self.ENGS:
                            if e2 != "sp" and self.ops[e2]:
                                c = self.ops[e2][-1].count
                                if c > 0:
                                    eng.wait_ge(esem[e2], c)
                return body

            for ename in self.ENGS:
                getattr(block, engobj[ename])(make(ename))

from contextlib import ExitStack

D = 1024
S = 2048
SD = 16
NT = 16
DFF = 2816
EPS = 1e-6
ARENA_WORDS = 52600


def _prod(s):
    r = 1
    for v in s:
        r *= v
    return r


class Arena:
    def __init__(self, t, nwords):
        self.t = t
        self.n = nwords
        self.top = 0
        self.peak = 0

    def alloc(self, shape, dt=F32, top=False):
        sz = mybir.dt.size(dt)
        n = _prod(shape)
        nw = (n * sz + 31) // 32 * 8
        if top:
            self.n -= nw
            off = self.n
        else:
            off = self.top
            self.top += nw
        self.peak = max(self.peak, self.top)
        assert self.top <= self.n, ("arena overflow", self.top, self.n)
        ap = self.t[:, off:off + nw]
        if dt != F32:
            ap = ap.bitcast(dt)
        ap = ap[:, 0:n]
        if len(shape) == 2:
            ap = ap.rearrange("p (a b) -> p a b", b=shape[1])
        elif len(shape) == 3:
            ap = ap.rearrange("p (a b c) -> p a b c", b=shape[1], c=shape[2])
        return ap

    def mark(self):
        return self.top

    def reset(self, m):
        self.top = m


class Ring:
    def __init__(self, arena, n, shape, dt=F32):
        self.bufs = [arena.alloc(shape, dt) for _ in range(n)]
        self.i = 0

    def next(self):
        b = self.bufs[self.i % len(self.bufs)]
        self.i += 1
        return b


class KTile:
    def __init__(self, kT, vaug, nk, parts, zeros, union, bias=None, cbias=False):
        self.kT = kT
        self.vaug = vaug
        self.nk = nk
        self.parts = parts
        self.zeros = zeros
        self.union = union
        self.bias = bias
        self.cbias = cbias


class _Stop(Exception):
    pass


def build_program(stop=None, dbg=None):
    nc = bass.Bass("TRN2", target_bir_lowering=False)

    def din(name, shape):
        return nc.dram_tensor(name, list(shape), F32, kind="ExternalInput").ap()

    def dout(name, shape):
        return nc.dram_tensor(name, list(shape), F32, kind="ExternalOutput").ap()

    xp = din("xp", [S, D]); xs = din("xs", [SD, D]); cT = din("cT", [128, 16])
    latc = din("latc", [2048, 256]); krc = din("krc", [2048, 32])
    bkc = din("bkc", [512, 512]); bvc = din("bvc", [512, 512])
    w_ada = din("w_ada", [D, 6 * D]); b_ada = din("b_ada", [1, 6 * D])
    g_norm_mix = din("g_norm_mix", [1, D]); w_in = din("w_in", [D, 4256])
    g_q_lora = din("g_q_lora", [1, 384]); w_q_up = din("w_q_up", [384, 768])
    g_kv_lora = din("g_kv_lora", [1, 256]); w_kv_up = din("w_kv_up", [256, 1024])
    g_qn_a = din("g_qn_a", [1, 64]); g_kn_a = din("g_kn_a", [1, 64])
    g_qr_a = din("g_qr_a", [1, 32]); g_kr_a = din("g_kr_a", [1, 32])
    g_q_b = din("g_q_b", [1, 64]); g_k_b = din("g_k_b", [1, 64])
    rel_bias = din("rel_bias", [8, 257])
    w_o_a = din("w_o_a", [512, D]); w_o_b = din("w_o_b", [512, D]); w_out = din("w_out", [D, D])
    g_norm_ffn = din("g_norm_ffn", [1, D])
    w_gate = din("w_gate", [D, DFF]); w_up = din("w_up", [D, DFF]); w_down = din("w_down", [DFF, D])
    identd = din("identd", [128, 128]); antid = din("antid", [128, 128])
    cstab = din("cstab", [128, 17, 32]); sntab = din("sntab", [128, 17, 32])

    yp = dout("yp", [S, D]); ys = dout("ys", [SD, D])
    latp = dout("latp", [S, 256]); krp = dout("krp", [S, 32])
    bkp = dout("bkp", [512, 512]); bvp = dout("bvp", [512, 512])
    lats = dout("lats", [SD, 256]); krs = dout("krs", [SD, 32])
    bks = dout("bks", [SD, 512]); bvs = dout("bvs", [SD, 512])
    ext2 = nc.dram_tensor("ext2", [8, 768], F32, kind="Internal").ap()

    def bc_rows(src, nparts, n, off=0):
        return bass.AP(src.tensor, off, [[0, nparts], [1, n]])

    with ExitStack() as es:
        arena_t = es.enter_context(nc.sbuf_tensor("arena", [128, ARENA_WORDS], F32))
        PS = [es.enter_context(nc.psum_tensor("ps%d" % i, [128, 1024], F32)) for i in range(4)]
        P = Prog(nc)
        P.track_dram.add("ext2")
        A = Arena(arena_t, ARENA_WORDS)

        def bank(b):
            return PS[b // 2][:, (b % 2) * 512:(b % 2) * 512 + 512]

        def bankbf(b):
            return PS[b // 2][:, (b % 2) * 512:(b % 2) * 512 + 512].bitcast(BF16)

        def dbl(d):
            return PS[d][:, :]

        def recip(out, in_):
            return P.add("dve", lambda e: e.reciprocal(out, in_), reads=(in_,), writes=(out,))

        ident_b = A.alloc([128], BF16)
        identf = A.alloc([128], F32)
        cs_sb = A.alloc([17, 32], F32)
        sn_sb = A.alloc([17, 32], F32)
        epsT = A.alloc([1], F32)
        modv = A.alloc([6, 8, 2], F32)
        ones_b = A.alloc([128], BF16)
        P.dma("pool", ident_b, identd)
        P.dma("sp", identf, identd)
        P.dma("sp", cs_sb, cstab)
        P.dma("sp", sn_sb, sntab)
        P.memset("pool", epsT, EPS)
        P.memset("pool", ones_b, 1.0)
        persist_mark = A.mark()

        def checkpoint(name):
            if stop == name:
                if dbg:
                    for (dname, fn) in dbg.items():
                        ap = fn(locals_ref)
                        shp = list(ap.shape)
                        dt_ = nc.dram_tensor(dname, shp, ap.dtype, kind="ExternalOutput").ap()
                        P.dma("sp", dt_, ap)
                raise _Stop()

        locals_ref = {}

        def rstd(ssq, inv_n, out, T):
            P.act(out, ssq, AF.Ln, bias=epsT[0:T, 0:1], scale=inv_n)
            P.act(out, out, AF.Exp, scale=-0.5)

        try:
            sT = A.alloc([8, 2], BF16)
            cTf = A.alloc([8, 2], F32)
            m2 = A.alloc([6 * D], F32)
            bad = A.alloc([6 * D], F32)
            gm2 = A.alloc([2, D], F32)
            rows = A.alloc([2, D], F32)
            wring = Ring(A, 3, [8, 1024], BF16)
            P.dma("sp", cTf, cT.rearrange("p (k j) -> p k j", j=2))
            P.act(sT, cTf, AF.Silu)
            P.dma("sp", bad[0:2, :], bc_rows(b_ada, 2, 6 * D))
            P.dma("sp", gm2[0:2, 0, :], bc_rows(g_norm_mix, 2, D))
            P.dma("sp", gm2[0:2, 1, :], bc_rows(g_norm_ffn, 2, D))
            w_ada_v = w_ada.rearrange("(k p) c -> p k c", p=128)
            for cc in range(6):
                wch = wring.next()
                P.dma("pool", wch, w_ada_v[:, :, cc * 1024:(cc + 1) * 1024])
                for half in range(2):
                    pb = bank(cc % 2 * 2 + half)
                    for k in range(8):
                        P.mm(pb[0:2, :], sT[:, k, :], wch[:, k, half * 512:(half + 1) * 512], start=(k == 0), stop=(k == 7))
                    col = cc * 1024 + half * 512
                    P.tt("dve", m2[0:2, col:col + 512], pb[0:2, :], bad[0:2, col:col + 512], ALU.add)
            P.stt("dve", rows[0:2, 0, :], m2[0:2, 1024:2048], 1.0, gm2[0:2, 0, :], ALU.add, ALU.mult)
            P.stt("dve", rows[0:2, 1, :], m2[0:2, 4096:5120], 1.0, gm2[0:2, 1, :], ALU.add, ALU.mult)
            vecs = [m2[0:2, 0:1024], rows[0:2, 0, :], m2[0:2, 3072:4096], rows[0:2, 1, :], m2[0:2, 2048:3072],
                    m2[0:2, 5120:6144]]
            pb = bank(4)
            vhi = A.alloc([D], BF16)
            vlo = A.alloc([D], BF16)
            for v in range(6):
                P.copy("dve", vhi[0:2, :], vecs[v])
                P.tt("dve", vlo[0:2, :], vecs[v], vhi[0:2, :], ALU.subtract)
                for k in range(8):
                    c = (v * 8 + k) * 2
                    P.mm(pb[:, c:c + 2], vhi[0:2, k * 128:(k + 1) * 128], ident_b[0:2, 0:2], start=True, stop=False)
                    P.mm(pb[:, c:c + 2], vlo[0:2, k * 128:(k + 1) * 128], ident_b[0:2, 0:2], start=False, stop=True)
            P.copy("dve", modv, pb[:, 0:96].rearrange("p (v k j) -> p v k j", k=8, j=2))
            A.reset(persist_mark)
            locals_ref.update(modv=modv, m2=m2)
            checkpoint("p0")

            def build_gbc(vi, j, dst, dring):
                dhring = Ring(A, 4, [128], BF16)
                for half in range(2):
                    pb = bank(2 + half)
                    for c in range(4):
                        k = half * 4 + c
                        dg = dring.next()
                        dh = dhring.next()
                        dl = dhring.next()
                        P.ts("dve", dg, identf, modv[:, vi, k, j:j + 1], None, ALU.mult)
                        P.copy("dve", dh, dg)
                        P.tt("dve", dl, dg, dh, ALU.subtract)
                        P.mm(pb[:, c * 128:(c + 1) * 128], ones_b, dh, start=True, stop=False)
                        P.mm(pb[:, c * 128:(c + 1) * 128], ones_b, dl, start=False, stop=True)
                    P.copy("act", dst[:, half * 512:(half + 1) * 512], pb)

            def make_hT(xsrc, T, j, vA, vS, dst, junkb, xn, st):
                checkpoint("m0")
                P.act(junkb[0:T, :], xsrc, AF.Square, accum_out=st[0:T, 0:1])
                checkpoint("m1")
                rstd(st[0:T, 0:1], 1.0 / D, st[0:T, 1:2], T)
                checkpoint("m2")
                P.act(xn[0:T, :], xsrc, AF.Copy, scale=st[0:T, 1:2])
                checkpoint("m3")
                pb = bankbf(0)
                for k in range(8):
                    P.tr(pb[:, k * T:(k + 1) * T], xn[0:T, k * 128:(k + 1) * 128], ident_b[0:T, 0:T])
                checkpoint("m4")
                for k in range(8):
                    if k % 2 == 0:
                        P.ts("dve", dst[:, k, 0:T], pb[:, k * T:(k + 1) * T], modv[:, vA, k, j:j + 1],
                             modv[:, vS, k, j:j + 1], ALU.mult, ALU.add)
                checkpoint("m5")
                for k in range(8):
                    if k % 2 == 1:
                        P.act(dst[:, k, 0:T], pb[:, k * T:(k + 1) * T], AF.Identity, bias=modv[:, vS, k, j:j + 1],
                              scale=modv[:, vA, k, j:j + 1])
                checkpoint("m6")

            def rope(x, H, ti, T, out, ra, rb):
                csb = cs_sb[0:T, ti, :].unsqueeze(1).to_broadcast([T, H, 32])
                sn1 = sn_sb[0:T, ti, 0:16].unsqueeze(1).to_broadcast([T, H, 16])
                sn2 = sn_sb[0:T, ti, 16:32].unsqueeze(1).to_broadcast([T, H, 16])
                P.tt("pool", ra[0:T, 0:H, :], x, csb, ALU.mult)
                P.tt("pool", rb[0:T, 0:H, 0:16], x[:, :, 16:32], sn1, ALU.mult)
                P.tt("pool", rb[0:T, 0:H, 16:32], x[:, :, 0:16], sn2, ALU.mult)
                P.tt("pool", out, ra[0:T, 0:H, :], rb[0:T, 0:H, :], ALU.add)

            def attend_head(h, qT, ktiles, N, dst, sbanks, obank, ptring, bring, rring, cb):
                O = obank
                nkt = len(ktiles)

                def issue_S(i):
                    kt = ktiles[i]
                    Sb = sbanks.next()
                    c0, c1 = kt.union
                    P.mm(Sb[0:kt.nk, c0:c1], kt.kT(h), qT[:, c0:c1])
                    return Sb

                DEPTH = max(1, len(sbanks.ids) - 1)
                Sq = [issue_S(i) for i in range(min(DEPTH, nkt))]
                for i in range(nkt):
                    Sb = Sq.pop(0)
                    if i + DEPTH < nkt:
                        Sq.append(issue_S(i + DEPTH))
                    kt = ktiles[i]
                    PT = ptring.next()
                    for (p0, p1, (a, b)) in kt.parts:
                        if a >= b:
                            continue
                        src = Sb[p0:p1, a:b]
                        if kt.bias is not None:
                            tb = bring.next()
                            P.tt("dve", tb[p0:p1, a:b], src, kt.bias(h)[p0:p1, a:b], ALU.add)
                            P.act(PT[p0:p1, a:b], tb[p0:p1, a:b], AF.Exp)
                        elif kt.cbias:
                            P.act(PT[p0:p1, a:b], src, AF.Exp, bias=cb[p0:p1, h:h + 1])
                        else:
                            P.act(PT[p0:p1, a:b], src, AF.Exp)
                    for (p0, p1, (a, b)) in kt.zeros:
                        P.memset("pool", PT[p0:p1, a:b], 0.0)
                    c0, c1 = kt.union
                    P.mm(O[:, c0:c1], kt.vaug(h), PT[0:kt.nk, c0:c1], start=(i == 0), stop=(i == nkt - 1))
                rb = rring.next()
                if h % 2 == 0:
                    recip(rb[0:64, 0:N], O[64:128, 0:N])
                    P.tt("dve", dst, O[0:64, 0:N], rb[0:64, 0:N], ALU.mult)
                else:
                    recip(rb[64:128, 0:N], O[0:64, 0:N])
                    P.tt("dve", dst, O[64:128, 0:N], rb[64:128, 0:N], ALU.mult)

            class BankRing:
                def __init__(self, ids):
                    self.ids = ids
                    self.i = 0

                def next(self):
                    b = bank(self.ids[self.i % len(self.ids)])
                    self.i += 1
                    return b

            w_in_v = w_in.rearrange("(k p) c -> p k c", p=128)
            oTa = A.alloc([4, S + SD], BF16, top=True)
            oTb = A.alloc([4, S + SD], BF16, top=True)
            p1_mark = A.mark()

            def xsrc_of(ti):
                return (xs, SD, 1) if ti == 16 else (xp[ti * 128:(ti + 1) * 128, :], 128, 0)

            KT = A.alloc([8, 2048], BF16)
            KTn = A.alloc([8, SD], BF16)
            Vst = A.alloc([16, 768], BF16)
            Vn = A.alloc([768], BF16)
            w1 = A.alloc([8, 672], BF16)
            wq = A.alloc([3, 768], BF16)
            wkv = A.alloc([2, 1024], BF16)
            gql = A.alloc([384]); gkv = A.alloc([256]); gqn_s = A.alloc([64]); gqr_s = A.alloc([32])
            gkn = A.alloc([64]); gkr = A.alloc([32])
            P.dma("pool", w1, w_in_v[:, :, 0:672])
            P.dma("pool", wq, w_q_up.rearrange("(k p) c -> p k c", p=128))
            P.dma("pool", wkv, w_kv_up.rearrange("(k p) c -> p k c", p=128))
            checkpoint("a00")
            for (dst, src, n) in [(gql, g_q_lora, 384), (gkv, g_kv_lora, 256), (gqn_s, g_qn_a, 64), (gqr_s, g_qr_a, 32),
                                  (gkn, g_kn_a, 64), (gkr, g_kr_a, 32)]:
                P.dma("sp", dst, bc_rows(src, 128, n))
            checkpoint("a01")
            P.ts("dve", gqn_s, gqn_s, 96.0 ** -0.5, None, ALU.mult)
            P.ts("dve", gqr_s, gqr_s, 96.0 ** -0.5, None, ALU.mult)
            checkpoint("a02")
            P.memset("dve", Vst, 1.0)
            P.memset("dve", Vn, 1.0)
            xring = Ring(A, 2, [D], F32)
            junkb = A.alloc([D], BF16)
            xnr = Ring(A, 2, [D], BF16)
            hTr = Ring(A, 2, [8, 128], BF16)
            sring = Ring(A, 4, [32], F32)
            sqf = A.alloc([D], F32)
            tqf = A.alloc([D], F32)
            cqnr = Ring(A, 2, [384], BF16)
            cqnTr = Ring(A, 2, [3, 128], BF16)
            qanr = Ring(A, 2, [512], BF16)
            qrrr = Ring(A, 2, [256], BF16)
            kanr = Ring(A, 4, [512], BF16)
            krbr = Ring(A, 4, [32], BF16)
            lat32r = Ring(A, 4, [256], F32)
            lat16r = Ring(A, 4, [256], BF16)
            latTr = Ring(A, 4, [2, 128], BF16)
            krnr = Ring(A, 4, [32], F32)
            kr32r = Ring(A, 2, [32], F32)
            kr16r = Ring(A, 2, [32], BF16)
            rar = Ring(A, 2, [8, 32], F32)
            rbr = Ring(A, 2, [8, 32], F32)
            QTs2 = [A.alloc([8, 512], BF16), A.alloc([8, 512], BF16)]
            ptring = Ring(A, 4, [512], BF16)
            rring = Ring(A, 2, [512], F32)

            def kv_from_lat(lat16, T, ka, st, Vdst):
                pb = bankbf(0)
                for k in range(2):
                    P.tr(pb[:, k * T:(k + 1) * T], lat16[0:T, k * 128:(k + 1) * 128], ident_b[0:T, 0:T])
                latT = latTr.next()
                P.copy("dve", latT[:, :, 0:T], pb[:, 0:2 * T].rearrange("p (k t) -> p k t", t=T))
                checkpoint("k2")
                kz = dbl(2)
                for half in range(2):
                    for k in range(2):
                        P.mm(kz[0:T, half * 512:(half + 1) * 512], latT[:, k, 0:T], wkv[:, k, half * 512:(half + 1) * 512],
                             start=(k == 0), stop=(k == 1))
                kv = kz[0:T, :].rearrange("p (h d) -> p h d", d=128)
                sqv = sqf[0:T, 0:512].rearrange("p (h d) -> p h d", d=64)
                checkpoint("k3")
                P.act(sqv, kv[:, :, 0:64], AF.Square)
                checkpoint("k4")
                P.reduce("dve", st[0:T, 22:30], sqv, ALU.add)
                checkpoint("k5")
                rstd(st[0:T, 22:30], 1.0 / 64, st[0:T, 22:30], T)
                checkpoint("k6")
                tk = tqf[0:T, 0:512].rearrange("p (h d) -> p h d", d=64)
                P.tt("dve", tk, kv[:, :, 0:64], st[0:T, 22:30].unsqueeze(2).to_broadcast([T, 8, 64]), ALU.mult)
                P.tt("pool", ka[0:T, :].rearrange("p (h d) -> p h d", d=64), tk,
                     gkn[0:T, :].unsqueeze(1).to_broadcast([T, 8, 64]), ALU.mult)
                checkpoint("k7")
                vsrc = kv[:, :, 64:128].rearrange("p (j e) d -> p j e d", e=2)
                vd = Vdst.rearrange("p (j e d) -> p j e d", e=3, d=64)
                P.copy("act", vd[:, :, 0, :], vsrc[:, :, 0, :])
                P.copy("act", vd[:, :, 2, :], vsrc[:, :, 1, :])
                checkpoint("k8")

            def k_transposes(kan, krb, T, KTdst):
                pb = bankbf(0)
                for pr in range(4):
                    P.tr(pb[:, pr * T:(pr + 1) * T], kan[0:T, pr * 128:(pr + 1) * 128], ident_b[0:T, 0:T])
                P.tr(pb[0:32, 4 * T:5 * T], krb[0:T, :], ident_b[0:T, 0:T])
                pv = pb[:, 0:4 * T].rearrange("p (j t) -> p j t", t=T)
                kd = KTdst.rearrange("p (j e) t -> p j e t", e=2)
                P.copy("act", kd[0:64, :, 0, :], pv[0:64, :, :])
                P.copy("dve", kd[0:64, :, 1, :], pv[64:128, :, :])
                P.copy("dve", KTdst[64:96, :, :], pb[0:32, 4 * T:5 * T].unsqueeze(1).to_broadcast([32, 8, T]))

            def p1a_tile(ti, QTdst, KTdst, Vdst, lat_out, kr_out):
                xsrc, T, j = xsrc_of(ti)
                xt = xring.next()
                P.dma("sp", xt[0:T, :], xsrc)
                hT = hTr.next()
                st = sring.next()
                make_hT(xt[0:T, :], T, j, 1, 0, hT, junkb, xnr.next(), st)
                if ti == 16:
                    checkpoint("t1")
                z = dbl(1)
                for k in range(8):
                    P.mm(z[0:T, 0:384], hT[:, k, 0:T], w1[:, k, 0:384], start=(k == 0), stop=(k == 7))
                for k in range(8):
                    P.mm(z[0:T, 512:800], hT[:, k, 0:T], w1[:, k, 384:672], start=(k == 0), stop=(k == 7))
                if ti == 16:
                    checkpoint("t2")
                P.act(junkb[0:T, 0:384], z[0:T, 0:384], AF.Square, accum_out=st[0:T, 2:3])
                rstd(st[0:T, 2:3], 1.0 / 384, st[0:T, 3:4], T)
                cqn = cqnr.next()
                P.stt("dve", cqn[0:T, :], z[0:T, 0:384], st[0:T, 3:4], gql[0:T, :], ALU.mult, ALU.mult)
                pb = bankbf(0)
                for k in range(3):
                    P.tr(pb[:, k * T:(k + 1) * T], cqn[0:T, k * 128:(k + 1) * 128], ident_b[0:T, 0:T])
                cqnT = cqnTr.next()
                P.copy("dve", cqnT[:, :, 0:T], pb[:, 0:3 * T].rearrange("p (k t) -> p k t", t=T))
                if ti == 16:
                    checkpoint("t3")
                qz = dbl(2)
                for (a, b) in ((0, 512), (512, 768)):
                    for k in range(3):
                        P.mm(qz[0:T, a:b], cqnT[:, k, 0:T], wq[:, k, a:b], start=(k == 0), stop=(k == 2))
                qv = qz[0:T, 0:768].rearrange("p (h d) -> p h d", d=96)
                sqv = sqf[0:T, 0:768].rearrange("p (h d) -> p h d", d=96)
                P.act(sqv, qv, AF.Square)
                P.reduce("dve", st[0:T, 4:12], sqv[:, :, 0:64], ALU.add)
                P.reduce("dve", st[0:T, 12:20], sqv[:, :, 64:96], ALU.add)
                rstd(st[0:T, 4:12], 1.0 / 64, st[0:T, 4:12], T)
                rstd(st[0:T, 12:20], 1.0 / 32, st[0:T, 12:20], T)
                if ti == 16:
                    checkpoint("t4")
                tq = tqf[0:T, 0:768].rearrange("p (h d) -> p h d", d=96)
                P.tt("dve", tq[:, :, 0:64], qv[:, :, 0:64], st[0:T, 4:12].unsqueeze(2).to_broadcast([T, 8, 64]), ALU.mult)
                P.tt("dve", tq[:, :, 64:96], qv[:, :, 64:96], st[0:T, 12:20].unsqueeze(2).to_broadcast([T, 8, 32]), ALU.mult)
                qan = qanr.next()
                qrr = qrrr.next()
                P.tt("pool", qan[0:T, :].rearrange("p (h d) -> p h d", d=64), tq[:, :, 0:64],
                     gqn_s[0:T, :].unsqueeze(1).to_broadcast([T, 8, 64]), ALU.mult)
                P.tt("pool", tq[:, :, 64:96], tq[:, :, 64:96], gqr_s[0:T, :].unsqueeze(1).to_broadcast([T, 8, 32]), ALU.mult)
                rope(tq[:, :, 64:96], 8, ti, T, qrr[0:T, :].rearrange("p (h d) -> p h d", d=32), rar.next(), rbr.next())
                pb = bankbf(0)
                for pr in range(4):
                    P.tr(pb[:, pr * T:(pr + 1) * T], qan[0:T, pr * 128:(pr + 1) * 128], ident_b[0:T, 0:T])
                for pr in range(2):
                    P.tr(pb[:, (4 + pr) * T:(5 + pr) * T], qrr[0:T, pr * 128:(pr + 1) * 128], ident_b[0:T, 0:T])
                pv = pb[:, 0:4 * T].rearrange("p (j t) -> p j t", t=T)
                qd = QTdst.rearrange("p (j e) t -> p j e t", e=2)
                P.copy("act", qd[0:64, :, 0, :], pv[0:64, :, :])
                P.copy("dve", qd[0:64, :, 1, :], pv[64:128, :, :])
                for hh in range(8):
                    src = pb[(hh % 4) * 32:(hh % 4) * 32 + 32, (4 + hh // 4) * T:(5 + hh // 4) * T]
                    P.copy("act" if hh % 2 == 0 else "dve", QTdst[64:96, hh, :], src)
                P.act(junkb[0:T, 0:256], z[0:T, 512:768], AF.Square, accum_out=st[0:T, 20:21])
                rstd(st[0:T, 20:21], 1.0 / 256, st[0:T, 21:22], T)
                lat32 = lat32r.next()
                P.stt("dve", lat32[0:T, :], z[0:T, 512:768], st[0:T, 21:22], gkv[0:T, :], ALU.mult, ALU.mult)
                P.dma("sp", lat_out, lat32[0:T, :])
                lat16 = lat16r.next()
                P.copy("pool", lat16[0:T, :], lat32[0:T, :])
                if ti == 16:
                    checkpoint("t8")
                kan = kanr.next()
                krb = krbr.next()
                P.act(junkb[0:T, 0:32], z[0:T, 768:800], AF.Square, accum_out=st[0:T, 30:31])
                rstd(st[0:T, 30:31], 1.0 / 32, st[0:T, 31:32], T)
                krn = krnr.next()
                P.stt("dve", krn[0:T, :], z[0:T, 768:800], st[0:T, 31:32], gkr[0:T, :], ALU.mult, ALU.mult)
                kr32 = kr32r.next()
                rope(krn[0:T, :].unsqueeze(1), 1, ti, T, kr32[0:T, :].unsqueeze(1), rar.next(), rbr.next())
                P.dma("sp", kr_out, kr32[0:T, :])
                P.copy("pool", krb[0:T, :], kr32[0:T, :])
                if ti == 16:
                    checkpoint("t9")
                kv_from_lat(lat16, T, kan, st, Vdst)
                k_transposes(kan, krb, T, KTdst)

            def full_tile(kTf, vf, nk, N, bias=None, cbias=False):
                return KTile(kTf, vf, nk, [(0, nk, (0, N))], [], (0, N), bias=bias, cbias=cbias)

            def vaug_of(vtile):
                def f(h):
                    pr = h // 2
                    o = pr * 192 + (0 if h % 2 == 0 else 64)
                    return vtile[:, o:o + 128]
                return f

            sb_a = BankRing([2, 3, 4])
            checkpoint("a0")
            for jt in range(16):
                lat16 = lat16r.next()
                lat32c = lat32r.next()
                P.dma("sp", lat32c, latc[jt * 128:(jt + 1) * 128, :])
                P.copy("dve", lat16, lat32c)
                kr32c = krnr.next()
                P.dma("sp", kr32c, krc[jt * 128:(jt + 1) * 128, :])
                kan = kanr.next()
                krb = krbr.next()
                checkpoint("k0")
                P.copy("pool", krb, kr32c)
                checkpoint("k1")
                st = sring.next()
                kv_from_lat(lat16, 128, kan, st, Vst[:, jt, :])
                k_transposes(kan, krb, 128, KT[0:96, :, jt * 128:(jt + 1) * 128])
                if jt == 0:
                    locals_ref.update(KT=KT, Vst=Vst)
                    checkpoint("a1")
                checkpoint("c%d" % jt)
            QTs = A.alloc([8, SD], BF16)
            checkpoint("a2")
            p1a_tile(16, QTs[0:96, :, :], KTn[0:96, :, :], Vn[0:SD, :], lats, krs)
            checkpoint("a3")
            for h in range(8):
                kts = []
                for jt in range(16):
                    kts.append(full_tile((lambda hh, jt=jt: KT[0:96, hh, jt * 128:(jt + 1) * 128]),
                                         (lambda hh, jt=jt: vaug_of(Vst[:, jt, :])(hh)), 128, SD))
                kts.append(full_tile((lambda hh: KTn[0:96, hh, :]), (lambda hh: vaug_of(Vn[0:SD, :])(hh)), SD, SD))
                dst = oTa[(h % 2) * 64:(h % 2) * 64 + 64, h // 2, S:S + SD]
                attend_head(h, QTs[0:96, h, :], kts, SD, dst, sb_a, bank(5 + h % 2), ptring, None, rring, None)
            locals_ref.update(oTa=oTa, KT=KT, Vst=Vst, QTs=QTs, KTn=KTn, Vn=Vn)
            checkpoint("p1as")
            sb_ap = BankRing([1, 6])
            capA = None
            for b in range(4):
                QT = QTs2[b % 2]
                P.begin_capture()
                for tt_ in range(4):
                    ti = 4 * b + tt_
                    p1a_tile(ti, QT[0:96, :, tt_ * 128:(tt_ + 1) * 128], KT[0:96, :, ti * 128:(ti + 1) * 128],
                             Vst[:, ti, :], latp[ti * 128:(ti + 1) * 128, :], krp[ti * 128:(ti + 1) * 128, :])
                capT = P.end_capture()
                P.replay([capA, capT] if capA else [capT])
                P.begin_capture()
                for h in range(8):
                    kts = []
                    for jt in range(4 * b + 4):
                        kf = (lambda hh, jt=jt: KT[0:96, hh, jt * 128:(jt + 1) * 128])
                        vf = (lambda hh, jt=jt: vaug_of(Vst[:, jt, :])(hh))
                        if jt < 4 * b:
                            kts.append(full_tile(kf, vf, 128, 512))
                        else:
                            c0 = 128 * (jt - 4 * b)
                            kts.append(KTile(kf, vf, 128, [(0, 64, (c0, 512)), (64, 128, (c0 + 64, 512))],
                                             [(64, 128, (c0, c0 + 64))], (c0, 512)))
                    dst = oTa[(h % 2) * 64:(h % 2) * 64 + 64, h // 2, b * 512:(b + 1) * 512]
                    attend_head(h, QT[0:96, h, :], kts, 512, dst, sb_ap, bank(7), ptring, None, rring, None)
                capA = P.end_capture()
            P.replay([capA])
            A.reset(p1_mark)

            checkpoint("p1a")
            KBT = A.alloc([4, 1024], BF16)
            VBst = A.alloc([8, 768], BF16)
            wb = A.alloc([8, 1536], BF16)
            gqb_s = A.alloc([64]); gkb = A.alloc([64])
            cb = A.alloc([8], F32)
            biasT = [A.alloc([8, 256], BF16) for _ in range(3)]
            P.dma("pool", wb, w_in_v[:, :, 672:2208])
            P.dma("sp", gqb_s, bc_rows(g_q_b, 128, 64))
            P.dma("sp", gkb, bc_rows(g_k_b, 128, 64))
            P.ts("dve", gqb_s, gqb_s, 0.125, None, ALU.mult)
            P.dma("sp", cb.unsqueeze(2), bass.AP(rel_bias.tensor, 256, [[0, 128], [257, 8], [1, 1]]),
                  allow_slow_non_contiguous=True)
            P.memset("dve", VBst, 1.0)
            bm = A.mark()
            e8 = A.alloc([768], F32)
            antif = A.alloc([128], BF16)
            Hb = A.alloc([2048], BF16)
            Hst = A.alloc([2048], F32)
            P.dma("pool", antif, antid)
            P.memset("pool", e8[0:8, :], 0.0)
            P.dma("sp", e8[0:8, 128:385], rel_bias)
            P.copy("dve", e8[0:8, 385:768], e8[0:8, 384:385].to_broadcast([8, 383]))
            P.dma("sp", ext2, e8[0:8, :])
            for r, Dd in enumerate((128, 0, -128)):
                P.dma("sp", Hst.rearrange("p (h q) -> p h q", q=256),
                      bass.AP(ext2.tensor, 128 + Dd + 1, [[1, 128], [768, 8], [1, 256]]))
                for c in range(4):
                    pb = bank(c % 2)
                    if c == 0:
                        P.copy("dve", Hb, Hst)
                    P.mm(pb, antif, Hb[:, c * 512:(c + 1) * 512])
                    P.copy("dve", biasT[r][:, 2 * c:2 * c + 2, :], pb.rearrange("p (h q) -> p h q", q=256))
            A.reset(bm)
            xring = Ring(A, 2, [D], F32)
            junkb = A.alloc([D], BF16)
            xnr = Ring(A, 2, [D], BF16)
            hTr = Ring(A, 2, [8, 128], BF16)
            sring = Ring(A, 4, [32], F32)
            sqf = A.alloc([512], F32)
            tqf = A.alloc([512], F32)
            qb16r = Ring(A, 2, [512], BF16)
            kb32r = Ring(A, 2, [512], F32)
            kb16r = Ring(A, 2, [512], BF16)
            vb32r = Ring(A, 2, [512], F32)
            QBT = A.alloc([4, 256], BF16)
            QBTs = A.alloc([4, SD], BF16)
            ptring = Ring(A, 4, [512], BF16)
            bring = Ring(A, 2, [512], F32)
            rring = Ring(A, 2, [512], F32)

            def headnorm(zsrc, T, st, c0, gain, out, outeng="pool"):
                zv = zsrc.rearrange("p (h d) -> p h d", d=64)
                sqv = sqf[0:T, :].rearrange("p (h d) -> p h d", d=64)
                P.act(sqv, zv, AF.Square)
                P.reduce("dve", st[0:T, c0:c0 + 8], sqv, ALU.add)
                rstd(st[0:T, c0:c0 + 8], 1.0 / 64, st[0:T, c0:c0 + 8], T)
                tv = tqf[0:T, :].rearrange("p (h d) -> p h d", d=64)
                P.tt("dve", tv, zv, st[0:T, c0:c0 + 8].unsqueeze(2).to_broadcast([T, 8, 64]), ALU.mult)
                P.tt(outeng, out.rearrange("p (h d) -> p h d", d=64), tv,
                     gain[0:T, :].unsqueeze(1).to_broadcast([T, 8, 64]), ALU.mult)

            def kb_to_store(kb16, T, slot):
                pb = bankbf(1)
                for pr in range(4):
                    P.tr(pb[:, pr * T:(pr + 1) * T], kb16[0:T, pr * 128:(pr + 1) * 128], ident_b[0:T, 0:T])
                P.copy("act", KBT[:, :, slot * 128:slot * 128 + T], pb[:, 0:4 * T].rearrange("p (j t) -> p j t", t=T))

            def vb_to_store(vsrc, T, slot):
                vd = VBst[0:T, slot, :].rearrange("p (j e d) -> p j e d", e=3, d=64)
                vs = vsrc.rearrange("p (j e d) -> p j e d", e=2, d=64)
                P.copy("pool", vd[:, :, 0, :], vs[:, :, 0, :])
                P.copy("pool", vd[:, :, 2, :], vs[:, :, 1, :])

            def p1b_tile(ti, QBdst, slot, bk_out, bv_out):
                xsrc, T, j = xsrc_of(ti)
                xt = xring.next()
                P.dma("sp", xt[0:T, :], xsrc)
                hT = hTr.next()
                st = sring.next()
                make_hT(xt[0:T, :], T, j, 1, 0, hT, junkb, xnr.next(), st)
                zqk = dbl(1)
                zv = dbl(2)
                for (dstp, c0) in ((zqk[0:T, 0:512], 0), (zqk[0:T, 512:1024], 512), (zv[0:T, 0:512], 1024)):
                    for k in range(8):
                        P.mm(dstp, hT[:, k, 0:T], wb[:, k, c0:c0 + 512], start=(k == 0), stop=(k == 7))
                qb16 = qb16r.next()
                headnorm(zqk[0:T, 0:512], T, st, 2, gqb_s, qb16[0:T, :])
                pb = bankbf(1)
                for pr in range(4):
                    P.tr(pb[:, pr * T:(pr + 1) * T], qb16[0:T, pr * 128:(pr + 1) * 128], ident_b[0:T, 0:T])
                P.copy("act", QBdst, pb[:, 0:4 * T].rearrange("p (j t) -> p j t", t=T))
                kb32 = kb32r.next()
                headnorm(zqk[0:T, 512:1024], T, st, 10, gkb, kb32[0:T, :])
                if bk_out is not None:
                    P.dma("sp", bk_out, kb32[0:T, :])
                kb16 = kb16r.next()
                P.copy("pool", kb16[0:T, :], kb32[0:T, :])
                kb_to_store(kb16, T, slot)
                vb32 = vb32r.next()
                P.copy("act", vb32[0:T, :], zv[0:T, 0:512])
                if bv_out is not None:
                    P.dma("sp", bv_out, vb32[0:T, :])
                vb_to_store(vb32[0:T, :], T, slot)

            def band_k(slot, nk):
                return lambda hh: KBT[(hh % 2) * 64:(hh % 2) * 64 + 64, hh // 2, slot * 128:slot * 128 + nk]

            def band_v(slot, nk):
                return lambda hh: vaug_of(VBst[0:nk, slot, :])(hh)

            sb_b = BankRing([4, 5, 6])
            for jt in range(4):
                kb16 = kb16r.next()
                P.dma("pool", kb16, bkc[jt * 128:(jt + 1) * 128, :])
                kb_to_store(kb16, 128, jt)
                vb16 = qb16r.next()
                P.dma("pool", vb16, bvc[jt * 128:(jt + 1) * 128, :])
                vb_to_store(vb16, 128, jt)
            p1b_tile(16, QBTs[:, :, :], 4, bks, bvs)
            for h in range(8):
                kts = []
                for jt in range(3):
                    kts.append(full_tile(band_k(jt, 128), band_v(jt, 128), 128, SD, cbias=True))
                kts.append(full_tile(band_k(3, 128), band_v(3, 128), 128, SD, bias=(lambda hh: biasT[0][:, hh, :])))
                kts.append(full_tile(band_k(4, SD), band_v(4, SD), SD, SD, bias=(lambda hh: biasT[1][:, hh, :])))
                dst = oTb[(h % 2) * 64:(h % 2) * 64 + 64, h // 2, S:S + SD]
                attend_head(h, QBTs[(h % 2) * 64:(h % 2) * 64 + 64, h // 2, :], kts, SD, dst, sb_b, bank(7), ptring, bring,
                            rring, cb)
            geom = {
                0: ([(0, 64, (0, 64)), (64, 128, (0, 128))], [(0, 64, (64, 128))], (0, 128)),
                1: ([(0, 64, (0, 192)), (64, 128, (0, 256))], [(0, 64, (192, 256))], (0, 256)),
                2: ([(0, 128, (0, 256))], [], (0, 256)),
                3: ([(0, 128, (0, 256))], [], (0, 256)),
                4: ([(0, 64, (0, 256)), (64, 128, (64, 256))], [(64, 128, (0, 64))], (0, 256)),
                5: ([(0, 64, (128, 256)), (64, 128, (192, 256))], [(64, 128, (128, 192))], (128, 256)),
            }
            for i in range(8):
                for tt_ in range(2):
                    ti = 2 * i + tt_
                    last = ti >= 12
                    p1b_tile(ti, QBT[:, :, tt_ * 128:(tt_ + 1) * 128], ti % 8,
                             bkp[(ti - 12) * 128:(ti - 11) * 128, :] if last else None,
                             bvp[(ti - 12) * 128:(ti - 11) * 128, :] if last else None)
                Q0 = 256 * i
                for h in range(8):
                    kts = []
                    for t in (2, 3, 1, 0, 4, 5):
                        K0 = Q0 - 512 + 128 * t
                        if K0 < 0:
                            continue
                        slot = (K0 // 128) % 8
                        parts, zeros, union = geom[t]
                        bias = None
                        if t >= 3:
                            bias = (lambda hh, r=t - 3: biasT[r][:, hh, :])
                        kts.append(KTile(band_k(slot, 128), band_v(slot, 128), 128, parts, zeros, union, bias=bias,
                                         cbias=(t < 3)))
                    dst = oTb[(h % 2) * 64:(h % 2) * 64 + 64, h // 2, Q0:Q0 + 256]
                    attend_head(h, QBT[(h % 2) * 64:(h % 2) * 64 + 64, h // 2, :], kts, 256, dst, sb_b, bank(7), ptring,
                                bring, rring, cb)
            A.reset(p1_mark)

            locals_ref.update(oTb=oTb)
            checkpoint("p1b")
            x1 = A.alloc([17, D], F32)
            p2_mark = A.mark()
            wg2 = A.alloc([8, 2048], BF16)
            woa = A.alloc([4, D], BF16)
            wob = A.alloc([4, D], BF16)
            wout = A.alloc([8, D], BF16)
            P.dma("pool", wg2, w_in_v[:, :, 2208:4256])
            P.dma("pool", woa, w_o_a.rearrange("(k p) c -> p k c", p=128))
            P.dma("pool", wob, w_o_b.rearrange("(k p) c -> p k c", p=128))
            P.dma("pool", wout, w_out.rearrange("(k p) c -> p k c", p=128))
            gbc1 = [A.alloc([D], F32) for _ in range(2)]
            tmpo = Ring(A, 1, [D], F32)
            dring = Ring(A, 2, [128], F32)
            build_gbc(4, 0, gbc1[0], dring)
            build_gbc(4, 1, gbc1[1], dring)
            xnr = Ring(A, 1, [D], BF16)
            sring = Ring(A, 4, [32], F32)
            hT2r = Ring(A, 2, [8, 128], BF16)
            Gr = Ring(A, 2, [D], BF16)
            t1r = Ring(A, 1, [D], BF16)
            t2r = Ring(A, 1, [D], BF16)
            mixr = Ring(A, 1, [D], BF16)
            mixTr = Ring(A, 1, [8, 128], BF16)
            for ti in list(range(16)) + [16]:
                xsrc, T, j = xsrc_of(ti)
                col0 = ti * 128
                P.dma("sp", x1[0:T, ti, :], xsrc)
                hT2 = hT2r.next()
                xn = xnr.next()
                make_hT(x1[0:T, ti, :], T, j, 1, 0, hT2, xn, xn, sring.next())
                tparts = []
                for br, (wo_, oT_, pg, py, tr_) in enumerate(((woa, oTa, dbl(1), dbl(2), t1r), (wob, oTb, dbl(3), dbl(1), t2r))):
                    for half in range(2):
                        c0 = br * 1024 + half * 512
                        for k in range(8):
                            P.mm(pg[0:T, half * 512:(half + 1) * 512], hT2[:, k, 0:T], wg2[:, k, c0:c0 + 512],
                                 start=(k == 0), stop=(k == 7))
                    Gs = Gr.next()
                    P.act(Gs[0:T, :], pg[0:T, :], AF.Sigmoid)
                    for half in range(2):
                        for pr in range(4):
                            P.mm(py[0:T, half * 512:(half + 1) * 512], oT_[:, pr, col0:col0 + T],
                                 wo_[:, pr, half * 512:(half + 1) * 512], start=(pr == 0), stop=(pr == 3))
                    tb_ = tr_.next()
                    P.tt("dve", tb_[0:T, :], py[0:T, :], Gs[0:T, :], ALU.mult)
                    tparts.append(tb_)
                mixed = mixr.next()
                P.tt("pool", mixed[0:T, :], tparts[0][0:T, :], tparts[1][0:T, :], ALU.add)
                pb = bankbf(0)
                for k in range(8):
                    P.tr(pb[:, k * T:(k + 1) * T], mixed[0:T, k * 128:(k + 1) * 128], ident_b[0:T, 0:T])
                mixT = mixTr.next()
                P.copy("act", mixT[:, :, 0:T], pb[:, 0:8 * T].rearrange("p (k t) -> p k t", t=T))
                po = dbl(2)
                for half in range(2):
                    for fc in range(8):
                        P.mm(po[0:T, half * 512:(half + 1) * 512], mixT[:, fc, 0:T],
                             wout[:, fc, half * 512:(half + 1) * 512], start=(fc == 0), stop=(fc == 7))
                to = tmpo.next()
                P.tt("dve", to[0:T, :], po[0:T, :], gbc1[j][0:T, :], ALU.mult)
                P.tt("pool", x1[0:T, ti, :], x1[0:T, ti, :], to[0:T, :], ALU.add)
            A.reset(p2_mark)

            locals_ref.update(x1=x1)
            checkpoint("p2")
            A.n = ARENA_WORDS
            h2T = A.alloc([8, S + SD], BF16)
            gbc2 = [A.alloc([D], F32) for _ in range(2)]
            dring = Ring(A, 2, [128], F32)
            build_gbc(5, 0, gbc2[0], dring)
            build_gbc(5, 1, gbc2[1], dring)
            xnr = Ring(A, 1, [D], BF16)
            sring = Ring(A, 4, [32], F32)
            for ti in range(17):
                T = 128 if ti < 16 else SD
                j = 0 if ti < 16 else 1
                xn_ = xnr.next()
                make_hT(x1[0:T, ti, :], T, j, 3, 2, h2T[:, :, ti * 128:ti * 128 + T], xn_, xn_, sring.next())
            GS = 4
            groups = []
            c = 0
            while c < 22:
                g = min(GS, 22 - c)
                groups.append((c, g))
                c += g
            wgr = Ring(A, 2, [8, GS * 128], BF16)
            wur = Ring(A, 2, [8, GS * 128], BF16)
            wdr = Ring(A, 2, [GS, D], BF16)
            aTr = Ring(A, 2, [GS, 512], BF16)
            sgr = Ring(A, 2, [512], F32)
            tmpo = Ring(A, 1, [D], F32)
            w_gate_v = w_gate.rearrange("(k p) c -> p k c", p=128)
            w_up_v = w_up.rearrange("(k p) c -> p k c", p=128)
            w_down_v = w_down.rearrange("(g p) c -> p g c", p=128)
            gub = BankRing([0, 1, 2, 3])
            blocks = [(0, 512), (512, 512), (1024, 512), (1536, 512), (2048, SD)]
            for gi, (c0, g) in enumerate(groups):
                wg_ = wgr.next(); wu_ = wur.next(); wd_ = wdr.next()
                P.dma("pool", wg_[:, :, 0:g * 128], w_gate_v[:, :, c0 * 128:(c0 + g) * 128])
                P.dma("pool", wu_[:, :, 0:g * 128], w_up_v[:, :, c0 * 128:(c0 + g) * 128])
                P.dma("pool", wd_[:, 0:g, :], w_down_v[:, c0:c0 + g, :])
                lastg = gi == len(groups) - 1
                for (t0, n) in blocks:
                    j = 0 if t0 < S else 1
                    aT = aTr.next()
                    for gg in range(g):
                        pg = gub.next()
                        for k in range(8):
                            P.mm(pg[:, 0:n], wg_[:, k, gg * 128:(gg + 1) * 128], h2T[:, k, t0:t0 + n], start=(k == 0), stop=(k == 7))
                        pu = gub.next()
                        for k in range(8):
                            P.mm(pu[:, 0:n], wu_[:, k, gg * 128:(gg + 1) * 128], h2T[:, k, t0:t0 + n], start=(k == 0), stop=(k == 7))
                        sg = sgr.next()
                        P.act(sg[:, 0:n], pg[:, 0:n], AF.Silu)
                        P.tt("dve", aT[:, gg, 0:n], pu[:, 0:n], sg[:, 0:n], ALU.mult)
                    ntile = (n + 127) // 128
                    for tl in range(ntile):
                        ti = t0 // 128 + tl
                        T = min(128, n - tl * 128)
                        po = dbl(2 + tl % 2)
                        for half in range(2):
                            for gg in range(g):
                                P.mm(po[0:T, half * 512:(half + 1) * 512], aT[:, gg, tl * 128:tl * 128 + T],
                                     wd_[:, gg, half * 512:(half + 1) * 512], start=(gg == 0), stop=(gg == g - 1))
                        to = tmpo.next()
                        P.tt("dve", to[0:T, :], po[0:T, :], gbc2[j][0:T, :], ALU.mult)
                        P.tt("pool", x1[0:T, ti, :], x1[0:T, ti, :], to[0:T, :], ALU.add)
                        if lastg:
                            if ti < 16:
                                P.dma("sp", yp[ti * 128:(ti + 1) * 128, :], x1[0:T, ti, :])
                            else:
                                P.dma("sp", ys, x1[0:T, ti, :])
        except _Stop:
            pass
        P.emit()
        print("arena peak words", A.peak, "of", ARENA_WORDS, "n_ops", len(P.all), {e: len(P.ops[e]) for e in P.ENGS})
    return nc


_NC_CACHE = {}


def _consts():
    ident = np.eye(128, dtype=np.float32)
    anti = np.ascontiguousarray(ident[::-1])
    half = 16
    inv_freq = (10000.0 ** (-np.arange(half, dtype=np.float32) / half)).astype(np.float32)
    cs = np.zeros((128, 17, 32), np.float32)
    sn = np.zeros((128, 17, 32), np.float32)
    for ti in range(17):
        pos = (np.arange(128) + ti * 128).astype(np.float32)
        ang = (pos[:, None] * inv_freq[None, :]).astype(np.float32)
        c = np.cos(ang).astype(np.float32)
        s = np.sin(ang).astype(np.float32)
        cs[:, ti, 0:16] = c
        cs[:, ti, 16:32] = c
        sn[:, ti, 0:16] = -s
        sn[:, ti, 16:32] = s
    return ident, anti, cs, sn


def kernel(x_prompt, x_sample, c_prompt, c_sample, cache_kv_latent, cache_k_rope, cache_band_k, cache_band_v,
           w_ada, b_ada, g_norm_mix, w_in, g_q_lora, w_q_up, g_kv_lora, w_kv_up, g_qn_a, g_kn_a, g_qr_a, g_kr_a,
           g_q_b, g_k_b, rel_bias, w_o_a, w_o_b, w_out, g_norm_ffn, w_gate, w_up, w_down):
    f = lambda a: np.ascontiguousarray(np.asarray(a, dtype=np.float32))
    if "nc" not in _NC_CACHE:
        _NC_CACHE["nc"] = build_program()
    nc = _NC_CACHE["nc"]
    ident, anti, cs, sn = _consts()
    shared = {
        "w_ada": f(w_ada[0]), "b_ada": f(b_ada[0]).reshape(1, -1), "g_norm_mix": f(g_norm_mix[0]).reshape(1, -1),
        "w_in": f(w_in[0]), "g_q_lora": f(g_q_lora[0]).reshape(1, -1), "w_q_up": f(w_q_up[0]),
        "g_kv_lora": f(g_kv_lora[0]).reshape(1, -1), "w_kv_up": f(w_kv_up[0]),
        "g_qn_a": f(g_qn_a[0]).reshape(1, -1), "g_kn_a": f(g_kn_a[0]).reshape(1, -1),
        "g_qr_a": f(g_qr_a[0]).reshape(1, -1), "g_kr_a": f(g_kr_a[0]).reshape(1, -1),
        "g_q_b": f(g_q_b[0]).reshape(1, -1), "g_k_b": f(g_k_b[0]).reshape(1, -1),
        "rel_bias": f(rel_bias[0]), "w_o_a": f(w_o_a[0]), "w_o_b": f(w_o_b[0]), "w_out": f(w_out[0]),
        "g_norm_ffn": f(g_norm_ffn[0]).reshape(1, -1), "w_gate": f(w_gate[0]), "w_up": f(w_up[0]),
        "w_down": f(w_down[0]), "identd": ident, "antid": anti, "cstab": cs, "sntab": sn,
    }
    in_maps = []
    for b in range(8):
        cpair = np.stack([np.asarray(c_prompt[b], np.float32), np.asarray(c_sample[b], np.float32)], axis=-1)
        cTl = np.ascontiguousarray(cpair.reshape(8, 128, 2).transpose(1, 0, 2).reshape(128, 16))
        m = dict(shared)
        m.update({
            "xp": f(x_prompt[b]), "xs": f(x_sample[b]), "cT": cTl,
            "latc": f(cache_kv_latent[0, b]), "krc": f(cache_k_rope[0, b]),
            "bkc": f(cache_band_k[0, b]).reshape(512, 512), "bvc": f(cache_band_v[0, b]).reshape(512, 512),
        })
        in_maps.append(m)
    res = run_bass_kernel_spmd(nc, in_maps, core_ids=list(range(8)))
    R = res.results
    st = lambda k: np.stack([np.asarray(R[b][k], np.float32) for b in range(8)], axis=0)
    y_p = st("yp"); y_s = st("ys")
    lat_p = st("latp")[None]; kr_p = st("krp")[None]
    bk_p = st("bkp").reshape(8, 512, 8, 64)[None]; bv_p = st("bvp").reshape(8, 512, 8, 64)[None]
    lat_s = st("lats")[None]; kr_s = st("krs")[None]
    bk_s = st("bks").reshape(8, SD, 8, 64)[None]; bv_s = st("bvs").reshape(8, SD, 8, 64)[None]
    return (y_p, y_s, lat_p, kr_p, bk_p, bv_p, lat_s, kr_s, bk_s, bv_s)
```
